# Optimizing a Trainium2 kernel written in Bass

```python
import math
import jax, jax.numpy as jnp
from jax import lax
import numpy as np

D_MODEL = 2048
BATCH = 4
SEQ = 4096
DEPTH = 4

CHUNK = 64
Q_BLOCK = 128
N_MIXERS = 3
N_A = (DEPTH + 2) // 3
N_B = (DEPTH + 1) // 3
N_C = DEPTH // 3
PLE_DIM = 256
ROPE_THETA = 10000.0
EPS = 1e-6

DA_HEAD_DIM = 128
DA_V_DIM = 2 * DA_HEAD_DIM
DA_HEADS = D_MODEL // DA_V_DIM
DA_WIDTH = DA_HEADS * DA_V_DIM
MLA_NOPE = 128
MLA_ROPE = 64
MLA_V = 128
MLA_HEADS = D_MODEL // MLA_V
MLA_Q_RANK = 512
MLA_KV_RANK = 512
MLA_WIDTH = MLA_HEADS * MLA_V
LRU_WIDTH = D_MODEL
LRU_BLOCKS = 8
LRU_BLOCK = LRU_WIDTH // LRU_BLOCKS
CONV_WIDTH = 4
LRU_C = 8.0

kernel_name = "hybrid_diffattn_mla_rglru_streaming"

F32 = jnp.float32


def rms_norm(x, g):
    xf = x.astype(F32)
    y = xf * lax.rsqrt(jnp.mean(xf * xf, axis=-1, keepdims=True) + EPS)
    return (y * g.astype(F32)).astype(x.dtype)


def rope(x, positions):
    d = x.shape[-1]
    half = d // 2
    inv = ROPE_THETA ** (-jnp.arange(half, dtype=F32) / half)
    ang = positions.astype(F32)[..., None] * inv
    ang = ang.reshape(ang.shape[:2] + (1,) * (x.ndim - 3) + (half,))
    cos, sin = jnp.cos(ang), jnp.sin(ang)
    xf = x.astype(F32)
    x1, x2 = xf[..., :half], xf[..., half:]
    return jnp.concatenate([x1 * cos - x2 * sin, x2 * cos + x1 * sin], axis=-1).astype(x.dtype)


def chunk_mask(q_start, q_len, k_len):
    qc = (q_start + jnp.arange(q_len)) // CHUNK
    kc = jnp.arange(k_len) // CHUNK
    return kc[None, :] <= qc[:, None]


def block_sweep(attend, seq):
    return jnp.concatenate([attend(s, s + Q_BLOCK) for s in range(0, seq, Q_BLOCK)], axis=1)


def diff_attention(xn, positions, w_in, q_gain, k_gain, lq1, lk1, lq2, lk2, sub_gain, w_out, layer_idx):
    B, S, _ = xn.shape
    H, d = DA_HEADS, DA_HEAD_DIM
    u = xn @ w_in
    q, k, v, gate = jnp.split(u, 4, axis=-1)
    q = rope(rms_norm(q.reshape(B, S, H, 2, d), q_gain), positions)
    k = rope(rms_norm(k.reshape(B, S, H, 2, d), k_gain), positions)
    v = v.reshape(B, S, H, DA_V_DIM)
    lam_init = 0.8 - 0.6 * math.exp(-0.3 * layer_idx)
    lam = (jnp.exp(jnp.sum(lq1.astype(F32) * lk1.astype(F32)))
           - jnp.exp(jnp.sum(lq2.astype(F32) * lk2.astype(F32))) + lam_init)
    scale = d ** -0.5

    def attend(s, e):
        sc = jnp.einsum('bqhcd,bkhcd->bhcqk', q[:, s:e].astype(F32), k[:, :e].astype(F32)) * scale
        sc = jnp.where(chunk_mask(s, e - s, e), sc, -jnp.inf)
        pr = jax.nn.softmax(sc, axis=-1)
        w = pr[:, :, 0] - lam * pr[:, :, 1]
        return jnp.einsum('bhqk,bkhe->bqhe', w, v[:, :e].astype(F32))

    o = block_sweep(attend, S)
    o = rms_norm(o, sub_gain) * (1.0 - lam_init)
    y = o.reshape(B, S, DA_WIDTH).astype(xn.dtype) * jax.nn.silu(gate)
    return y @ w_out


def mla(xn, positions, w_in, cq_gain, ckv_gain, w_uq, w_ukv, qn_gain, qr_gain, kn_gain, kr_gain, w_out):
    B, S, _ = xn.shape
    H = MLA_HEADS
    u = xn @ w_in
    c_q, c_kv, k_pe, gate = jnp.split(
        u, [MLA_Q_RANK, MLA_Q_RANK + MLA_KV_RANK, MLA_Q_RANK + MLA_KV_RANK + MLA_ROPE], axis=-1)
    q = (rms_norm(c_q, cq_gain) @ w_uq).reshape(B, S, H, MLA_NOPE + MLA_ROPE)
    kv = (rms_norm(c_kv, ckv_gain) @ w_ukv).reshape(B, S, H, MLA_NOPE + MLA_V)
    q_nope = rms_norm(q[..., :MLA_NOPE], qn_gain)
    q_pe = rope(rms_norm(q[..., MLA_NOPE:], qr_gain), positions)
    k_nope = rms_norm(kv[..., :MLA_NOPE], kn_gain)
    v = kv[..., MLA_NOPE:]
    k_pe = rope(rms_norm(k_pe, kr_gain), positions)
    scale = (MLA_NOPE + MLA_ROPE) ** -0.5

    def attend(s, e):
        sc = (jnp.einsum('bqhd,bkhd->bhqk', q_nope[:, s:e].astype(F32), k_nope[:, :e].astype(F32))
              + jnp.einsum('bqhr,bkr->bhqk', q_pe[:, s:e].astype(F32), k_pe[:, :e].astype(F32))) * scale
        sc = jnp.where(chunk_mask(s, e - s, e), sc, -jnp.inf)
        pr = jax.nn.softmax(sc, axis=-1)
        return jnp.einsum('bhqk,bkhe->bqhe', pr, v[:, :e].astype(F32))

    o = block_sweep(attend, S)
    y = o.reshape(B, S, MLA_WIDTH).astype(xn.dtype) * jax.nn.silu(gate)
    return y @ w_out


def rglru_block(xn, w_in, conv_w, conv_b, w_a, b_a, w_x, b_x, lam, w_out):
    B, S, _ = xn.shape
    u = xn @ w_in
    xb, gate = jnp.split(u, 2, axis=-1)
    xc = lax.conv_general_dilated(
        xb.astype(F32), conv_w.astype(F32)[:, None, :], window_strides=(1,),
        padding=[(CONV_WIDTH - 1, 0)], dimension_numbers=('NWC', 'WIO', 'NWC'),
        feature_group_count=LRU_WIDTH) + conv_b.astype(F32)
    xg = xc.reshape(B, S, LRU_BLOCKS, LRU_BLOCK)
    r = jax.nn.sigmoid(jnp.einsum('bsgi,gij->bsgj', xg, w_a.astype(F32)).reshape(B, S, LRU_WIDTH)
                       + b_a.astype(F32))
    i_gate = jax.nn.sigmoid(jnp.einsum('bsgi,gij->bsgj', xg, w_x.astype(F32)).reshape(B, S, LRU_WIDTH)
                            + b_x.astype(F32))
    log_a = -LRU_C * r * jax.nn.softplus(-lam.astype(F32))
    a = jnp.exp(log_a)
    bt = jnp.sqrt(-jnp.expm1(2.0 * log_a)) * (i_gate * xc)

    def combine(left, right):
        a1, b1 = left
        a2, b2 = right
        return a1 * a2, a2 * b1 + b2

    _, h = lax.associative_scan(combine, (a, bt), axis=1)
    y = h.astype(xn.dtype) * jax.nn.silu(gate)
    return y @ w_out


def setup_inputs(seed: int = 0) -> dict:
    key = jax.random.key(seed)
    ks = iter(jax.random.split(key, 48))

    def nrm(shape, scale):
        return jax.random.normal(next(ks), shape, F32) * scale

    def gain(shape):
        return 1.0 + 0.05 * jax.random.normal(next(ks), shape, F32)

    D = D_MODEL
    x = nrm((BATCH, SEQ, D), 1.0)
    p = nrm((DEPTH, BATCH, SEQ, PLE_DIM), 1.0)
    offset = jax.random.randint(next(ks), (BATCH, 1), 0, 4096, dtype=jnp.int32)
    positions = (offset + jnp.arange(SEQ, dtype=jnp.int32)[None, :]).astype(jnp.int32)
    norm_gain = gain((DEPTH, D))
    a_w_in = nrm((N_A, D, 4 * DA_WIDTH), D ** -0.5)
    a_q_norm = gain((N_A, DA_HEAD_DIM))
    a_k_norm = gain((N_A, DA_HEAD_DIM))
    a_lambda_q1 = nrm((N_A, DA_HEAD_DIM), 0.1)
    a_lambda_k1 = nrm((N_A, DA_HEAD_DIM), 0.1)
    a_lambda_q2 = nrm((N_A, DA_HEAD_DIM), 0.1)
    a_lambda_k2 = nrm((N_A, DA_HEAD_DIM), 0.1)
    a_sub_norm = gain((N_A, DA_V_DIM))
    a_w_out = nrm((N_A, DA_WIDTH, D), DA_WIDTH ** -0.5)
    b_w_in = nrm((N_B, D, MLA_Q_RANK + MLA_KV_RANK + MLA_ROPE + MLA_WIDTH), D ** -0.5)
    b_cq_norm = gain((N_B, MLA_Q_RANK))
    b_ckv_norm = gain((N_B, MLA_KV_RANK))
    b_w_uq = nrm((N_B, MLA_Q_RANK, MLA_HEADS * (MLA_NOPE + MLA_ROPE)), MLA_Q_RANK ** -0.5)
    b_w_ukv = nrm((N_B, MLA_KV_RANK, MLA_HEADS * (MLA_NOPE + MLA_V)), MLA_KV_RANK ** -0.5)
    b_q_nope_norm = gain((N_B, MLA_NOPE))
    b_q_rope_norm = gain((N_B, MLA_ROPE))
    b_k_nope_norm = gain((N_B, MLA_NOPE))
    b_k_rope_norm = gain((N_B, MLA_ROPE))
    b_w_out = nrm((N_B, MLA_WIDTH, D), MLA_WIDTH ** -0.5)
    c_w_in = nrm((N_C, D, 2 * LRU_WIDTH), D ** -0.5)
    c_conv_w = nrm((N_C, CONV_WIDTH, LRU_WIDTH), CONV_WIDTH ** -0.5)
    c_conv_b = nrm((N_C, LRU_WIDTH), 0.02)
    c_w_a = nrm((N_C, LRU_BLOCKS, LRU_BLOCK, LRU_BLOCK), LRU_BLOCK ** -0.5)
    c_b_a = nrm((N_C, LRU_WIDTH), 0.02)
    c_w_x = nrm((N_C, LRU_BLOCKS, LRU_BLOCK, LRU_BLOCK), LRU_BLOCK ** -0.5)
    c_b_x = nrm((N_C, LRU_WIDTH), 0.02)
    a_target = jax.random.uniform(next(ks), (N_C, LRU_WIDTH), F32, minval=0.9, maxval=0.999)
    s = a_target ** (1.0 / LRU_C)
    c_lambda = jnp.log(s) - jnp.log1p(-s)
    c_w_out = nrm((N_C, LRU_WIDTH, D), LRU_WIDTH ** -0.5)
    ple_norm = gain((DEPTH, D))
    ple_w_gate = nrm((DEPTH, D, D), D ** -0.5)
    ple_w_proj = nrm((DEPTH, PLE_DIM, D), PLE_DIM ** -0.5)
    return {
        "x": x, "p": p, "positions": positions, "norm_gain": norm_gain,
        "a_w_in": a_w_in, "a_q_norm": a_q_norm, "a_k_norm": a_k_norm,
        "a_lambda_q1": a_lambda_q1, "a_lambda_k1": a_lambda_k1,
        "a_lambda_q2": a_lambda_q2, "a_lambda_k2": a_lambda_k2,
        "a_sub_norm": a_sub_norm, "a_w_out": a_w_out,
        "b_w_in": b_w_in, "b_cq_norm": b_cq_norm, "b_ckv_norm": b_ckv_norm,
        "b_w_uq": b_w_uq, "b_w_ukv": b_w_ukv, "b_q_nope_norm": b_q_nope_norm,
        "b_q_rope_norm": b_q_rope_norm, "b_k_nope_norm": b_k_nope_norm,
        "b_k_rope_norm": b_k_rope_norm, "b_w_out": b_w_out,
        "c_w_in": c_w_in, "c_conv_w": c_conv_w, "c_conv_b": c_conv_b,
        "c_w_a": c_w_a, "c_b_a": c_b_a, "c_w_x": c_w_x, "c_b_x": c_b_x,
        "c_lambda": c_lambda, "c_w_out": c_w_out,
        "ple_norm": ple_norm, "ple_w_gate": ple_w_gate, "ple_w_proj": ple_w_proj,
    }


def reference(x, p, positions, norm_gain,
              a_w_in, a_q_norm, a_k_norm, a_lambda_q1, a_lambda_k1, a_lambda_q2, a_lambda_k2,
              a_sub_norm, a_w_out,
              b_w_in, b_cq_norm, b_ckv_norm, b_w_uq, b_w_ukv, b_q_nope_norm, b_q_rope_norm,
              b_k_nope_norm, b_k_rope_norm, b_w_out,
              c_w_in, c_conv_w, c_conv_b, c_w_a, c_b_a, c_w_x, c_b_x, c_lambda, c_w_out,
              ple_norm, ple_w_gate, ple_w_proj):
    h = x
    for i in range(DEPTH):
        xn = rms_norm(h, norm_gain[i])
        j = i // N_MIXERS
        kind = i % N_MIXERS
        if kind == 0:
            m = diff_attention(xn, positions, a_w_in[j], a_q_norm[j], a_k_norm[j],
                               a_lambda_q1[j], a_lambda_k1[j], a_lambda_q2[j], a_lambda_k2[j],
                               a_sub_norm[j], a_w_out[j], i)
        elif kind == 1:
            m = mla(xn, positions, b_w_in[j], b_cq_norm[j], b_ckv_norm[j], b_w_uq[j], b_w_ukv[j],
                    b_q_nope_norm[j], b_q_rope_norm[j], b_k_nope_norm[j], b_k_rope_norm[j],
                    b_w_out[j])
        else:
            m = rglru_block(xn, c_w_in[j], c_conv_w[j], c_conv_b[j], c_w_a[j], c_b_a[j],
                            c_w_x[j], c_b_x[j], c_lambda[j], c_w_out[j])
        h = h + m.astype(h.dtype)
        g = jax.nn.sigmoid((rms_norm(h, ple_norm[i]) @ ple_w_gate[i]).astype(F32))
        h = h + (g * (p[i] @ ple_w_proj[i]).astype(F32)).astype(h.dtype)
    return h
```

```python
import contextlib
import math
import numpy as np
import ml_dtypes
import concourse.bass as bass
import concourse.mybir as mybir
from concourse.bass_utils import run_bass_kernel_spmd

F32 = mybir.dt.float32
BF16 = mybir.dt.bfloat16
I32 = mybir.dt.int32
AF = mybir.ActivationFunctionType
ALU = mybir.AluOpType
AX = mybir.AxisListType

NCORES = 8
T = 2048
NT = 16
S = 4096
D = 2048
KC = 16
EPS = 1e-6
PI = math.pi
TWO_PI = 2.0 * math.pi


class Tk:
    __slots__ = ("w", "r", "name", "dsem", "dcnt")

    def __init__(self, name=""):
        self.w = []
        self.r = []
        self.name = name
        self.dsem = None
        self.dcnt = 0


class Prog:
    COMPUTE = ("pe", "act", "dve", "pool")

    def __init__(self, nc, stack):
        self.nc = nc
        self.stack = stack
        self.E = {"pe": nc.tensor, "act": nc.scalar, "dve": nc.vector,
                  "pool": nc.gpsimd, "sp": nc.sync}
        self.sems = {}
        self.cnt = {}
        for e in self.COMPUTE:
            self.sems[e] = stack.enter_context(nc.semaphore("c_" + e))
            self.cnt[e] = 0
        self.seen = {e: {} for e in self.E}
        self.ninst = 0
        self.uid = 0
        self.pstack = None
        self.sempool = []
        self.phase_tks = []
        self.phase_sem = stack.enter_context(nc.semaphore("phase"))
        self.phase_no = 0
        self.cc_sem = stack.enter_context(nc.semaphore("cc"))
        self.cc_cnt = 0

    def sbuf(self, name, shape, dtype):
        st = self.pstack if self.pstack is not None else self.stack
        self.uid += 1
        t = st.enter_context(self.nc.sbuf_tensor("%s_%d" % (name, self.uid), shape, dtype))
        return t, Tk(name)

    def begin_phase(self):
        self.pstack = contextlib.ExitStack()
        self.phase_tks = []

    def end_phase(self):
        toks = [(self.sems[e], self.cnt[e], e) for e in self.COMPUTE]
        toks.append((self.cc_sem, self.cc_cnt, "dma"))
        for tk in self.phase_tks:
            toks.append((tk.dsem, tk.dcnt, "dma"))
        self._waits("sp", toks)
        self.phase_no += 1
        self.nc.sync.sem_inc(self.phase_sem, 1)
        for e in self.COMPUTE:
            self.E[e].wait_ge(self.phase_sem, self.phase_no)
        self.ninst += 5
        for tk in self.phase_tks:
            self.sempool.append((tk.dsem, tk.dcnt))
            tk.dsem = None
        self.phase_tks = []
        if self.pstack is not None:
            self.pstack.close()
            self.pstack = None

    def gather_chunks(self, send, recv, nrows, rpc, groups, reads, writes):
        i, r0 = 0, 0
        while r0 < nrows:
            n = min(rpc, nrows - r0)
            self.collective("AllGather", send[r0:r0 + n, :], recv[2 * r0:2 * r0 + 2 * n, :], groups, reads, writes)
            r0 += n

    def collective(self, kind, src, dst, groups, reads, writes):
        self._waits("pool", self._deps("pool", reads, writes))
        ins = self.nc.gpsimd.collective_compute(kind, ALU.bypass, replica_groups=groups, ins=[src.opt()], outs=[dst.opt()])
        self.cc_cnt += 1
        ins.then_inc(self.cc_sem)
        self.ninst += 1
        self._record((self.cc_sem, self.cc_cnt, "dma"), reads, writes)

    def psum(self, name, shape, dtype):
        st = self.pstack if self.pstack is not None else self.stack
        self.uid += 1
        t = st.enter_context(self.nc.psum_tensor("%s_%d" % (name, self.uid), shape, dtype))
        return t, Tk(name)

    def newsem(self, name):
        self.nsems = getattr(self, "nsems", 0) + 1
        self.uid += 1
        return self.stack.enter_context(self.nc.semaphore("%s_%d" % (name, self.uid)))

    def _waits(self, e, toks):
        need = {}
        for (s, v, src) in toks:
            k = id(s)
            if k not in need or need[k][1] < v:
                need[k] = (s, v)
        seen = self.seen[e]
        for k, (s, v) in need.items():
            if seen.get(k, 0) >= v:
                continue
            self.E[e].wait_ge(s, v)
            self.ninst += 1
            seen[k] = v

    def _deps(self, e, reads, writes):
        toks = []
        for t in reads:
            toks += t.w
        for t in writes:
            toks += t.w
            for tok in t.r:
                toks.append(tok)
        if e == "pe":
            toks = [t for t in toks if t[2] != "pe"]
        return toks

    def _record(self, tok, reads, writes):
        for t in reads:
            t.r.append(tok)
            if len(t.r) > 48:
                best = {}
                for (s, v, src) in t.r:
                    k = id(s)
                    if k not in best or best[k][1] < v:
                        best[k] = (s, v, src)
                t.r = list(best.values())
        for t in writes:
            t.w = [tok]
            t.r = []

    def op(self, e, fn, reads=(), writes=(), mark=True):
        self._waits(e, self._deps(e, reads, writes))
        ins = fn()
        self.ninst += 1
        if mark:
            self.cnt[e] += 1
            ins.then_inc(self.sems[e], 1)
            tok = (self.sems[e], self.cnt[e], e)
        else:
            tok = (self.sems[e], self.cnt[e] + 1, e)
        self._record(tok, reads, writes)
        return ins

    def dma(self, q, out, in_, reads, writes, semtk, nobarrier=False, **kw):
        if semtk.dsem is None:
            if nobarrier:
                semtk.dsem, semtk.dcnt = self.newsem("bg"), 0
            else:
                if self.sempool:
                    semtk.dsem, semtk.dcnt = self.sempool.pop()
                else:
                    semtk.dsem, semtk.dcnt = self.newsem("d"), 0
                self.phase_tks.append(semtk)
        toks = self._deps(q, reads, writes)
        if semtk.dcnt > 0:
            toks.append((semtk.dsem, semtk.dcnt, "dma"))
        self._waits(q, toks)
        ins = self.E[q].dma_start(out=out, in_=in_, **kw)
        self.ninst += 1
        semtk.dcnt += 16
        ins.then_inc(semtk.dsem, 16)
        tok = (semtk.dsem, semtk.dcnt, "dma")
        self._record(tok, reads, writes)
        return ins

    def finish(self, tks):
        toks = []
        for t in tks:
            toks += t.w
            toks += t.r
        self._waits("sp", toks)


class K:
    def __init__(self, nc, P, ident_dram):
        self.nc = nc
        self.P = P
        self.ident, self.ident_k = P.sbuf("ident_sb", [128, 128], BF16)
        P.dma("pool", self.ident[:, :], ident_dram, [], [self.ident_k], self.ident_k)
        self.pf, self.pb = [], []
        self.pfi = 0
        self.pbi = 0
        self.rr = 0
        self.dq = []

    def setup_psum(self, nf32=6, nbf16=2):
        P = self.P
        self.pf = [P.psum("pf%d" % i, [128, 512], F32) for i in range(nf32)]
        self.pb = [P.psum("pb%d" % i, [128, 1024], BF16) for i in range(nbf16)]

    def defer(self, fn):
        self.dq.append(fn)

    def run_deferred(self, lag):
        while len(self.dq) > lag:
            self.dq.pop(0)()

    def flush(self):
        self.run_deferred(0)

    def next_pf(self, lo=0, hi=6):
        n = hi - lo
        i = lo + (self.pfi % n)
        self.pfi += 1
        return self.pf[i]

    def next_pb(self):
        i = self.pbi % 2
        self.pbi += 1
        return self.pb[i]

    def act(self, out, in_, func, reads, writes, **kw):
        nc = self.nc
        return self.P.op("act", lambda: nc.scalar.activation(out=out, in_=in_, func=func, **kw), reads, writes)

    def tt(self, eng, out, in0, in1, op, reads, writes):
        nc = self.nc
        E = nc.vector if eng == "dve" else nc.gpsimd
        return self.P.op(eng, lambda: E.tensor_tensor(out=out, in0=in0, in1=in1, op=op), reads, writes)

    def ts(self, eng, out, in0, s1, s2, op0, op1, reads, writes):
        nc = self.nc
        E = nc.vector if eng == "dve" else nc.gpsimd
        if op1 is None:
            return self.P.op(eng, lambda: E.tensor_scalar(out=out, in0=in0, scalar1=s1, scalar2=None, op0=op0), reads, writes)
        return self.P.op(eng, lambda: E.tensor_scalar(out=out, in0=in0, scalar1=s1, scalar2=s2, op0=op0, op1=op1), reads, writes)

    def stt(self, out, in0, scalar, in1, op0, op1, reads, writes):
        nc = self.nc
        return self.P.op("dve", lambda: nc.vector.scalar_tensor_tensor(out=out, in0=in0, scalar=scalar, in1=in1, op0=op0, op1=op1), reads, writes)

    def red(self, out, in_, reads, writes):
        nc = self.nc
        return self.P.op("dve", lambda: nc.vector.tensor_reduce(out=out, in_=in_, axis=AX.X, op=ALU.add), reads, writes)

    def recip(self, out, in_, reads, writes):
        nc = self.nc
        return self.P.op("dve", lambda: nc.vector.reciprocal(out=out, in_=in_), reads, writes)

    def copy(self, eng, out, in_, reads, writes):
        nc = self.nc
        if eng == "act":
            return self.P.op("act", lambda: nc.scalar.activation(out=out, in_=in_, func=AF.Copy), reads, writes)
        E = nc.vector if eng == "dve" else nc.gpsimd
        return self.P.op(eng, lambda: E.tensor_copy(out=out, in_=in_), reads, writes)

    def memset(self, eng, ap, val, reads, writes):
        nc = self.nc
        E = nc.vector if eng == "dve" else nc.gpsimd
        return self.P.op(eng, lambda: E.memset(ap, val), reads, writes)

    def mm(self, out, lhsT, rhs, start, stop, reads, writes, mark):
        nc = self.nc
        return self.P.op("pe", lambda: nc.tensor.matmul(out, lhsT=lhsT, rhs=rhs, start=start, stop=stop), reads, writes, mark=mark)

    def tr(self, out, in_, reads, writes, mark):
        nc = self.nc
        ident = self.ident
        return self.P.op("pe", lambda: nc.tensor.transpose(out=out, in_=in_, identity=ident[:, :]),
                         list(reads) + [self.ident_k], writes, mark=mark)

    def rstd(self, out, ss, scale, out_k, ss_k):
        self.ts("dve", ss, ss, scale, EPS, ALU.mult, ALU.add, [ss_k], [ss_k])
        self.act(ss, ss, AF.Sqrt, [ss_k], [ss_k])
        self.recip(out, ss, [ss_k], [out_k])


def bcast_load(k, name, vec_dram, n, q="sp"):
    t, tk = k.P.sbuf(name, [128, n], F32)
    k.P.dma(q, t[:, :], vec_dram.partition_broadcast(128), [], [tk], tk)
    return t, tk


class NormT:
    def __init__(self, k, name):
        P = k.P
        self.k = k
        self.hb = [P.sbuf("%s_hb%d" % (name, i), [128, D], F32) for i in range(2)]
        self.sq = P.sbuf(name + "_sq", [128, D], BF16)
        self.ss = P.sbuf(name + "_ss", [128, 1], F32)
        self.rs = P.sbuf(name + "_rs", [128, 1], F32)
        self.xn = [P.sbuf("%s_xn%d" % (name, i), [128, D], BF16) for i in range(2)]

    def run(self, h_dram, h_deps, gain_bc, gain_k, xT, xT_ks, ntiles=NT, tile0=0):
        k, P = self.k, self.k.P
        for t in range(ntiles):
            hb, hb_k = self.hb[t % 2]
            xn, xn_k = self.xn[t % 2]
            sq, sq_k = self.sq
            ss, ss_k = self.ss
            rs, rs_k = self.rs
            P.dma("sp", hb[:, :], h_dram[t * 128:(t + 1) * 128, :], h_deps(t), [hb_k], hb_k)
            k.act(sq[:, :], hb[:, :], AF.Square, [hb_k], [sq_k])
            k.red(ss[:, 0:1], sq[:, :], [sq_k], [ss_k])
            k.rstd(rs[:, 0:1], ss[:, 0:1], 1.0 / D, rs_k, ss_k)
            k.stt(xn[:, :], hb[:, :], rs[:, 0:1], gain_bc[:, :], ALU.mult, ALU.mult, [hb_k, rs_k, gain_k], [xn_k])
            tt = tile0 + t
            for half in range(2):
                pb, pb_k = k.next_pb()
                for j in range(8):
                    c = half * 8 + j
                    k.tr(pb[:, j * 128:(j + 1) * 128], xn[:, c * 128:(c + 1) * 128], [xn_k], [pb_k], mark=(j == 7))
                k.copy("act" if half == 0 else "dve",
                       xT[:, half * 8:(half + 1) * 8, tt * 128:(tt + 1) * 128],
                       pb[:, :].rearrange("p (j q) -> p j q", j=8), [pb_k], [xT_ks[tt]])


class PreCast:
    def __init__(self, src, ap, rows, cols):
        self.src, self.ap, self.tk = src, ap, Tk("precast")
        self.sem_tk = Tk("precast_sem")
        rpp = max(128, (8 << 20) // (cols * 4) // 128 * 128)
        self.pieces = [(r0, min(rows, r0 + rpp)) for r0 in range(0, rows, rpp)]

    def emit_piece(self, P, i, pace):
        r0, r1 = self.pieces[i]
        P.dma("pool", self.ap[r0:r1, :], self.src[r0:r1, :], pace, [self.tk], self.sem_tk, nobarrier=True)


class WStream:
    def __init__(self, k, name, kcmax, nbuf=2):
        self.k = k
        self.buf = [k.P.sbuf("%s_w%d" % (name, i), [128, kcmax, 512], BF16) for i in range(nbuf)]
        self.i = 0

    def load(self, W_dram, kc, n0, nsz):
        w, w_k = self.buf[self.i % len(self.buf)]
        self.i += 1
        if isinstance(W_dram, PreCast):
            self.k.P.dma("act", w[:, 0:kc, 0:nsz],
                         W_dram.ap[:, n0:n0 + nsz].rearrange("(c p) n -> p c n", p=128), [W_dram.tk], [w_k], w_k)
        else:
            self.k.P.dma("pool", w[:, 0:kc, 0:nsz],
                         W_dram[:, n0:n0 + nsz].rearrange("(c p) n -> p c n", p=128), [], [w_k], w_k)
        return w, w_k

    def stream(self, jobs):
        nxt = self.load(*jobs[0])
        for i in range(len(jobs)):
            cur = nxt
            if i + 1 < len(jobs):
                nxt = self.load(*jobs[i + 1])
            yield i, cur[0], cur[1]


def linear_tok(k, xT, xT_ks, kc, w, w_k, nsz, tiles, evac, pf_lo=0, pf_hi=3):
    for t in tiles:
        ps, ps_k = k.next_pf(pf_lo, pf_hi)
        for c in range(kc):
            k.mm(ps[:, 0:nsz], xT[:, c, t * 128:(t + 1) * 128], w[:, c, 0:nsz], c == 0, c == kc - 1,
                 [xT_ks[t], w_k], [ps_k], mark=(c == kc - 1))
        evac(t, ps, ps_k)
        k.run_deferred(3)


class Rope:
    def __init__(self, k, name, posT_dram, inv_dram, ntiles, half):
        P, nc = k.P, k.nc
        n = ntiles * half
        self.cos, self.cos_k = P.sbuf(name + "_cos", [128, ntiles, half], F32)
        self.sin, self.sin_k = P.sbuf(name + "_sin", [128, ntiles, half], F32)
        pi_, pi_k = P.sbuf(name + "_pi", [128, ntiles], I32)
        pf, pf_k = P.sbuf(name + "_pf", [128, ntiles], F32)
        inv, inv_k = bcast_load(k, name + "_inv", inv_dram, half)
        ang, ang_k = P.sbuf(name + "_ang", [128, ntiles, half], F32)
        ki, ki_k = P.sbuf(name + "_ki", [128, n], I32)
        kf, kf_k = P.sbuf(name + "_kf", [128, n], F32)
        m, m_k = P.sbuf(name + "_m", [128, n], F32)
        P.dma("sp", pi_[:, :], posT_dram, [], [pi_k], pi_k)
        k.copy("dve", pf[:, :], pi_[:, :], [pi_k], [pf_k])
        for t in range(ntiles):
            k.ts("dve", ang[:, t, :], inv[:, :], pf[:, t:t + 1], None, ALU.mult, None, [inv_k, pf_k], [ang_k])
        angf = ang[:, :, :].rearrange("p t h -> p (t h)")
        for which, (dst, dst_k) in enumerate([(self.sin, self.sin_k), (self.cos, self.cos_k)]):
            if which == 0:
                src, src_k = angf, ang_k
            else:
                k.ts("dve", angf, angf, PI / 2.0, None, ALU.add, None, [ang_k], [ang_k])
                src, src_k = angf, ang_k
            k.ts("dve", kf[:, :], src, 1.0 / TWO_PI, None, ALU.mult, None, [src_k], [kf_k])
            k.copy("dve", ki[:, :], kf[:, :], [kf_k], [ki_k])
            k.copy("dve", kf[:, :], ki[:, :], [ki_k], [kf_k])
            k.stt(m[:, :], kf[:, :], -TWO_PI, src, ALU.mult, ALU.add, [kf_k, src_k], [m_k])
            k.ts("dve", kf[:, :], m[:, :], PI, TWO_PI, ALU.is_gt, ALU.mult, [m_k], [kf_k])
            k.tt("dve", m[:, :], m[:, :], kf[:, :], ALU.subtract, [m_k, kf_k], [m_k])
            k.ts("dve", kf[:, :], m[:, :], -PI, TWO_PI, ALU.is_lt, ALU.mult, [m_k], [kf_k])
            k.tt("dve", m[:, :], m[:, :], kf[:, :], ALU.add, [m_k, kf_k], [m_k])
            k.ts("dve", m[:, :], m[:, :], 3.1415925, -3.1415925, ALU.min, ALU.max, [m_k], [m_k])
            k.act(dst[:, :, :].rearrange("p t h -> p (t h)"), m[:, :], AF.Sin, [m_k], [dst_k])


class QKProc:
    def __init__(self, k, name, ngrp, dn, with_rope=True, nob=6, pool_ok=False):
        P = k.P
        self.k = k
        self.pe2 = "pool" if pool_ok else "dve"
        self.ngrp, self.dn = ngrp, dn
        w = ngrp * dn
        self.sq = [P.sbuf("%s_sq%d" % (name, i), [128, w], F32) for i in range(2)]
        self.ss = [P.sbuf("%s_ss%d" % (name, i), [128, ngrp], F32) for i in range(2)]
        self.rs = [P.sbuf("%s_rs%d" % (name, i), [128, ngrp], F32) for i in range(2)]
        self.tq = [P.sbuf("%s_tq%d" % (name, i), [128, w], F32) for i in range(2)]
        self.ra = [P.sbuf("%s_ra%d" % (name, i), [128, w], F32) if with_rope else (None, None) for i in range(2)]
        self.nob = nob
        self.ob = [P.sbuf("%s_ob%d" % (name, i), [128, w], BF16) for i in range(nob)]
        self.i = 0

    def run(self, src3, src_k, gain, gain_k, rope, t, do_rope=True, ngrp=None):
        k = self.k
        g, dn = (ngrp or self.ngrp), self.dn
        w = g * dn
        i = self.i % 2
        self.i += 1
        sq, sq_k = self.sq[i]
        ss, ss_k = self.ss[i]
        rs, rs_k = self.rs[i]
        tq, tq_k = self.tq[i]
        ra, ra_k = self.ra[i]
        ob, ob_k = self.ob[(self.i - 1) % self.nob]
        v3 = lambda tl: tl[:, 0:w].rearrange("p (g d) -> p g d", g=g)
        k.act(v3(sq), src3, AF.Square, [src_k], [sq_k])
        k.red(ss[:, 0:g], v3(sq), [sq_k], [ss_k])
        k.rstd(rs[:, 0:g], ss[:, 0:g], 1.0 / dn, rs_k, ss_k)
        k.tt("dve", v3(tq), src3, rs[:, 0:g].unsqueeze(2).to_broadcast([128, g, dn]), ALU.mult,
             [src_k, rs_k], [tq_k])
        gb = gain[:, :].unsqueeze(1).to_broadcast([128, g, dn])
        if not do_rope:
            k.tt(self.pe2, v3(ob), v3(tq), gb, ALU.mult, [tq_k, gain_k], [ob_k])
            return ob, ob_k
        k.tt(self.pe2, v3(tq), v3(tq), gb, ALU.mult, [tq_k, gain_k], [tq_k])
        h = dn // 2
        cosb = rope.cos[:, t, :].unsqueeze(1).to_broadcast([128, g, h])
        sinb = rope.sin[:, t, :].unsqueeze(1).to_broadcast([128, g, h])
        t3, r3, o3 = v3(tq), v3(ra), v3(ob)
        x1, x2 = t3[:, :, 0:h], t3[:, :, h:dn]
        k.tt("dve", r3[:, :, 0:h], x1, cosb, ALU.mult, [tq_k, rope.cos_k], [ra_k])
        k.tt(self.pe2, r3[:, :, h:dn], x2, sinb, ALU.mult, [tq_k, rope.sin_k], [ra_k])
        k.tt("dve", o3[:, :, 0:h], r3[:, :, 0:h], r3[:, :, h:dn], ALU.subtract, [ra_k], [ob_k])
        k.tt("dve", r3[:, :, 0:h], x2, cosb, ALU.mult, [tq_k, rope.cos_k], [ra_k])
        k.tt(self.pe2, r3[:, :, h:dn], x1, sinb, ALU.mult, [tq_k, rope.sin_k], [ra_k])
        k.tt("dve", o3[:, :, h:dn], r3[:, :, 0:h], r3[:, :, h:dn], ALU.add, [ra_k], [ob_k])
        return ob, ob_k


def phaseA_diff(k, io):
    P = k.P
    gain, gain_k = bcast_load(k, "A_ng", io["ng"], D)
    qg, qg_k = bcast_load(k, "A_qg", io["qg"], 128)
    kg, kg_k = bcast_load(k, "A_kg", io["kg"], 128)
    rope = Rope(k, "A_rp", io["posT"], io["inv64"], NT, 64)
    xT, _ = P.sbuf("A_xT", [128, KC, T], BF16)
    xT_ks = [Tk("xT%d" % t) for t in range(NT)]
    nt = NormT(k, "A_n")
    nt.run(io["h"], lambda t: io["h_deps"], gain, gain_k, xT, xT_ks)
    ws = WStream(k, "A", KC)
    qk = QKProc(k, "A_qk", 4, 128, pool_ok=io.get("pool_ok", False))
    stb = [(nt.hb[i][0][:, :].bitcast(BF16), nt.hb[i][1]) for i in range(2)]
    outs = {"QT": Tk("QT"), "KT": Tk("KT"), "V": Tk("V"), "G": Tk("G")}
    si = 0
    for nb, w, w_k in ws.stream([(io["w_in"], KC, nb * 512, 512) for nb in range(16)]):
        kind = nb // 4
        for hf in range(2):
            stf, st_k = stb[si % 2]
            si += 1
            if kind < 2:
                st = stf.rearrange("p (g t) -> p g t", g=4)
                gn, gn_k = (qg, qg_k) if kind == 0 else (kg, kg_k)

                def evac(t, ps, ps_k, st=st, st_k=st_k, gn=gn, gn_k=gn_k, hf=hf):
                    ob, ob_k = qk.run(ps[:, :].rearrange("p (g d) -> p g d", g=4), ps_k, gn, gn_k, rope, t)
                    tl = t - hf * 8
                    transpose_to(k, ob, ob_k, 4, 128, st[:, :, tl * 128:(tl + 1) * 128], st_k)
                linear_tok(k, xT, xT_ks, KC, w, w_k, 512, range(hf * 8, hf * 8 + 8), evac)
                nm = "QT" if kind == 0 else "KT"
                g0 = (nb % 4) * 4

                def out_dma(nm=nm, g0=g0, hf=hf, st=st, st_k=st_k):
                    P.dma("sp", io[nm][g0:g0 + 4, :, hf * 1024:(hf + 1) * 1024].rearrange("g d t -> d g t"), st, [st_k],
                          [outs[nm]], st_k)
                k.defer(out_dma)
            else:
                st = stf.rearrange("p (t n) -> p t n", t=8)
                func = AF.Copy if kind == 2 else AF.Silu

                def evac(t, ps, ps_k, st=st, st_k=st_k, func=func, hf=hf):
                    k.act(st[:, t - hf * 8, :], ps[:, :], func, [ps_k], [st_k])
                linear_tok(k, xT, xT_ks, KC, w, w_k, 512, range(hf * 8, hf * 8 + 8), evac)
                nm = "V" if kind == 2 else "G"
                c0 = (nb % 4) * 512
                k.flush()
                P.dma("sp", io[nm][hf * 1024:(hf + 1) * 1024, c0:c0 + 512].rearrange("(t p) n -> p t n", p=128), st, [st_k],
                      [outs[nm]], st_k)
    k.flush()
    return outs


def phaseB_diff(k, io, lam_init):
    P, nc = k.P, k.nc
    NTS = S // 128
    scale = 128 ** -0.5
    SHIFT = -8.0
    lv = [bcast_load(k, "B_lv%d" % i, io[n], 128) for i, n in enumerate(["lq1", "lk1", "lq2", "lk2"])]
    sg, sg_k = bcast_load(k, "B_sg", io["sg"], 256)
    lt, lt_k = P.sbuf("B_lt", [128, 128], F32)
    l2, l2_k = P.sbuf("B_l2", [128, 2], F32)
    lam, lam_k = P.sbuf("B_lam", [128, 1], F32)
    nlam, nlam_k = P.sbuf("B_nlam", [128, 1], F32)
    for j in range(2):
        k.tt("dve", lt[:, :], lv[2 * j][0][:, :], lv[2 * j + 1][0][:, :], ALU.mult, [lv[2 * j][1], lv[2 * j + 1][1]], [lt_k])
        k.red(l2[:, j:j + 1], lt[:, :], [lt_k], [l2_k])
    k.act(l2[:, :], l2[:, :], AF.Exp, [l2_k], [l2_k])
    k.tt("dve", lam[:, :], l2[:, 0:1], l2[:, 1:2], ALU.subtract, [l2_k], [lam_k])
    k.ts("dve", nlam[:, :], lam[:, :], lam_init, -1.0, ALU.add, ALU.mult, [lam_k], [nlam_k])
    k.ts("dve", sg[:, :], sg[:, :], 1.0 - lam_init, None, ALU.mult, None, [sg_k], [sg_k])
    bias, bias_k = P.sbuf("B_bias", [128, 1], F32)
    k.memset("dve", bias[:, :], SHIFT, [], [bias_k])

    qT = [P.sbuf("B_qT%d" % i, [128, S], BF16) for i in range(2)]
    kT = [P.sbuf("B_kT%d" % i, [128, S], BF16) for i in range(2)]
    va = [P.sbuf("B_va%d" % i, [128, NTS, 257], BF16) for i in range(2)]
    for (v_, v_k) in va:
        k.memset("dve", v_[:, :, 256:257], 1.0, [], [v_k])
    E = [P.sbuf("B_E%d" % i, [128, 512], BF16) for i in range(3)]
    o1 = [P.sbuf("B_o1_%d" % i, [128, 256], F32) for i in range(4)]
    oo = [P.sbuf("B_oo%d" % i, [128, 256], F32) for i in range(2)]
    sq = P.sbuf("B_sq", [128, 256], F32)
    rr = [P.sbuf("B_rr%d" % i, [128, 1], F32) for i in range(2)]
    ss = P.sbuf("B_ss", [128, 1], F32)
    rs = P.sbuf("B_rs", [128, 1], F32)
    ost = [P.sbuf("B_ost%d" % i, [128, 4, 256], BF16) for i in range(2)]
    O_k = Tk("O")
    ei = 0
    for hd in range(4):
        v_, v_k = va[hd % 2]
        P.dma("sp", v_[:, :, 0:256], io["V"][:, hd * 256:(hd + 1) * 256].rearrange("(t p) e -> p t e", p=128),
              io["in_deps"], [v_k], v_k)
        for qb in range(S // 512):
            os_, os_k = ost[qb % 2]
            for c in range(2):
                if qb == 0:
                    q_, q_k = qT[c]
                    k_, k_k = kT[c]
                    P.dma("sp", q_[:, :], io["QT"][hd * 2 + c, :, :], io["in_deps"], [q_k], q_k)
                    P.dma("sp", k_[:, :], io["KT"][hd * 2 + c, :, :], io["in_deps"], [k_k], k_k)
                q_, q_k = qT[c]
                k_, k_k = kT[c]
                acc = [k.pf[j] for j in range(4)]
                nkt = 4 * (qb + 1)
                for kt in range(nkt):
                    j0 = max(0, kt - 4 * qb)
                    c0 = j0 * 128
                    ps, ps_k = k.next_pf(4, 6)
                    k.mm(ps[:, c0:512], k_[:, kt * 128:(kt + 1) * 128], q_[:, qb * 512 + c0:(qb + 1) * 512], True, True,
                         [k_k, q_k], [ps_k], mark=True)
                    e_, e_k = E[ei % 3]
                    ei += 1
                    k.act(e_[:, c0:512], ps[:, c0:512], AF.Exp, [ps_k, bias_k], [e_k], scale=scale, bias=bias[:, 0:1])
                    if kt >= 4 * qb:
                        k.memset("dve", e_[64:128, c0:c0 + 64], 0.0, [], [e_k])
                    for j in range(j0, 4):
                        a_, a_k = acc[j]
                        k.mm(a_[:, 0:257], e_[:, j * 128:(j + 1) * 128], v_[:, kt, 0:257], kt == 0, kt == 4 * qb + j,
                             [e_k, v_k], [a_k], mark=(kt == 4 * qb + j))
                for j in range(4):
                    a_, a_k = acc[j]
                    r_, r_k = rr[j % 2]
                    k.recip(r_[:, :], a_[:, 256:257], [a_k], [r_k])
                    o_, o_k = o1[j]
                    if c == 0:
                        k.ts("dve", o_[:, :], a_[:, 0:256], r_[:, 0:1], None, ALU.mult, None, [a_k, r_k], [o_k])
                    else:
                        k.tt("dve", r_[:, :], r_[:, :], nlam[:, :], ALU.mult, [r_k, nlam_k], [r_k])
                        f_, f_k = oo[j % 2]
                        k.stt(f_[:, :], a_[:, 0:256], r_[:, 0:1], o_[:, :], ALU.mult, ALU.add, [a_k, r_k, o_k], [f_k])
                        k.act(sq[0][:, :], f_[:, :], AF.Square, [f_k], [sq[1]])
                        k.red(ss[0][:, :], sq[0][:, :], [sq[1]], [ss[1]])
                        k.rstd(rs[0][:, :], ss[0][:, :], 1.0 / 256, rs[1], ss[1])
                        k.stt(os_[:, j, :], f_[:, :], rs[0][:, 0:1], sg[:, :], ALU.mult, ALU.mult, [f_k, rs[1], sg_k], [os_k])
            P.dma("sp", io["O"][qb * 512:(qb + 1) * 512, hd * 256:(hd + 1) * 256].rearrange("(j p) e -> p j e", p=128),
                  os_[:, :, :], [os_k], [O_k], os_k)
    return {"O": O_k}


def phaseC(k, io, feat_major_y=False):
    P = k.P
    pg, pg_k = bcast_load(k, "C_pg", io["pg"], D)
    yT, _ = P.sbuf("C_yT", [128, KC, T], BF16)
    yT_ks = [Tk("yT%d" % t) for t in range(NT)]
    ws = WStream(k, "C", KC)
    nt = NormT(k, "C_n")
    if not feat_major_y:
        for t in range(NT):
            hbf = nt.hb[t % 2][0][:, :].bitcast(BF16)
            b_k = nt.hb[t % 2][1]
            o_, g_ = hbf[:, 0:D], hbf[:, D:2 * D]
            P.dma("sp", o_, io["O"][t * 128:(t + 1) * 128, :], io["in_deps"], [b_k], b_k)
            P.dma("sp", g_, io["G"][t * 128:(t + 1) * 128, :], io["in_deps"], [b_k], b_k)
            k.tt("dve", o_, o_, g_, ALU.mult, [b_k], [b_k])
            for half in range(2):
                pb, pb_k = k.next_pb()
                for j in range(8):
                    c = half * 8 + j
                    k.tr(pb[:, j * 128:(j + 1) * 128], o_[:, c * 128:(c + 1) * 128], [b_k], [pb_k], mark=(j == 7))
                k.copy("act" if half == 0 else "dve", yT[:, half * 8:(half + 1) * 8, t * 128:(t + 1) * 128],
                       pb[:, :].rearrange("p (j q) -> p j q", j=8), [pb_k], [yT_ks[t]])
    else:
        for c in range(KC):
            P.dma("sp", yT[:, c, :], io["YT"][c, :, :], io["in_deps"], yT_ks, yT_ks[c])
    slab = [P.sbuf("C_sl%d" % i, [128, 4, 512], F32) for i in range(3)]
    h1_ks = {}

    def slab_load(g, src, deps):
        nb, qt = (g // 4) % 4, g % 4
        sl, sl_k = slab[g % 3]
        P.dma("sp", sl[:, :, :],
              src[qt * 512:qt * 512 + 512, nb * 512:(nb + 1) * 512].rearrange("(t p) n -> p t n", p=128),
              deps(nb, qt), [sl_k], sl_k)

    slab_load(0, io["h"], lambda nb, qt: io["h_deps"])
    for nb, w, w_k in ws.stream([(io["w_out"], KC, nb * 512, 512) for nb in range(4)]):
        for qt in range(4):
            g = nb * 4 + qt
            if g + 1 < 16:
                slab_load(g + 1, io["h"], lambda nb, qt: io["h_deps"])
            sl, sl_k = slab[g % 3]
            r0 = qt * 512

            def evac(t, ps, ps_k, sl=sl, sl_k=sl_k, qt=qt):
                k.tt("dve", sl[:, t - qt * 4, :], sl[:, t - qt * 4, :], ps[:, :], ALU.add, [sl_k, ps_k], [sl_k])
            linear_tok(k, yT, yT_ks, KC, w, w_k, 512, range(qt * 4, qt * 4 + 4), evac)
            hk = Tk("h1_%d_%d" % (nb, qt))
            h1_ks[(nb, qt)] = hk
            P.dma("sp", io["h1"][r0:r0 + 512, nb * 512:(nb + 1) * 512].rearrange("(t p) n -> p t n", p=128),
                  sl[:, :, :], [sl_k], [hk], sl_k)
    nt.run(io["h1"], lambda t: [h1_ks[(nb, t // 4)] for nb in range(4)], pg, pg_k, yT, yT_ks)
    xT, xT_ks = yT, yT_ks
    pT, _ = P.sbuf("C_pT", [128, 2, T], BF16)
    pT_ks = [Tk("pT%d" % t) for t in range(NT)]
    pl = [P.sbuf("C_pl%d" % i, [128, 256], F32) for i in range(2)]
    pc = [P.sbuf("C_pc%d" % i, [128, 256], BF16) for i in range(2)]
    for t in range(NT):
        a_, a_k = pl[t % 2]
        b_, b_k = pc[t % 2]
        P.dma("sp", a_[:, :], io["p"][t * 128:(t + 1) * 128, :], [], [a_k], a_k)
        k.copy("dve", b_[:, :], a_[:, :], [a_k], [b_k])
        pb, pb_k = k.next_pb()
        for j in range(2):
            k.tr(pb[:, j * 128:(j + 1) * 128], b_[:, j * 128:(j + 1) * 128], [b_k], [pb_k], mark=(j == 1))
        k.copy("act", pT[:, :, t * 128:(t + 1) * 128], pb[:, 0:256].rearrange("p (j q) -> p j q", j=2), [pb_k], [pT_ks[t]])
    wp = WStream(k, "Cp", 2)
    gs = [P.sbuf("C_gs%d" % i, [128, 512], F32) for i in range(2)]
    out_k = Tk("hout")
    gi = 0
    slab_load(16, io["h1"], lambda nb, qt: [h1_ks[(nb % 4, qt)]])
    for nb, w, w_k in ws.stream([(io["w_g"], KC, nb * 512, 512) for nb in range(4)]):
        w2, w2_k = wp.load(io["w_p"], 2, nb * 512, 512)
        for qt in range(4):
            g = 16 + nb * 4 + qt
            if g + 1 < 32:
                slab_load(g + 1, io["h1"], lambda nb, qt: [h1_ks[(nb % 4, qt)]])
            sl, sl_k = slab[g % 3]
            r0 = qt * 512
            for t in range(qt * 4, qt * 4 + 4):
                ps, ps_k = k.next_pf(0, 3)
                for c in range(KC):
                    k.mm(ps[:, :], xT[:, c, t * 128:(t + 1) * 128], w[:, c, :], c == 0, c == KC - 1,
                         [xT_ks[t], w_k], [ps_k], mark=(c == KC - 1))
                ps2, ps2_k = k.next_pf(3, 6)
                for c in range(2):
                    k.mm(ps2[:, :], pT[:, c, t * 128:(t + 1) * 128], w2[:, c, :], c == 0, c == 1,
                         [pT_ks[t], w2_k], [ps2_k], mark=(c == 1))
                g_, g_k = gs[gi % 2]
                gi += 1
                k.act(g_[:, :], ps[:, :], AF.Sigmoid, [ps_k], [g_k])
                k.tt("dve", g_[:, :], g_[:, :], ps2[:, :], ALU.mult, [g_k, ps2_k], [g_k])
                k.tt("dve", sl[:, t - qt * 4, :], sl[:, t - qt * 4, :], g_[:, :], ALU.add, [sl_k, g_k], [sl_k])
            P.dma("sp", io["hout"][r0:r0 + 512, nb * 512:(nb + 1) * 512].rearrange("(t p) n -> p t n", p=128),
                  sl[:, :, :], [sl_k], [out_k], sl_k)
    return {"hout": out_k}


def transpose_to(k, ob, ob_k, ngrp, dn, dst3, dst_k):
    def go():
        pb, pb_k = k.next_pb()
        for g in range(ngrp):
            k.tr(pb[0:dn, g * 128:(g + 1) * 128], ob[:, g * dn:(g + 1) * dn], [ob_k], [pb_k], mark=(g == ngrp - 1))
        k.copy("act", dst3, pb[0:dn, 0:ngrp * 128].rearrange("p (g q) -> p g q", g=ngrp), [pb_k], [dst_k])
    k.defer(go)


def phaseA_mla(k, io):
    P = k.P
    gain, gain_k = bcast_load(k, "M_ng", io["ng"], D)
    cqg, cqg_k = bcast_load(k, "M_cqg", io["cqg"], 512)
    ckvg, ckvg_k = bcast_load(k, "M_ckvg", io["ckvg"], 512)
    qng, qng_k = bcast_load(k, "M_qng", io["qng"], 128)
    qrg, qrg_k = bcast_load(k, "M_qrg", io["qrg"], 64)
    kng, kng_k = bcast_load(k, "M_kng", io["kng"], 128)
    krg, krg_k = bcast_load(k, "M_krg", io["krg"], 64)
    rope = Rope(k, "M_rp", io["posT"], io["inv32"], NT, 32)
    xT, _ = P.sbuf("M_xT", [128, KC, T], BF16)
    xT_ks = [Tk("xT%d" % t) for t in range(NT)]
    nt = NormT(k, "M_n")
    nt.run(io["h"], lambda t: io["h_deps"], gain, gain_k, xT, xT_ks)
    ws = WStream(k, "M", KC)
    lat = QKProc(k, "M_lat", 1, 512, with_rope=False, nob=4, pool_ok=io.get("pool_ok", False))
    p128 = QKProc(k, "M_p128", 2, 128, with_rope=False, pool_ok=io.get("pool_ok", False))
    p64 = QKProc(k, "M_p64", 2, 64, pool_ok=io.get("pool_ok", False))
    cqT, _ = P.sbuf("M_cqT", [128, 4, T], BF16)
    ckvT, _ = P.sbuf("M_ckvT", [128, 4, T], BF16)
    cq_ks = [Tk("cq%d" % t) for t in range(NT)]
    ckv_ks = [Tk("ckv%d" % t) for t in range(NT)]
    stb = [(nt.hb[i][0][:, :].bitcast(BF16), nt.hb[i][1]) for i in range(2)]
    stc = [(nt.sq[0][:, :], nt.sq[1])]
    outs = {n: Tk(n) for n in ["QNT", "QPT", "KNT", "KPT", "V", "G"]}
    si = 0
    jobs = [(io["w_in"], KC, 0, 512), (io["w_in"], KC, 512, 512), (io["w_in"], KC, 1024, 64)] + \
           [(io["w_in"], KC, 1088 + i * 512, 512) for i in range(4)]
    for jb, w, w_k in ws.stream(jobs):
        if jb < 2:
            dstT, dks = (cqT, cq_ks) if jb == 0 else (ckvT, ckv_ks)
            gn, gn_k = (cqg, cqg_k) if jb == 0 else (ckvg, ckvg_k)

            def evac(t, ps, ps_k, dstT=dstT, dks=dks, gn=gn, gn_k=gn_k):
                ob, ob_k = lat.run(ps[:, :].rearrange("p (g d) -> p g d", g=1), ps_k, gn, gn_k, None, t, do_rope=False)
                transpose_to(k, ob, ob_k, 4, 128, dstT[:, :, t * 128:(t + 1) * 128], dks[t])
            linear_tok(k, xT, xT_ks, KC, w, w_k, 512, range(NT), evac)
        elif jb == 2:
            kp, kp_k = stc[0]

            def evac(t, ps, ps_k, kp=kp, kp_k=kp_k):
                ob, ob_k = p64.run(ps[:, 0:64].rearrange("p (g d) -> p g d", g=1), ps_k, krg, krg_k, rope, t, ngrp=1)
                transpose_to(k, ob, ob_k, 1, 64, kp[0:64, t * 128:(t + 1) * 128].rearrange("p (g q) -> p g q", g=1), kp_k)
            linear_tok(k, xT, xT_ks, KC, w, w_k, 64, range(NT), evac)
            k.flush()
            P.dma("sp", io["KPT"], kp[0:64, :], [kp_k], [outs["KPT"]], kp_k)
        else:
            for hf in range(2):
                stf, st_k = stb[si % 2]
                si += 1
                st = stf.rearrange("p (t n) -> p t n", t=8)

                def evac(t, ps, ps_k, st=st, st_k=st_k, hf=hf):
                    k.act(st[:, t - hf * 8, :], ps[:, :], AF.Silu, [ps_k], [st_k])
                linear_tok(k, xT, xT_ks, KC, w, w_k, 512, range(hf * 8, hf * 8 + 8), evac)
                k.flush()
                c0 = (jb - 3) * 512
                P.dma("sp", io["G"][hf * 1024:(hf + 1) * 1024, c0:c0 + 512].rearrange("(t p) n -> p t n", p=128), st, [st_k],
                      [outs["G"]], st_k)
    k.flush()
    ws4 = ws
    for jb, w, w_k in ws4.stream([(io["w_uq"], 4, i * 384, 384) for i in range(8)]):
        for hf in range(2):
            stf, st_k = stb[si % 2]
            si += 1
            stn = stf[:, 0:2048].rearrange("p (g t) -> p g t", g=2)
            stp = stf[:, 2048:4096].rearrange("p (g t) -> p g t", g=2)

            def evac(t, ps, ps_k, stn=stn, stp=stp, st_k=st_k, hf=hf):
                v = ps[:, 0:384].rearrange("p (g d) -> p g d", g=2)
                tl = t - hf * 8
                ob, ob_k = p128.run(v[:, :, 0:128], ps_k, qng, qng_k, None, t, do_rope=False)
                transpose_to(k, ob, ob_k, 2, 128, stn[:, :, tl * 128:(tl + 1) * 128], st_k)
                ob, ob_k = p64.run(v[:, :, 128:192], ps_k, qrg, qrg_k, rope, t)
                transpose_to(k, ob, ob_k, 2, 64, stp[0:64, :, tl * 128:(tl + 1) * 128], st_k)
            linear_tok(k, cqT, cq_ks, 4, w, w_k, 384, range(hf * 8, hf * 8 + 8), evac)

            def out_dma(jb=jb, hf=hf, stn=stn, stp=stp, st_k=st_k):
                P.dma("sp", io["QNT"][2 * jb:2 * jb + 2, :, hf * 1024:(hf + 1) * 1024].rearrange("g d t -> d g t"), stn, [st_k],
                      [outs["QNT"]], st_k)
                P.dma("sp", io["QPT"][2 * jb:2 * jb + 2, :, hf * 1024:(hf + 1) * 1024].rearrange("g d t -> d g t"), stp[0:64, :, :], [st_k],
                      [outs["QPT"]], st_k)
            k.defer(out_dma)
    for jb, w, w_k in ws4.stream([(io["w_ukv"], 4, i * 512, 512) for i in range(8)]):
        for hf in range(2):
            stf, st_k = stb[si % 2]
            si += 1
            stn = stf[:, 0:2048].rearrange("p (g t) -> p g t", g=2)
            stv = stf[:, 2048:4096].rearrange("p (t g e) -> p t g e", t=8, g=2)

            def evac(t, ps, ps_k, stn=stn, stv=stv, st_k=st_k, hf=hf):
                v = ps[:, :].rearrange("p (g d) -> p g d", g=2)
                tl = t - hf * 8
                ob, ob_k = p128.run(v[:, :, 0:128], ps_k, kng, kng_k, None, t, do_rope=False)
                transpose_to(k, ob, ob_k, 2, 128, stn[:, :, tl * 128:(tl + 1) * 128], st_k)
                k.act(stv[:, tl, :, :], v[:, :, 128:256], AF.Copy, [ps_k], [st_k])
            linear_tok(k, ckvT, ckv_ks, 4, w, w_k, 512, range(hf * 8, hf * 8 + 8), evac)

            def out_dma(jb=jb, hf=hf, stn=stn, stv=stv, st_k=st_k):
                P.dma("sp", io["KNT"][2 * jb:2 * jb + 2, :, hf * 1024:(hf + 1) * 1024].rearrange("g d t -> d g t"), stn, [st_k],
                      [outs["KNT"]], st_k)
                P.dma("sp", io["V"][hf * 1024:(hf + 1) * 1024, jb * 256:(jb + 1) * 256].rearrange("(t p) (g e) -> p t g e", p=128, g=2),
                      stv, [st_k], [outs["V"]], st_k)
            k.defer(out_dma)
    k.flush()
    return outs


def phaseB_mla(k, io):
    P = k.P
    NTS = S // 128
    scale = 192 ** -0.5
    SHIFT = -8.0
    bias, bias_k = P.sbuf("B_bias", [128, 1], F32)
    k.memset("dve", bias[:, :], SHIFT, [], [bias_k])
    kp, kp_k = P.sbuf("B_kp", [64, S], BF16)
    P.dma("sp", kp[:, :], io["KPT"], io["in_deps"], [kp_k], kp_k)
    qn = [P.sbuf("B_qn%d" % i, [128, S], BF16) for i in range(2)]
    qp = [P.sbuf("B_qp%d" % i, [64, S], BF16) for i in range(2)]
    kn = [P.sbuf("B_kn%d" % i, [128, S], BF16) for i in range(2)]
    va = [P.sbuf("B_va%d" % i, [128, NTS, 129], BF16) for i in range(2)]
    for (v_, v_k) in va:
        k.memset("dve", v_[:, :, 128:129], 1.0, [], [v_k])
    E = [P.sbuf("B_E%d" % i, [128, 512], BF16) for i in range(3)]
    rr = [P.sbuf("B_rr%d" % i, [128, 1], F32) for i in range(2)]
    ost = [P.sbuf("B_ost%d" % i, [128, 4, 128], BF16) for i in range(2)]
    O_k = Tk("O")
    ei = 0
    oi = 0
    for hd in range(8):
        v_, v_k = va[hd % 2]
        q_, q_k = qn[hd % 2]
        qp_, qp_k = qp[hd % 2]
        k_, k_k = kn[hd % 2]
        P.dma("sp", v_[:, :, 0:128], io["V"][:, hd * 128:(hd + 1) * 128].rearrange("(t p) e -> p t e", p=128),
              io["in_deps"], [v_k], v_k)
        P.dma("sp", q_[:, :], io["QNT"][hd, :, :], io["in_deps"], [q_k], q_k)
        P.dma("sp", qp_[:, :], io["QPT"][hd, :, :], io["in_deps"], [qp_k], qp_k)
        P.dma("sp", k_[:, :], io["KNT"][hd, :, :], io["in_deps"], [k_k], k_k)
        for qb in range(S // 512):
            os_, os_k = ost[oi % 2]
            oi += 1
            acc = [k.pf[j] for j in range(4)]
            nkt = 4 * (qb + 1)
            for kt in range(nkt):
                j0 = max(0, kt - 4 * qb)
                c0 = j0 * 128
                ps, ps_k = k.next_pf(4, 6)
                k.mm(ps[:, c0:512], k_[:, kt * 128:(kt + 1) * 128], q_[:, qb * 512 + c0:(qb + 1) * 512], True, False,
                     [k_k, q_k], [ps_k], mark=False)
                k.mm(ps[:, c0:512], kp[:, kt * 128:(kt + 1) * 128], qp_[:, qb * 512 + c0:(qb + 1) * 512], False, True,
                     [kp_k, qp_k], [ps_k], mark=True)
                e_, e_k = E[ei % 3]
                ei += 1
                k.act(e_[:, c0:512], ps[:, c0:512], AF.Exp, [ps_k, bias_k], [e_k], scale=scale, bias=bias[:, 0:1])
                if kt >= 4 * qb:
                    k.memset("dve", e_[64:128, c0:c0 + 64], 0.0, [], [e_k])
                for j in range(j0, 4):
                    a_, a_k = acc[j]
                    k.mm(a_[:, 0:129], e_[:, j * 128:(j + 1) * 128], v_[:, kt, 0:129], kt == 0, kt == 4 * qb + j,
                         [e_k, v_k], [a_k], mark=(kt == 4 * qb + j))
            for j in range(4):
                a_, a_k = acc[j]
                r_, r_k = rr[j % 2]
                k.recip(r_[:, :], a_[:, 128:129], [a_k], [r_k])
                k.ts("dve", os_[:, j, :], a_[:, 0:128], r_[:, 0:1], None, ALU.mult, None, [a_k, r_k], [os_k])
            P.dma("sp", io["O"][qb * 512:(qb + 1) * 512, hd * 128:(hd + 1) * 128].rearrange("(j p) e -> p j e", p=128),
                  os_[:, :, :], [os_k], [O_k], os_k)
    return {"O": O_k}


def phaseA_lru(k, io):
    P = k.P
    gain, gain_k = bcast_load(k, "R_ng", io["ng"], D)
    xT, _ = P.sbuf("R_xT", [128, KC, T], BF16)
    xT_ks = [Tk("xT%d" % t) for t in range(NT)]
    nt = NormT(k, "R_n")
    nt.run(io["h"], lambda t: io["h_deps"], gain, gain_k, xT, xT_ks)
    ws = WStream(k, "R", KC)
    stx = [P.sbuf("R_stx%d" % i, [128, T], F32) for i in range(2)]
    stg = [(nt.hb[i][0][:, :].bitcast(BF16)[:, 0:T], nt.hb[i][1]) for i in range(2)]
    outs = {"XBT": Tk("XBT"), "GT": Tk("GT")}
    si = 0
    for nb, w, w_k in ws.stream([(io["w_in"], KC, nb * 512, 512) for nb in range(8)]):
        for fl in range(4):
            fc = (nb % 4) * 4 + fl
            if nb < 4:
                st, st_k = stx[si % 2]
                st = st[:, :]
            else:
                st, st_k = stg[si % 2]
            si += 1
            for tb in range(4):
                ps, ps_k = k.next_pf(0, 3)
                for c in range(KC):
                    k.mm(ps[:, :], w[:, c, fl * 128:(fl + 1) * 128], xT[:, c, tb * 512:(tb + 1) * 512], c == 0, c == KC - 1,
                         [w_k] + xT_ks[tb * 4:tb * 4 + 4], [ps_k], mark=(c == KC - 1))
                k.act(st[:, tb * 512:(tb + 1) * 512], ps[:, :], AF.Copy if nb < 4 else AF.Silu, [ps_k], [st_k])
            nm = "XBT" if nb < 4 else "GT"
            P.dma("sp", io[nm][fc, :, :], st, [st_k], [outs[nm]], st_k)
    return outs


def phaseB_lru(k, io):
    P, nc = k.P, k.nc
    cw, cw_k = P.sbuf("L_cw", [128, 8, 4], F32)
    P.dma("sp", cw[:, :, :], io["cw"], [], [cw_k], cw_k)
    small = {}
    for n in ["cb", "ba", "bx", "lam"]:
        t_, t_k = P.sbuf("L_" + n, [128, 8], F32)
        P.dma("sp", t_[:, :], io[n], [], [t_k], t_k)
        small[n] = (t_, t_k)
    ce, ce_k = P.sbuf("L_ce", [128, 8], F32)
    k.act(ce[:, :], small["lam"][0][:, :], AF.Exp, [small["lam"][1]], [ce_k], scale=-1.0)
    k.act(ce[:, :], ce[:, :], AF.Ln, [ce_k], [ce_k], bias=1.0)
    k.ts("dve", ce[:, :], ce[:, :], -8.0, None, ALU.mult, None, [ce_k], [ce_k])
    xb = [P.sbuf("L_xb%d" % i, [128, S], F32) for i in range(2)]
    xc = [P.sbuf("L_xc%d" % i, [128, S], F32) for i in range(2)]
    xcb = [P.sbuf("L_xcb%d" % i, [128, S], BF16) for i in range(2)]
    rg = [P.sbuf("L_r%d" % i, [128, S], F32) for i in range(2)]
    ig = [P.sbuf("L_i%d" % i, [128, S], F32) for i in range(2)]
    tmp = P.sbuf("L_tmp", [128, S], F32)
    gt = [P.sbuf("L_g%d" % i, [128, S], BF16) for i in range(2)]
    wa = P.sbuf("L_wa", [128, 2, 256], BF16)
    wx = P.sbuf("L_wx", [128, 2, 256], BF16)
    Y_k = Tk("YT")
    for gb in range(4):
        P.dma("pool", wa[0][:, :, :], io["wa"][gb].rearrange("(c p) n -> p c n", p=128), [], [wa[1]], wa[1])
        P.dma("pool", wx[0][:, :, :], io["wx"][gb].rearrange("(c p) n -> p c n", p=128), [], [wx[1]], wx[1])
        for ic in range(2):
            ch = gb * 2 + ic
            b_, b_k = xb[ic]
            c_, c_k = xc[ic]
            cb_, cb_k = xcb[ic]
            P.dma("sp", b_[:, :], io["XBT"][ch, :, :], io["in_deps"], [b_k], b_k)
            P.dma("sp", gt[ic][0][:, :], io["GT"][ch, :, :], io["in_deps"], [gt[ic][1]], gt[ic][1])
            k.act(c_[:, :], b_[:, :], AF.Identity, [b_k, cw_k, small["cb"][1]], [c_k],
                  scale=cw[:, ch, 3:4], bias=small["cb"][0][:, ch:ch + 1])
            for sh in (1, 2, 3):
                k.stt(c_[:, sh:S], b_[:, 0:S - sh], cw[:, ch, 3 - sh:4 - sh], c_[:, sh:S], ALU.mult, ALU.add,
                      [b_k, cw_k, c_k], [c_k])
            k.copy("act", cb_[:, :], c_[:, :], [c_k], [cb_k])
        for oc in range(2):
            ch = gb * 2 + oc
            for (wt, gate, bname) in ((wa, rg, "ba"), (wx, ig, "bx")):
                g_, g_k = gate[oc]
                for tb in range(S // 512):
                    ps, ps_k = k.next_pf(0, 4)
                    for ic in range(2):
                        k.mm(ps[:, :], wt[0][:, ic, oc * 128:(oc + 1) * 128], xcb[ic][0][:, tb * 512:(tb + 1) * 512], ic == 0, ic == 1,
                             [wt[1], xcb[ic][1]], [ps_k], mark=(ic == 1))
                    k.act(g_[:, tb * 512:(tb + 1) * 512], ps[:, :], AF.Sigmoid, [ps_k, small[bname][1]], [g_k],
                          bias=small[bname][0][:, ch:ch + 1])
            r_, r_k = rg[oc]
            i_, i_k = ig[oc]
            c_, c_k = xc[oc]
            t_, t_k = tmp
            h_, h_k = xb[oc]
            k.act(r_[:, :], r_[:, :], AF.Exp, [r_k, ce_k], [r_k], scale=ce[:, ch:ch + 1])
            k.tt("dve", t_[:, :], r_[:, :], r_[:, :], ALU.mult, [r_k], [t_k])
            k.ts("dve", t_[:, :], t_[:, :], -1.0, 1.0, ALU.mult, ALU.add, [t_k], [t_k])
            k.act(t_[:, :], t_[:, :], AF.Sqrt, [t_k], [t_k])
            k.tt("dve", i_[:, :], i_[:, :], c_[:, :], ALU.mult, [i_k, c_k], [i_k])
            k.tt("dve", i_[:, :], i_[:, :], t_[:, :], ALU.mult, [i_k, t_k], [i_k])
            P.op("dve", lambda: nc.vector.tensor_tensor_scan(out=h_[:, :], data0=r_[:, :], data1=i_[:, :], initial=0.0,
                                                             op0=ALU.mult, op1=ALU.add), [r_k, i_k], [h_k])
            y_, y_k = gt[oc]
            k.tt("dve", y_[:, :], h_[:, :], y_[:, :], ALU.mult, [h_k, y_k], [y_k])
            P.dma("sp", io["YT"][ch, :, :], y_[:, :], [y_k], [Y_k], y_k)
    return {"YT": Y_k}


IDENT = np.eye(128, dtype=np.float32)
INV64 = (10000.0 ** (-np.arange(64, dtype=np.float32) / np.float32(64))).astype(np.float32)
INV32 = (10000.0 ** (-np.arange(32, dtype=np.float32) / np.float32(32))).astype(np.float32)
BF = ml_dtypes.bfloat16


class Launch:
    def __init__(self):
        self.nc = bass.Bass("TRN2", target_bir_lowering=False)
        self.stack = contextlib.ExitStack()
        self.P = Prog(self.nc, self.stack)
        self.ident_d = self.inp("ident", [128, 128], F32)
        self.k = K(self.nc, self.P, self.ident_d)
        self.outs = []

    def inp(self, name, shape, dt):
        return self.nc.dram_tensor(name, list(shape), dt, kind="ExternalInput").ap()

    def out(self, name, shape, dt):
        self.outs.append(name)
        return self.nc.dram_tensor(name, list(shape), dt, kind="ExternalOutput").ap()

    def scratch(self, name, shape, dt):
        return self.nc.dram_tensor(name, list(shape), dt, kind="Internal").ap()

    def run(self, in_maps, final_tks):
        self.P.finish(final_tks)
        self.stack.close()
        for m in in_maps:
            m["ident"] = IDENT
        res = run_bass_kernel_spmd(self.nc, in_maps, core_ids=list(range(NCORES)))
        return res.results


def posT_of(positions, c):
    b, hf = c // 2, c % 2
    return np.ascontiguousarray(positions[b, hf * T:(hf + 1) * T].reshape(NT, 128).T.astype(np.int32))


def launch_A_diff(h_cores, positions, ng, w_in, qg, kg):
    L = Launch()
    io = {"h": L.inp("h", [T, D], F32), "ng": L.inp("ng", [D], F32), "w_in": L.inp("w_in", [D, 8192], F32),
          "qg": L.inp("qg", [128], F32), "kg": L.inp("kg", [128], F32), "posT": L.inp("posT", [128, NT], I32),
          "inv64": L.inp("inv64", [64], F32), "h_deps": [],
          "QT": L.out("QT", [16, 128, T], BF16), "KT": L.out("KT", [16, 128, T], BF16),
          "V": L.out("V", [T, 2048], BF16), "G": L.out("G", [T, 2048], BF16)}
    outs = phaseA_diff(L.k, io)
    in_maps = [{"h": h_cores[c], "ng": ng, "w_in": w_in, "qg": qg, "kg": kg, "posT": posT_of(positions, c),
                "inv64": INV64} for c in range(NCORES)]
    return L.run(in_maps, list(outs.values()))


def launch_B_diff(resA, lq1, lk1, lq2, lk2, sg, lam_init):
    L = Launch()
    io = {"QT": L.inp("QT", [8, 128, S], BF16), "KT": L.inp("KT", [8, 128, S], BF16), "V": L.inp("V", [S, 1024], BF16),
          "lq1": L.inp("lq1", [128], F32), "lk1": L.inp("lk1", [128], F32), "lq2": L.inp("lq2", [128], F32),
          "lk2": L.inp("lk2", [128], F32), "sg": L.inp("sg", [256], F32), "in_deps": [],
          "O": L.out("O", [S, 1024], BF16)}
    outs = phaseB_diff(L.k, io, lam_init)
    in_maps = []
    for c in range(NCORES):
        b, g = c // 2, c % 2
        r0, r1 = resA[2 * b], resA[2 * b + 1]
        in_maps.append({
            "QT": np.concatenate([r0["QT"][g * 8:(g + 1) * 8], r1["QT"][g * 8:(g + 1) * 8]], axis=2),
            "KT": np.concatenate([r0["KT"][g * 8:(g + 1) * 8], r1["KT"][g * 8:(g + 1) * 8]], axis=2),
            "V": np.concatenate([r0["V"][:, g * 1024:(g + 1) * 1024], r1["V"][:, g * 1024:(g + 1) * 1024]], axis=0),
            "lq1": lq1, "lk1": lk1, "lq2": lq2, "lk2": lk2, "sg": sg})
    return L.run(in_maps, list(outs.values()))


def gather_O(resB, width):
    out = []
    for c in range(NCORES):
        b, hf = c // 2, c % 2
        out.append(np.concatenate([resB[2 * b]["O"][hf * T:(hf + 1) * T], resB[2 * b + 1]["O"][hf * T:(hf + 1) * T]], axis=1))
    return out


def launch_C(h_cores, O_cores, G_cores, w_out, pg, w_g, w_p, p_l):
    L = Launch()
    io = {"h": L.inp("h", [T, D], F32), "O": L.inp("O", [T, D], BF16), "G": L.inp("G", [T, D], BF16),
          "w_out": L.inp("w_out", [D, D], F32), "pg": L.inp("pg", [D], F32), "w_g": L.inp("w_g", [D, D], F32),
          "w_p": L.inp("w_p", [256, D], F32), "p": L.inp("p", [T, 256], F32), "in_deps": [], "h_deps": [],
          "h1": L.scratch("h1", [T, D], F32), "hout": L.out("hout", [T, D], F32)}
    outs = phaseC(L.k, io)
    in_maps = []
    for c in range(NCORES):
        b, hf = c // 2, c % 2
        in_maps.append({"h": h_cores[c], "O": O_cores[c], "G": G_cores[c], "w_out": w_out, "pg": pg, "w_g": w_g,
                        "w_p": w_p, "p": np.ascontiguousarray(p_l[b, hf * T:(hf + 1) * T])})
    return L.run(in_maps, list(outs.values()))


def split_cores(x):
    return [np.ascontiguousarray(x[c // 2, (c % 2) * T:(c % 2 + 1) * T]) for c in range(NCORES)]


def layer_diff(h_cores, positions, li, j, I):
    lam_init = 0.8 - 0.6 * math.exp(-0.3 * li)
    rA = launch_A_diff(h_cores, positions, I["norm_gain"][li], I["a_w_in"][j], I["a_q_norm"][j], I["a_k_norm"][j])
    rB = launch_B_diff(rA, I["a_lambda_q1"][j], I["a_lambda_k1"][j], I["a_lambda_q2"][j], I["a_lambda_k2"][j],
                       I["a_sub_norm"][j], lam_init)
    O = gather_O(rB, 1024)
    rC = launch_C(h_cores, O, [r["G"] for r in rA], I["a_w_out"][j], I["ple_norm"][li], I["ple_w_gate"][li],
                  I["ple_w_proj"][li], I["p"][li])
    return [r["hout"] for r in rC], (rA, rB, rC)


def launch_A_mla(h_cores, positions, I, li, j):
    L = Launch()
    io = {"h": L.inp("h", [T, D], F32), "ng": L.inp("ng", [D], F32), "w_in": L.inp("w_in", [D, 3136], F32),
          "cqg": L.inp("cqg", [512], F32), "ckvg": L.inp("ckvg", [512], F32),
          "w_uq": L.inp("w_uq", [512, 3072], F32), "w_ukv": L.inp("w_ukv", [512, 4096], F32),
          "qng": L.inp("qng", [128], F32), "qrg": L.inp("qrg", [64], F32), "kng": L.inp("kng", [128], F32),
          "krg": L.inp("krg", [64], F32), "posT": L.inp("posT", [128, NT], I32), "inv32": L.inp("inv32", [32], F32),
          "h_deps": [],
          "QNT": L.out("QNT", [16, 128, T], BF16), "QPT": L.out("QPT", [16, 64, T], BF16),
          "KNT": L.out("KNT", [16, 128, T], BF16), "KPT": L.out("KPT", [64, T], BF16),
          "V": L.out("V", [T, 2048], BF16), "G": L.out("G", [T, 2048], BF16)}
    outs = phaseA_mla(L.k, io)
    in_maps = [{"h": h_cores[c], "ng": I["norm_gain"][li], "w_in": I["b_w_in"][j], "cqg": I["b_cq_norm"][j],
                "ckvg": I["b_ckv_norm"][j], "w_uq": I["b_w_uq"][j], "w_ukv": I["b_w_ukv"][j],
                "qng": I["b_q_nope_norm"][j], "qrg": I["b_q_rope_norm"][j], "kng": I["b_k_nope_norm"][j],
                "krg": I["b_k_rope_norm"][j], "posT": posT_of(positions, c), "inv32": INV32} for c in range(NCORES)]
    return L.run(in_maps, list(outs.values()))


def launch_B_mla(resA):
    L = Launch()
    io = {"QNT": L.inp("QNT", [8, 128, S], BF16), "QPT": L.inp("QPT", [8, 64, S], BF16),
          "KNT": L.inp("KNT", [8, 128, S], BF16), "KPT": L.inp("KPT", [64, S], BF16),
          "V": L.inp("V", [S, 1024], BF16), "in_deps": [], "O": L.out("O", [S, 1024], BF16)}
    outs = phaseB_mla(L.k, io)
    in_maps = []
    for c in range(NCORES):
        b, g = c // 2, c % 2
        r0, r1 = resA[2 * b], resA[2 * b + 1]
        cat = lambda n, sl, ax: np.concatenate([r0[n][sl], r1[n][sl]], axis=ax)
        in_maps.append({
            "QNT": cat("QNT", slice(g * 8, (g + 1) * 8), 2), "QPT": cat("QPT", slice(g * 8, (g + 1) * 8), 2),
            "KNT": cat("KNT", slice(g * 8, (g + 1) * 8), 2), "KPT": cat("KPT", slice(None), 1),
            "V": np.concatenate([r0["V"][:, g * 1024:(g + 1) * 1024], r1["V"][:, g * 1024:(g + 1) * 1024]], axis=0)})
    return L.run(in_maps, list(outs.values()))


def launch_A_lru(h_cores, I, li, j):
    L = Launch()
    io = {"h": L.inp("h", [T, D], F32), "ng": L.inp("ng", [D], F32), "w_in": L.inp("w_in", [D, 4096], F32), "h_deps": [],
          "XBT": L.out("XBT", [16, 128, T], F32), "GT": L.out("GT", [16, 128, T], BF16)}
    outs = phaseA_lru(L.k, io)
    in_maps = [{"h": h_cores[c], "ng": I["norm_gain"][li], "w_in": I["c_w_in"][j]} for c in range(NCORES)]
    return L.run(in_maps, list(outs.values()))


def launch_B_lru(resA, I, j):
    L = Launch()
    io = {"XBT": L.inp("XBT", [8, 128, S], F32), "GT": L.inp("GT", [8, 128, S], BF16),
          "cw": L.inp("cw", [128, 8, 4], F32), "cb": L.inp("cb", [128, 8], F32), "ba": L.inp("ba", [128, 8], F32),
          "bx": L.inp("bx", [128, 8], F32), "lam": L.inp("lam", [128, 8], F32),
          "wa": L.inp("wa", [4, 256, 256], F32), "wx": L.inp("wx", [4, 256, 256], F32), "in_deps": [],
          "YT": L.out("YT", [8, 128, S], BF16)}
    outs = phaseB_lru(L.k, io)
    in_maps = []
    fm = lambda v, g: np.ascontiguousarray(v[g * 1024:(g + 1) * 1024].reshape(8, 128).T)
    for c in range(NCORES):
        b, g = c // 2, c % 2
        r0, r1 = resA[2 * b], resA[2 * b + 1]
        cwl = I["c_conv_w"][j][:, g * 1024:(g + 1) * 1024]
        in_maps.append({
            "XBT": np.concatenate([r0["XBT"][g * 8:(g + 1) * 8], r1["XBT"][g * 8:(g + 1) * 8]], axis=2),
            "GT": np.concatenate([r0["GT"][g * 8:(g + 1) * 8], r1["GT"][g * 8:(g + 1) * 8]], axis=2),
            "cw": np.ascontiguousarray(cwl.reshape(4, 8, 128).transpose(2, 1, 0)),
            "cb": fm(I["c_conv_b"][j], g), "ba": fm(I["c_b_a"][j], g), "bx": fm(I["c_b_x"][j], g), "lam": fm(I["c_lambda"][j], g),
            "wa": np.ascontiguousarray(I["c_w_a"][j][g * 4:(g + 1) * 4]), "wx": np.ascontiguousarray(I["c_w_x"][j][g * 4:(g + 1) * 4])})
    return L.run(in_maps, list(outs.values()))


def launch_C_lru(h_cores, YT_cores, w_out, pg, w_g, w_p, p_l):
    L = Launch()
    io = {"h": L.inp("h", [T, D], F32), "YT": L.inp("YT", [KC, 128, T], BF16),
          "w_out": L.inp("w_out", [D, D], F32), "pg": L.inp("pg", [D], F32), "w_g": L.inp("w_g", [D, D], F32),
          "w_p": L.inp("w_p", [256, D], F32), "p": L.inp("p", [T, 256], F32), "in_deps": [], "h_deps": [],
          "h1": L.scratch("h1", [T, D], F32), "hout": L.out("hout", [T, D], F32)}
    outs = phaseC(L.k, io, feat_major_y=True)
    in_maps = []
    for c in range(NCORES):
        b, hf = c // 2, c % 2
        in_maps.append({"h": h_cores[c], "YT": YT_cores[c], "w_out": w_out, "pg": pg, "w_g": w_g,
                        "w_p": w_p, "p": np.ascontiguousarray(p_l[b, hf * T:(hf + 1) * T])})
    return L.run(in_maps, list(outs.values()))


def layer_mla(h_cores, positions, li, j, I):
    rA = launch_A_mla(h_cores, positions, I, li, j)
    rB = launch_B_mla(rA)
    O = gather_O(rB, 1024)
    rC = launch_C(h_cores, O, [r["G"] for r in rA], I["b_w_out"][j], I["ple_norm"][li], I["ple_w_gate"][li],
                  I["ple_w_proj"][li], I["p"][li])
    return [r["hout"] for r in rC], (rA, rB, rC)


def layer_lru(h_cores, li, j, I):
    rA = launch_A_lru(h_cores, I, li, j)
    rB = launch_B_lru(rA, I, j)
    YT = []
    for c in range(NCORES):
        b, hf = c // 2, c % 2
        YT.append(np.concatenate([rB[2 * b]["YT"][:, :, hf * T:(hf + 1) * T], rB[2 * b + 1]["YT"][:, :, hf * T:(hf + 1) * T]], axis=0))
    rC = launch_C_lru(h_cores, YT, I["c_w_out"][j], I["ple_norm"][li], I["ple_w_gate"][li], I["ple_w_proj"][li], I["p"][li])
    return [r["hout"] for r in rC], (rA, rB, rC)


NEG = -30000.0
SHIFT = -8.0
PAIRS = [[0, 1], [2, 3], [4, 5], [6, 7]]


def attn_core(k, io, nheads_c, dv, load_head, score_mm, finish_q, mb, mb_k, bias, bias_k, scale):
    P = k.P
    LA = 3
    NE = LA + 2
    E = [P.sbuf("B_E%d" % i, [128, 512], BF16) for i in range(NE)]
    st = {"ei": 0}
    units = [(qb, ktg) for qb in range(T // 512) for ktg in range(8 * qb + 8)]
    acc = [k.pf[j] for j in range(4)]

    def emit_score(u, rd):
        qb, ktg = units[u]
        rank, j = ktg % 2, ktg // 2
        d = ktg - 8 * qb
        j0 = 0 if d < 0 else d // 2
        c0 = j0 * 128
        ps, ps_k = k.next_pf(4, 8)
        score_mm(ps, ps_k, c0, rank, j, qb, rd)
        e_, e_k = E[st["ei"] % NE]
        st["ei"] += 1
        if d < 0:
            k.act(e_[:, c0:512], ps[:, c0:512], AF.Exp, [ps_k, bias_k], [e_k], scale=scale, bias=bias[:, 0:1])
        else:
            if d % 2 == 0:
                b0, b1 = mb[:, 0:1], bias[:, 0:1]
            else:
                b0, b1 = mb[:, 1:2], mb[:, 2:3]
            k.act(e_[:, c0:c0 + 64], ps[:, c0:c0 + 64], AF.Exp, [ps_k, mb_k, bias_k], [e_k], scale=scale, bias=b0)
            k.act(e_[:, c0 + 64:c0 + 128], ps[:, c0 + 64:c0 + 128], AF.Exp, [ps_k, mb_k, bias_k], [e_k], scale=scale, bias=b1)
            if c0 + 128 < 512:
                k.act(e_[:, c0 + 128:512], ps[:, c0 + 128:512], AF.Exp, [ps_k, bias_k], [e_k], scale=scale, bias=bias[:, 0:1])
        return (e_, e_k, j0, rank, j)

    nxt_head = load_head(0)
    for hc in range(nheads_c):
        rd, va, va_k = nxt_head
        pendq = [emit_score(u, rd) for u in range(LA)]
        if hc + 1 < nheads_c:
            nxt_head = load_head(hc + 1)
        if io.get("bg_hook"):
            pace = Tk("pace")
            pace.w = list(acc[3][1].w)
            io["bg_hook"]([pace])
        for u in range(len(units)):
            if u + LA < len(units):
                pendq.append(emit_score(u + LA, rd))
            qb, ktg = units[u]
            e_, e_k, j0, rank, j = pendq.pop(0)
            for jq in range(j0, 4):
                a_, a_k = acc[jq]
                last = 8 * qb + 2 * jq + 1
                k.mm(a_[:, 0:dv + 1], e_[:, jq * 128:(jq + 1) * 128], va[:, rank * 16 + j, 0:dv + 1], ktg == 0, ktg == last,
                     [e_k, va_k], [a_k], mark=(ktg == last))
            if ktg == 8 * qb + 7:
                finish_q(hc, qb, acc)


def phaseB2_diff(k, io, lam_init):
    P = k.P
    scale = 128 ** -0.5
    lv = [bcast_load(k, "B_lv%d" % i, io[n], 128) for i, n in enumerate(["lq1", "lk1", "lq2", "lk2"])]
    sg, sg_k = bcast_load(k, "B_sg", io["sg"], 256)
    lt, lt_k = P.sbuf("B_lt", [128, 128], F32)
    l2, l2_k = P.sbuf("B_l2", [128, 2], F32)
    lam, lam_k = P.sbuf("B_lam", [128, 1], F32)
    nlam, nlam_k = P.sbuf("B_nlam", [128, 1], F32)
    for j in range(2):
        k.tt("dve", lt[:, :], lv[2 * j][0][:, :], lv[2 * j + 1][0][:, :], ALU.mult, [lv[2 * j][1], lv[2 * j + 1][1]], [lt_k])
        k.red(l2[:, j:j + 1], lt[:, :], [lt_k], [l2_k])
    k.act(l2[:, :], l2[:, :], AF.Exp, [l2_k], [l2_k])
    k.tt("dve", lam[:, :], l2[:, 0:1], l2[:, 1:2], ALU.subtract, [l2_k], [lam_k])
    k.ts("dve", nlam[:, :], lam[:, :], lam_init, -1.0, ALU.add, ALU.mult, [lam_k], [nlam_k])
    k.ts("dve", sg[:, :], sg[:, :], 1.0 - lam_init, None, ALU.mult, None, [sg_k], [sg_k])
    bias, bias_k = P.sbuf("B_bias", [128, 1], F32)
    k.memset("dve", bias[:, :], SHIFT, [], [bias_k])
    mb, mb_k = P.sbuf("B_mb", [128, 3], F32)
    P.dma("sp", mb[:, :], io["mb"], [], [mb_k], mb_k)
    qT = [P.sbuf("B_qT%d" % i, [128, T], BF16) for i in range(2)]
    kT = [P.sbuf("B_kT%d" % i, [128, 2, T], BF16) for i in range(2)]
    va = [P.sbuf("B_va%d" % i, [128, 32, 257], BF16) for i in range(2)]
    for (v_, v_k) in va:
        k.memset("dve", v_[:, :, 256:257], 1.0, [], [v_k])
    o1 = [P.sbuf("B_o1_%d" % i, [128, 256], F32) for i in range(4)]
    oo = [P.sbuf("B_oo%d" % i, [128, 256], F32) for i in range(2)]
    sq = P.sbuf("B_sq", [128, 256], F32)
    rr = [P.sbuf("B_rr%d" % i, [128, 1], F32) for i in range(2)]
    ss = P.sbuf("B_ss", [128, 1], F32)
    rs = P.sbuf("B_rs", [128, 1], F32)
    ost = [P.sbuf("B_ost%d" % i, [128, 4, 256], BF16) for i in range(2)]
    O_k = Tk("O")
    state = {"oi": 0}

    def load_head(hc):
        hd, c = hc // 2, hc % 2
        v_, v_k = va[hd % 2]
        if c == 0:
            for rank in range(2):
                for i in range(4):
                    P.dma("sp", v_[:, rank * 16 + 4 * i:rank * 16 + 4 * i + 4, 0:256],
                          io["Vall"](rank, i)[:, hd * 256:(hd + 1) * 256].rearrange("(t p) e -> p t e", p=128),
                          io["in_deps"], [v_k], v_k)
        q_, q_k = qT[hc % 2]
        k_, k_k = kT[hc % 2]
        P.dma("sp", q_[:, :], io["QT"][hc, :, :], io["q_deps"], [q_k], q_k)
        P.dma("sp", k_[:, :, :], io["KTall"](hc).rearrange("r d t -> d r t"), io["in_deps"], [k_k], k_k)
        return (q_, q_k, k_, k_k), v_, v_k

    def score_mm(ps, ps_k, c0, rank, j, qb, rd):
        q_, q_k, k_, k_k = rd
        k.mm(ps[:, c0:512], k_[:, rank, j * 128:(j + 1) * 128], q_[:, qb * 512 + c0:(qb + 1) * 512], True, True,
             [k_k, q_k], [ps_k], mark=True)

    def finish_q(hc, qb, acc):
        hd, c = hc // 2, hc % 2
        for j in range(4):
            a_, a_k = acc[j]
            r_, r_k = rr[j % 2]
            k.recip(r_[:, :], a_[:, 256:257], [a_k], [r_k])
            o_, o_k = o1[j]
            if c == 0:
                k.ts("dve", io["_o1"][qb][0][:, j, :], a_[:, 0:256], r_[:, 0:1], None, ALU.mult, None, [a_k, r_k], [io["_o1"][qb][1]])
            else:
                os_, os_k = ost[state["oi"] % 2]
                k.tt("dve", r_[:, :], r_[:, :], nlam[:, :], ALU.mult, [r_k, nlam_k], [r_k])
                f_, f_k = oo[j % 2]
                k.stt(f_[:, :], a_[:, 0:256], r_[:, 0:1], io["_o1"][qb][0][:, j, :], ALU.mult, ALU.add,
                      [a_k, r_k, io["_o1"][qb][1]], [f_k])
                k.act(sq[0][:, :], f_[:, :], AF.Square, [f_k], [sq[1]])
                k.red(ss[0][:, :], sq[0][:, :], [sq[1]], [ss[1]])
                k.rstd(rs[0][:, :], ss[0][:, :], 1.0 / 256, rs[1], ss[1])
                k.stt(os_[:, j, :], f_[:, :], rs[0][:, 0:1], sg[:, :], ALU.mult, ALU.mult, [f_k, rs[1], sg_k], [os_k])
        if c == 1:
            os_, os_k = ost[state["oi"] % 2]
            state["oi"] += 1
            P.dma("sp", io["O"][qb * 512:(qb + 1) * 512, hd * 256:(hd + 1) * 256].rearrange("(j p) e -> p j e", p=128),
                  os_[:, :, :], [os_k], [O_k], os_k)

    io["_o1"] = [P.sbuf("B_o1q%d" % i, [128, 4, 256], F32) for i in range(4)]
    attn_core(k, io, 16, 256, load_head, score_mm, finish_q, mb, mb_k, bias, bias_k, scale)
    return {"O": O_k}


def phaseB2_mla(k, io):
    P = k.P
    scale = 192 ** -0.5
    bias, bias_k = P.sbuf("B_bias", [128, 1], F32)
    k.memset("dve", bias[:, :], SHIFT, [], [bias_k])
    mb, mb_k = P.sbuf("B_mb", [128, 3], F32)
    P.dma("sp", mb[:, :], io["mb"], [], [mb_k], mb_k)
    kp, kp_k = P.sbuf("B_kp", [64, 2, T], BF16)
    P.dma("sp", kp[:, :, :], io["KPall"].rearrange("r d t -> d r t"), io["in_deps"], [kp_k], kp_k)
    qn = [P.sbuf("B_qn%d" % i, [128, T], BF16) for i in range(2)]
    qp = [P.sbuf("B_qp%d" % i, [64, T], BF16) for i in range(2)]
    kn = [P.sbuf("B_kn%d" % i, [128, 2, T], BF16) for i in range(2)]
    va = [P.sbuf("B_va%d" % i, [128, 32, 129], BF16) for i in range(2)]
    for (v_, v_k) in va:
        k.memset("dve", v_[:, :, 128:129], 1.0, [], [v_k])
    rr = [P.sbuf("B_rr%d" % i, [128, 1], F32) for i in range(2)]
    ost = [P.sbuf("B_ost%d" % i, [128, 4, 128], BF16) for i in range(2)]
    O_k = Tk("O")
    state = {"oi": 0}

    def load_head(hd):
        v_, v_k = va[hd % 2]
        for rank in range(2):
            for i in range(4):
                P.dma("sp", v_[:, rank * 16 + 4 * i:rank * 16 + 4 * i + 4, 0:128],
                      io["Vall"](rank, i)[:, hd * 128:(hd + 1) * 128].rearrange("(t p) e -> p t e", p=128),
                      io["in_deps"], [v_k], v_k)
        q_, q_k = qn[hd % 2]
        qp_, qp_k = qp[hd % 2]
        k_, k_k = kn[hd % 2]
        P.dma("sp", q_[:, :], io["QNT"][hd, :, :], io["q_deps"], [q_k], q_k)
        P.dma("sp", qp_[:, :], io["QPT"][hd, :, :], io["q_deps"], [qp_k], qp_k)
        P.dma("sp", k_[:, :, :], io["KNall"](hd).rearrange("r d t -> d r t"), io["in_deps"], [k_k], k_k)
        return (q_, q_k, qp_, qp_k, k_, k_k), v_, v_k

    def score_mm(ps, ps_k, c0, rank, j, qb, rd):
        q_, q_k, qp_, qp_k, k_, k_k = rd
        k.mm(ps[:, c0:512], k_[:, rank, j * 128:(j + 1) * 128], q_[:, qb * 512 + c0:(qb + 1) * 512], True, False,
             [k_k, q_k], [ps_k], mark=False)
        k.mm(ps[:, c0:512], kp[:, rank, j * 128:(j + 1) * 128], qp_[:, qb * 512 + c0:(qb + 1) * 512], False, True,
             [kp_k, qp_k], [ps_k], mark=True)

    def finish_q(hd, qb, acc):
        os_, os_k = ost[state["oi"] % 2]
        state["oi"] += 1
        for j in range(4):
            a_, a_k = acc[j]
            r_, r_k = rr[j % 2]
            k.recip(r_[:, :], a_[:, 128:129], [a_k], [r_k])
            k.ts("dve", os_[:, j, :], a_[:, 0:128], r_[:, 0:1], None, ALU.mult, None, [a_k, r_k], [os_k])
        P.dma("sp", io["O"][qb * 512:(qb + 1) * 512, hd * 128:(hd + 1) * 128].rearrange("(j p) e -> p j e", p=128),
              os_[:, :, :], [os_k], [O_k], os_k)

    attn_core(k, io, 16, 128, load_head, score_mm, finish_q, mb, mb_k, bias, bias_k, scale)
    return {"O": O_k}


def phaseB2_lru(k, io):
    P, nc = k.P, k.nc
    cw, cw_k = P.sbuf("L_cw", [128, 16, 4], F32)
    P.dma("sp", cw[:, :, :], io["cw"], [], [cw_k], cw_k)
    sel, sel_k = P.sbuf("L_sel", [128, 2], F32)
    P.dma("sp", sel[:, :], io["sel"], [], [sel_k], sel_k)
    small = {}
    for n in ["cb", "ba", "bx", "lam"]:
        t_, t_k = P.sbuf("L_" + n, [128, 16], F32)
        P.dma("sp", t_[:, :], io[n], [], [t_k], t_k)
        small[n] = (t_, t_k)
    ce, ce_k = P.sbuf("L_ce", [128, 16], F32)
    k.act(ce[:, :], small["lam"][0][:, :], AF.Exp, [small["lam"][1]], [ce_k], scale=-1.0)
    k.act(ce[:, :], ce[:, :], AF.Ln, [ce_k], [ce_k], bias=1.0)
    k.ts("dve", ce[:, :], ce[:, :], -8.0, None, ALU.mult, None, [ce_k], [ce_k])
    xb = [P.sbuf("L_xb%d" % i, [128, S], F32) for i in range(2)]
    xc = [P.sbuf("L_xc%d" % i, [128, S], F32) for i in range(2)]
    xcb = [P.sbuf("L_xcb%d" % i, [128, S], BF16) for i in range(2)]
    rg = [P.sbuf("L_r%d" % i, [128, S], F32) for i in range(2)]
    ig = [P.sbuf("L_i%d" % i, [128, S], F32) for i in range(2)]
    tmp = P.sbuf("L_tmp", [128, S], F32)
    gt = [P.sbuf("L_g%d" % i, [128, S], BF16) for i in range(2)]
    wa = P.sbuf("L_wa", [128, 2, 256], BF16)
    wx = P.sbuf("L_wx", [128, 2, 256], BF16)
    Y_k = Tk("YT")
    il = lambda ap, rank: ap.rearrange("p (s r t) -> p s r t", r=2, t=128)[:, :, rank, :]
    for gb in range(8):
        if io.get("bg_hook") and gb > 0:
            pace = Tk("pace")
            pace.w = list(rg[0][1].w)
            io["bg_hook"]([pace])
        for (wt, nm) in ((wa, "wa"), (wx, "wx")):
            if isinstance(io[nm], PreCast):
                P.dma("act", wt[0][:, :, :], io[nm].ap[gb * 256:(gb + 1) * 256, :].rearrange("(c p) n -> p c n", p=128),
                      [io[nm].tk], [wt[1]], wt[1])
            else:
                P.dma("pool", wt[0][:, :, :], io[nm][gb].rearrange("(c p) n -> p c n", p=128), [], [wt[1]], wt[1])
        for ic in range(2):
            ch = gb * 2 + ic
            b_, b_k = xb[ic]
            c_, c_k = xc[ic]
            cb_, cb_k = xcb[ic]
            for rank in range(2):
                P.dma("sp", il(b_[:, :], rank), io["XBall"](rank, ch).rearrange("p (s t) -> p s t", t=128),
                      io["in_deps"], [b_k], b_k)
                P.dma("sp", il(gt[ic][0][:, :], rank), io["GTall"](rank, ch).rearrange("p (s t) -> p s t", t=128),
                      io["in_deps"], [gt[ic][1]], gt[ic][1])
            k.act(c_[:, :], b_[:, :], AF.Identity, [b_k, cw_k, small["cb"][1]], [c_k],
                  scale=cw[:, ch, 3:4], bias=small["cb"][0][:, ch:ch + 1])
            for sh in (1, 2, 3):
                k.stt(c_[:, sh:S], b_[:, 0:S - sh], cw[:, ch, 3 - sh:4 - sh], c_[:, sh:S], ALU.mult, ALU.add,
                      [b_k, cw_k, c_k], [c_k])
            k.copy("act", cb_[:, :], c_[:, :], [c_k], [cb_k])
        for oc in range(2):
            ch = gb * 2 + oc
            for (wt, gate, bname) in ((wa, rg, "ba"), (wx, ig, "bx")):
                g_, g_k = gate[oc]
                for tb in range(S // 512):
                    ps, ps_k = k.next_pf(0, 4)
                    for ic in range(2):
                        k.mm(ps[:, :], wt[0][:, ic, oc * 128:(oc + 1) * 128], xcb[ic][0][:, tb * 512:(tb + 1) * 512], ic == 0, ic == 1,
                             [wt[1], xcb[ic][1]], [ps_k], mark=(ic == 1))
                    k.act(g_[:, tb * 512:(tb + 1) * 512], ps[:, :], AF.Sigmoid, [ps_k, small[bname][1]], [g_k],
                          bias=small[bname][0][:, ch:ch + 1])
            r_, r_k = rg[oc]
            i_, i_k = ig[oc]
            c_, c_k = xc[oc]
            t_, t_k = tmp
            h_, h_k = xb[oc]
            k.act(r_[:, :], r_[:, :], AF.Exp, [r_k, ce_k], [r_k], scale=ce[:, ch:ch + 1])
            k.tt("dve", t_[:, :], r_[:, :], r_[:, :], ALU.mult, [r_k], [t_k])
            k.ts("dve", t_[:, :], t_[:, :], -1.0, 1.0, ALU.mult, ALU.add, [t_k], [t_k])
            k.act(t_[:, :], t_[:, :], AF.Sqrt, [t_k], [t_k])
            k.tt("dve", i_[:, :], i_[:, :], c_[:, :], ALU.mult, [i_k, c_k], [i_k])
            k.tt("dve", i_[:, :], i_[:, :], t_[:, :], ALU.mult, [i_k, t_k], [i_k])
            P.op("dve", lambda: nc.vector.tensor_tensor_scan(out=h_[:, :], data0=r_[:, :], data1=i_[:, :], initial=0.0,
                                                             op0=ALU.mult, op1=ALU.add), [r_k, i_k], [h_k])
            y_, y_k = gt[oc]
            k.tt("dve", t_[:, :], h_[:, :], y_[:, :], ALU.mult, [h_k, y_k], [t_k])
            own = y_[:, 0:T].rearrange("p (s t) -> p s t", t=128)
            k.ts("dve", own, il(t_[:, :], 0), sel[:, 0:1], None, ALU.mult, None, [t_k, sel_k], [y_k])
            k.stt(own, il(t_[:, :], 1), sel[:, 1:2], own, ALU.mult, ALU.add, [t_k, sel_k, y_k], [y_k])
            P.dma("sp", io["YT"][ch, :, :], y_[:, 0:T], [y_k], [Y_k], y_k)
    return {"YT": Y_k}


def tok_split(x_b, r):
    sh = x_b.shape
    return np.ascontiguousarray(x_b.reshape((NT, 2, 128) + sh[1:])[:, r].reshape((T,) + sh[1:]))


def feat_major(v, nch):
    return np.ascontiguousarray(np.asarray(v, np.float32).reshape(nch, 128).T)


LAYER_W = {
    0: [("a_w_in", "w_in"), ("a_q_norm", "qg"), ("a_k_norm", "kg"), ("a_lambda_q1", "lq1"), ("a_lambda_k1", "lk1"),
        ("a_lambda_q2", "lq2"), ("a_lambda_k2", "lk2"), ("a_sub_norm", "sg"), ("a_w_out", "w_out")],
    1: [("b_w_in", "w_in"), ("b_cq_norm", "cqg"), ("b_ckv_norm", "ckvg"), ("b_w_uq", "w_uq"), ("b_w_ukv", "w_ukv"),
        ("b_q_nope_norm", "qng"), ("b_q_rope_norm", "qrg"), ("b_k_nope_norm", "kng"), ("b_k_rope_norm", "krg"),
        ("b_w_out", "w_out")],
    2: [("c_w_in", "w_in"), ("c_w_a", "wa"), ("c_w_x", "wx"), ("c_w_out", "w_out")],
}
COMMON_W = [("norm_gain", "ng"), ("ple_norm", "pg"), ("ple_w_gate", "w_g"), ("ple_w_proj", "w_p")]
W_SHAPES = {"a_w_in": [D, 8192], "a_q_norm": [128], "a_k_norm": [128], "a_lambda_q1": [128], "a_lambda_k1": [128],
            "a_lambda_q2": [128], "a_lambda_k2": [128], "a_sub_norm": [256], "a_w_out": [D, D],
            "b_w_in": [D, 3136], "b_cq_norm": [512], "b_ckv_norm": [512], "b_w_uq": [512, 3072], "b_w_ukv": [512, 4096],
            "b_q_nope_norm": [128], "b_q_rope_norm": [64], "b_k_nope_norm": [128], "b_k_rope_norm": [64], "b_w_out": [D, D],
            "c_w_in": [D, 4096], "c_w_a": [8, 256, 256], "c_w_x": [8, 256, 256], "c_w_out": [D, D],
            "norm_gain": [D], "ple_norm": [D], "ple_w_gate": [D, D], "ple_w_proj": [256, D]}


def build_fused(nlayers=4, stop=None):
    L = Launch()
    P, k = L.P, L.k
    inp, scr = L.inp, L.scratch
    x = inp("x", [T, D], F32)
    p = inp("p", [nlayers, T, 256], F32)
    posT = inp("posT", [128, NT], I32)
    inv64 = inp("inv64", [64], F32)
    inv32 = inp("inv32", [32], F32)
    mb = inp("mb", [128, 3], F32)
    sel = inp("sel", [128, 2], F32)
    out = L.out("hout", [T, D], F32)
    hbuf = [scr("hA", [T, D], F32), scr("hB", [T, D], F32)]
    h1 = scr("h1", [T, D], F32)
    QT = scr("QT", [16, 128, T], BF16)
    QPT = scr("QPT", [16, 64, T], BF16)
    G = scr("G", [T, D], BF16)
    O = scr("O", [T, D], BF16)
    YT = scr("YT", [16, 128, T], BF16)
    h_in = x
    BIGW = ("w_in", "w_out", "w_g", "w_p", "w_uq", "w_ukv", "wa", "wx")
    WALL = {}
    casts = []
    for li in range(nlayers):
        kind = li % 3
        WALL[li] = {}
        for (src, nm) in LAYER_W[kind] + COMMON_W:
            ap = inp("L%d_%s" % (li, nm), W_SHAPES[src], F32)
            if nm in BIGW and not (li == 0 and nm == "w_in"):
                shp = W_SHAPES[src]
                if len(shp) == 3:
                    ap = ap.rearrange("g i j -> (g i) j")
                    shp = [shp[0] * shp[1], shp[2]]
                pc = PreCast(ap, scr("bf_L%d_%s" % (li, nm), shp, BF16), shp[0], shp[1])
                casts.append((li, nm, pc))
                ap = pc
            WALL[li][nm] = ap
    order = {"w_out": 2, "w_g": 2, "w_p": 2, "wa": 1, "wx": 1}
    casts.sort(key=lambda c: (c[0], order.get(c[1], 0)))
    cq = [(pc, i) for (_, _, pc) in casts for i in range(len(pc.pieces))]
    nmix = sum(1 for li in range(nlayers))
    state = {"pos": 0}

    def make_hook(n_hooks, upto):
        todo = max(0, upto - state["pos"])
        per = -(-todo // n_hooks) if n_hooks else 0

        def hook(pace):
            n = min(per, upto - state["pos"])
            for _ in range(max(0, n)):
                pc, i = cq[state["pos"]]
                state["pos"] += 1
                pc.emit_piece(P, i, pace)
        return hook

    def pieces_through(li_max):
        return sum(len(pc.pieces) for (l, _, pc) in casts if l <= li_max)

    for li in range(nlayers):
        kind, j = li % 3, li // 3
        Wl = WALL[li]
        h_out = out if li == nlayers - 1 else hbuf[li % 2]
        tgt = min(1 if li == 0 else nlayers - 1, nlayers - 1)
        upto = pieces_through(tgt)
        bg_hook = make_hook(16 if kind != 2 else 7, upto)
        P.begin_phase()
        k.setup_psum(6, 2)
        if kind == 0:
            send = scr("send%d" % li, [4096, 2048], BF16)
            recv = scr("recv%d" % li, [8192, 2048], BF16)
            r5 = recv.rearrange("(i r x) t -> i r x t", r=2, x=512)
            io = {"h": h_in, "ng": Wl["ng"], "w_in": Wl["w_in"], "qg": Wl["qg"], "kg": Wl["kg"],
                  "posT": posT, "inv64": inv64, "h_deps": [], "QT": QT, "pool_ok": li > 0,
                  "KT": send[0:2048, :].rearrange("(g d) t -> g d t", g=16), "V": send[2048:4096, :], "G": G}
            outs = phaseA_diff(k, io)
            P.gather_chunks(send, recv, 4096, 512, PAIRS, [outs["KT"], outs["V"]], [Tk("recv")])

        elif kind == 1:
            send = scr("send%d" % li, [4160, 2048], BF16)
            recv = scr("recv%d" % li, [8320, 2048], BF16)
            r5 = recv[0:8192, :].rearrange("(i r x) t -> i r x t", r=2, x=512)
            io = {"h": h_in, "ng": Wl["ng"], "w_in": Wl["w_in"], "cqg": Wl["cqg"], "ckvg": Wl["ckvg"],
                  "w_uq": Wl["w_uq"], "w_ukv": Wl["w_ukv"], "qng": Wl["qng"], "qrg": Wl["qrg"],
                  "kng": Wl["kng"], "krg": Wl["krg"], "posT": posT, "inv32": inv32, "h_deps": [], "pool_ok": li > 0,
                  "QNT": QT, "QPT": QPT, "KNT": send[0:2048, :].rearrange("(g d) t -> g d t", g=16),
                  "KPT": send[4096:4160, :], "V": send[2048:4096, :], "G": G}
            outs = phaseA_mla(k, io)
            P.gather_chunks(send, recv, 4160, 512, PAIRS, [outs["KNT"], outs["KPT"], outs["V"]], [Tk("recv")])
        else:
            sendx = scr("sendx%d" % li, [2048, T], F32)
            recvx = scr("recvx%d" % li, [4096, T], F32)
            send = scr("send%d" % li, [2048, T], BF16)
            recv = scr("recv%d" % li, [4096, T], BF16)
            io = {"h": h_in, "ng": Wl["ng"], "w_in": Wl["w_in"], "h_deps": [],
                  "XBT": sendx.rearrange("(g d) t -> g d t", g=16), "GT": send.rearrange("(g d) t -> g d t", g=16)}
            outs = phaseA_lru(k, io)
            P.gather_chunks(sendx, recvx, 2048, 256, PAIRS, [outs["XBT"]], [Tk("recvx")])
            P.gather_chunks(send, recv, 2048, 512, PAIRS, [outs["GT"]], [Tk("recv")])
            rx5 = recvx.rearrange("(i r x) t -> i r x t", r=2, x=256)
            r5 = recv.rearrange("(i r x) t -> i r x t", r=2, x=512)
        P.end_phase()
        if stop == "A":
            return L, []
        P.begin_phase()
        k.setup_psum(8, 0) if kind != 2 else k.setup_psum(6, 2)
        if kind == 0:
            io = {"QT": QT, "KTall": (lambda hc, r5=r5: r5[hc // 4, :, (hc % 4) * 128:(hc % 4 + 1) * 128, :]),
                  "Vall": (lambda rank, i, r5=r5: r5[4 + i, rank, :, :]), "mb": mb,
                  "lq1": Wl["lq1"], "lk1": Wl["lk1"], "lq2": Wl["lq2"], "lk2": Wl["lk2"],
                  "sg": Wl["sg"], "in_deps": [], "q_deps": [], "O": O, "bg_hook": bg_hook}
            phaseB2_diff(k, io, 0.8 - 0.6 * math.exp(-0.3 * li))
        elif kind == 1:
            io = {"QNT": QT, "QPT": QPT, "KNall": (lambda hc, r5=r5: r5[hc // 4, :, (hc % 4) * 128:(hc % 4 + 1) * 128, :]),
                  "KPall": recv[8192:8320, :].rearrange("(r d) t -> r d t", r=2),
                  "Vall": (lambda rank, i, r5=r5: r5[4 + i, rank, :, :]), "mb": mb, "in_deps": [], "q_deps": [], "O": O,
                  "bg_hook": bg_hook}
            phaseB2_mla(k, io)
        else:
            cw = inp("cw", [128, 16, 4], F32)
            small = {n: inp(n, [128, 16], F32) for n in ["cb", "ba", "bx", "lam"]}
            io = {"XBall": (lambda rank, ch, rx5=rx5: rx5[ch // 2, rank, (ch % 2) * 128:(ch % 2 + 1) * 128, :]),
                  "GTall": (lambda rank, ch, r5=r5: r5[ch // 4, rank, (ch % 4) * 128:(ch % 4 + 1) * 128, :]),
                  "cw": cw, "cb": small["cb"], "ba": small["ba"], "bx": small["bx"], "lam": small["lam"],
                  "wa": Wl["wa"], "wx": Wl["wx"], "sel": sel, "in_deps": [], "YT": YT, "bg_hook": bg_hook}
            phaseB2_lru(k, io)
        while state["pos"] < upto:
            pc, i = cq[state["pos"]]
            state["pos"] += 1
            pc.emit_piece(P, i, [])
        P.end_phase()
        if stop == "B":
            return L, []
        P.begin_phase()
        k.setup_psum(6, 2)
        io = {"h": h_in, "O": O, "G": G, "YT": YT, "w_out": Wl["w_out"], "pg": Wl["pg"], "w_g": Wl["w_g"],
              "w_p": Wl["w_p"], "p": p[li], "in_deps": [], "h_deps": [], "h1": h1, "hout": h_out}
        outs = phaseC(k, io, feat_major_y=(kind == 2))
        final = list(outs.values())
        P.end_phase()
        h_in = h_out
    return L, final


def fused_in_maps(I, nlayers=4):
    shared = {"inv64": INV64, "inv32": INV32}
    for li in range(nlayers):
        kind, j = li % 3, li // 3
        for (src, nm) in LAYER_W[kind]:
            shared["L%d_%s" % (li, nm)] = np.ascontiguousarray(I[src][j], dtype=np.float32)
        for (src, nm) in COMMON_W:
            shared["L%d_%s" % (li, nm)] = np.ascontiguousarray(I[src][li], dtype=np.float32)
        if kind == 2:
            shared["cw"] = np.ascontiguousarray(np.asarray(I["c_conv_w"][j], np.float32).reshape(4, 16, 128).transpose(2, 1, 0))
            shared["cb"] = feat_major(I["c_conv_b"][j], 16)
            shared["ba"] = feat_major(I["c_b_a"][j], 16)
            shared["bx"] = feat_major(I["c_b_x"][j], 16)
            shared["lam"] = feat_major(I["c_lambda"][j], 16)
    maps = []
    for c in range(NCORES):
        b, r = c // 2, c % 2
        m = dict(shared)
        m["x"] = tok_split(np.asarray(I["x"][b], np.float32), r)
        m["p"] = np.stack([tok_split(np.asarray(I["p"][l, b], np.float32), r) for l in range(nlayers)])
        m["posT"] = np.ascontiguousarray(tok_split(np.asarray(I["positions"][b]), r).reshape(NT, 128).T.astype(np.int32))
        mbv = np.full((128, 3), SHIFT, np.float32)
        if r == 0:
            mbv[64:, 0] = NEG
            mbv[:, 1] = NEG
            mbv[:, 2] = NEG
        else:
            mbv[64:, 1] = NEG
        m["mb"] = mbv
        m["sel"] = np.tile(np.array([[1.0 - r, float(r)]], np.float32), (128, 1))
        maps.append(m)
    return maps


def kernel(**inputs):
    I = {k_: np.asarray(v) for k_, v in inputs.items()}
    L, final = build_fused(4)
    res = L.run(fused_in_maps(I, 4), final)
    out = np.empty((4, S, D), np.float32)
    for c in range(NCORES):
        b, r = c // 2, c % 2
        out[b].reshape(NT, 2, 128, D)[:, r] = res[c]["hout"].reshape(NT, 128, D)
    return out
```

```python
import contextlib
import math
import numpy as np
import ml_dtypes
import concourse.bass as bass
import concourse.mybir as mybir
from concourse.bass_utils import run_bass_kernel_spmd

F32 = mybir.dt.float32
BF16 = mybir.dt.bfloat16
I32 = mybir.dt.int32
AF = mybir.ActivationFunctionType
ALU = mybir.AluOpType
AX = mybir.AxisListType

NCORES = 8
T = 2048
NT = 16
S = 4096
D = 2048
KC = 16
EPS = 1e-6
PI = math.pi
TWO_PI = 2.0 * math.pi


class Tk:
    __slots__ = ("w", "r", "name", "dsem", "dcnt")

    def __init__(self, name=""):
        self.w = []
        self.r = []
        self.name = name
        self.dsem = None
        self.dcnt = 0


class Prog:
    COMPUTE = ("pe", "act", "dve", "pool")

    def __init__(self, nc, stack):
        self.nc = nc
        self.stack = stack
        self.E = {"pe": nc.tensor, "act": nc.scalar, "dve": nc.vector,
                  "pool": nc.gpsimd, "sp": nc.sync}
        self.sems = {}
        self.cnt = {}
        for e in self.COMPUTE:
            self.sems[e] = stack.enter_context(nc.semaphore("c_" + e))
            self.cnt[e] = 0
        self.seen = {e: {} for e in self.E}
        self.ninst = 0
        self.uid = 0
        self.pstack = None
        self.sempool = []
        self.phase_tks = []
        self.phase_sem = stack.enter_context(nc.semaphore("phase"))
        self.phase_no = 0
        self.cc_sem = stack.enter_context(nc.semaphore("cc"))
        self.cc_cnt = 0

    def sbuf(self, name, shape, dtype):
        st = self.pstack if self.pstack is not None else self.stack
        self.uid += 1
        t = st.enter_context(self.nc.sbuf_tensor("%s_%d" % (name, self.uid), shape, dtype))
        return t, Tk(name)

    def begin_phase(self):
        self.pstack = contextlib.ExitStack()
        self.phase_tks = []

    def end_phase(self):
        toks = [(self.sems[e], self.cnt[e], e) for e in self.COMPUTE]
        toks.append((self.cc_sem, self.cc_cnt, "dma"))
        for tk in self.phase_tks:
            toks.append((tk.dsem, tk.dcnt, "dma"))
        self._waits("sp", toks)
        self.phase_no += 1
        self.nc.sync.sem_inc(self.phase_sem, 1)
        for e in self.COMPUTE:
            self.E[e].wait_ge(self.phase_sem, self.phase_no)
        self.ninst += 5
        for tk in self.phase_tks:
            self.sempool.append((tk.dsem, tk.dcnt))
            tk.dsem = None
        self.phase_tks = []
        if self.pstack is not None:
            self.pstack.close()
            self.pstack = None

    def gather_chunks(self, send, recv, nrows, rpc, groups, reads, writes):
        i, r0 = 0, 0
        while r0 < nrows:
            n = min(rpc, nrows - r0)
            self.collective("AllGather", send[r0:r0 + n, :], recv[2 * r0:2 * r0 + 2 * n, :], groups, reads, writes)
            r0 += n

    def collective(self, kind, src, dst, groups, reads, writes):
        self._waits("pool", self._deps("pool", reads, writes))
        ins = self.nc.gpsimd.collective_compute(kind, ALU.bypass, replica_groups=groups, ins=[src.opt()], outs=[dst.opt()])
        self.cc_cnt += 1
        ins.then_inc(self.cc_sem)
        self.ninst += 1
        self._record((self.cc_sem, self.cc_cnt, "dma"), reads, writes)

    def psum(self, name, shape, dtype):
        st = self.pstack if self.pstack is not None else self.stack
        self.uid += 1
        t = st.enter_context(self.nc.psum_tensor("%s_%d" % (name, self.uid), shape, dtype))
        return t, Tk(name)

    def newsem(self, name):
        self.nsems = getattr(self, "nsems", 0) + 1
        self.uid += 1
        return self.stack.enter_context(self.nc.semaphore("%s_%d" % (name, self.uid)))

    def _waits(self, e, toks):
        need = {}
        for (s, v, src) in toks:
            k = id(s)
            if k not in need or need[k][1] < v:
                need[k] = (s, v)
        seen = self.seen[e]
        for k, (s, v) in need.items():
            if seen.get(k, 0) >= v:
                continue
            self.E[e].wait_ge(s, v)
            self.ninst += 1
            seen[k] = v

    def _deps(self, e, reads, writes):
        toks = []
        for t in reads:
            toks += t.w
        for t in writes:
            toks += t.w
            for tok in t.r:
                toks.append(tok)
        if e == "pe":
            toks = [t for t in toks if t[2] != "pe"]
        return toks

    def _record(self, tok, reads, writes):
        for t in reads:
            t.r.append(tok)
            if len(t.r) > 48:
                best = {}
                for (s, v, src) in t.r:
                    k = id(s)
                    if k not in best or best[k][1] < v:
                        best[k] = (s, v, src)
                t.r = list(best.values())
        for t in writes:
            t.w = [tok]
            t.r = []

    def op(self, e, fn, reads=(), writes=(), mark=True):
        self._waits(e, self._deps(e, reads, writes))
        ins = fn()
        self.ninst += 1
        if mark:
            self.cnt[e] += 1
            ins.then_inc(self.sems[e], 1)
            tok = (self.sems[e], self.cnt[e], e)
        else:
            tok = (self.sems[e], self.cnt[e] + 1, e)
        self._record(tok, reads, writes)
        return ins

    def dma(self, q, out, in_, reads, writes, semtk, nobarrier=False, **kw):
        if semtk.dsem is None:
            if nobarrier:
                semtk.dsem, semtk.dcnt = self.newsem("bg"), 0
            else:
                if self.sempool:
                    semtk.dsem, semtk.dcnt = self.sempool.pop()
                else:
                    semtk.dsem, semtk.dcnt = self.newsem("d"), 0
                self.phase_tks.append(semtk)
        toks = self._deps(q, reads, writes)
        if semtk.dcnt > 0:
            toks.append((semtk.dsem, semtk.dcnt, "dma"))
        self._waits(q, toks)
        ins = self.E[q].dma_start(out=out, in_=in_, **kw)
        self.ninst += 1
        semtk.dcnt += 16
        ins.then_inc(semtk.dsem, 16)
        tok = (semtk.dsem, semtk.dcnt, "dma")
        self._record(tok, reads, writes)
        return ins

    def finish(self, tks):
        toks = []
        for t in tks:
            toks += t.w
            toks += t.r
        self._waits("sp", toks)


class K:
    def __init__(self, nc, P, ident_dram):
        self.nc = nc
        self.P = P
        self.ident, self.ident_k = P.sbuf("ident_sb", [128, 128], BF16)
        P.dma("pool", self.ident[:, :], ident_dram, [], [self.ident_k], self.ident_k)
        self.pf, self.pb = [], []
        self.pfi = 0
        self.pbi = 0
        self.rr = 0
        self.dq = []

    def setup_psum(self, nf32=6, nbf16=2):
        P = self.P
        self.pf = [P.psum("pf%d" % i, [128, 512], F32) for i in range(nf32)]
        self.pb = [P.psum("pb%d" % i, [128, 1024], BF16) for i in range(nbf16)]

    def defer(self, fn):
        self.dq.append(fn)

    def run_deferred(self, lag):
        while len(self.dq) > lag:
            self.dq.pop(0)()

    def flush(self):
        self.run_deferred(0)

    def next_pf(self, lo=0, hi=6):
        n = hi - lo
        i = lo + (self.pfi % n)
        self.pfi += 1
        return self.pf[i]

    def next_pb(self):
        i = self.pbi % 2
        self.pbi += 1
        return self.pb[i]

    def act(self, out, in_, func, reads, writes, **kw):
        nc = self.nc
        return self.P.op("act", lambda: nc.scalar.activation(out=out, in_=in_, func=func, **kw), reads, writes)

    def tt(self, eng, out, in0, in1, op, reads, writes):
        nc = self.nc
        E = nc.vector if eng == "dve" else nc.gpsimd
        return self.P.op(eng, lambda: E.tensor_tensor(out=out, in0=in0, in1=in1, op=op), reads, writes)

    def ts(self, eng, out, in0, s1, s2, op0, op1, reads, writes):
        nc = self.nc
        E = nc.vector if eng == "dve" else nc.gpsimd
        if op1 is None:
            return self.P.op(eng, lambda: E.tensor_scalar(out=out, in0=in0, scalar1=s1, scalar2=None, op0=op0), reads, writes)
        return self.P.op(eng, lambda: E.tensor_scalar(out=out, in0=in0, scalar1=s1, scalar2=s2, op0=op0, op1=op1), reads, writes)

    def stt(self, out, in0, scalar, in1, op0, op1, reads, writes):
        nc = self.nc
        return self.P.op("dve", lambda: nc.vector.scalar_tensor_tensor(out=out, in0=in0, scalar=scalar, in1=in1, op0=op0, op1=op1), reads, writes)

    def red(self, out, in_, reads, writes):
        nc = self.nc
        return self.P.op("dve", lambda: nc.vector.tensor_reduce(out=out, in_=in_, axis=AX.X, op=ALU.add), reads, writes)

    def recip(self, out, in_, reads, writes):
        nc = self.nc
        return self.P.op("dve", lambda: nc.vector.reciprocal(out=out, in_=in_), reads, writes)

    def copy(self, eng, out, in_, reads, writes):
        nc = self.nc
        if eng == "act":
            return self.P.op("act", lambda: nc.scalar.activation(out=out, in_=in_, func=AF.Copy), reads, writes)
        E = nc.vector if eng == "dve" else nc.gpsimd
        return self.P.op(eng, lambda: E.tensor_copy(out=out, in_=in_), reads, writes)

    def memset(self, eng, ap, val, reads, writes):
        nc = self.nc
        E = nc.vector if eng == "dve" else nc.gpsimd
        return self.P.op(eng, lambda: E.memset(ap, val), reads, writes)

    def mm(self, out, lhsT, rhs, start, stop, reads, writes, mark):
        nc = self.nc
        return self.P.op("pe", lambda: nc.tensor.matmul(out, lhsT=lhsT, rhs=rhs, start=start, stop=stop), reads, writes, mark=mark)

    def tr(self, out, in_, reads, writes, mark):
        nc = self.nc
        ident = self.ident
        return self.P.op("pe", lambda: nc.tensor.transpose(out=out, in_=in_, identity=ident[:, :]),
                         list(reads) + [self.ident_k], writes, mark=mark)

    def rstd(self, out, ss, scale, out_k, ss_k):
        self.ts("dve", ss, ss, scale, EPS, ALU.mult, ALU.add, [ss_k], [ss_k])
        self.act(ss, ss, AF.Sqrt, [ss_k], [ss_k])
        self.recip(out, ss, [ss_k], [out_k])


def bcast_load(k, name, vec_dram, n, q="sp"):
    t, tk = k.P.sbuf(name, [128, n], F32)
    k.P.dma(q, t[:, :], vec_dram.partition_broadcast(128), [], [tk], tk)
    return t, tk


class NormT:
    def __init__(self, k, name):
        P = k.P
        self.k = k
        self.hb = [P.sbuf("%s_hb%d" % (name, i), [128, D], F32) for i in range(2)]
        self.sq = P.sbuf(name + "_sq", [128, D], BF16)
        self.ss = P.sbuf(name + "_ss", [128, 1], F32)
        self.rs = P.sbuf(name + "_rs", [128, 1], F32)
        self.xn = [P.sbuf("%s_xn%d" % (name, i), [128, D], BF16) for i in range(2)]

    def run(self, h_dram, h_deps, gain_bc, gain_k, xT, xT_ks, ntiles=NT, tile0=0):
        k, P = self.k, self.k.P
        for t in range(ntiles):
            hb, hb_k = self.hb[t % 2]
            xn, xn_k = self.xn[t % 2]
            sq, sq_k = self.sq
            ss, ss_k = self.ss
            rs, rs_k = self.rs
            P.dma("sp", hb[:, :], h_dram[t * 128:(t + 1) * 128, :], h_deps(t), [hb_k], hb_k)
            k.act(sq[:, :], hb[:, :], AF.Square, [hb_k], [sq_k])
            k.red(ss[:, 0:1], sq[:, :], [sq_k], [ss_k])
            k.rstd(rs[:, 0:1], ss[:, 0:1], 1.0 / D, rs_k, ss_k)
            k.stt(xn[:, :], hb[:, :], rs[:, 0:1], gain_bc[:, :], ALU.mult, ALU.mult, [hb_k, rs_k, gain_k], [xn_k])
            tt = tile0 + t
            for half in range(2):
                pb, pb_k = k.next_pb()
                for j in range(8):
                    c = half * 8 + j
                    k.tr(pb[:, j * 128:(j + 1) * 128], xn[:, c * 128:(c + 1) * 128], [xn_k], [pb_k], mark=(j == 7))
                k.copy("act" if half == 0 else "dve",
                       xT[:, half * 8:(half + 1) * 8, tt * 128:(tt + 1) * 128],
                       pb[:, :].rearrange("p (j q) -> p j q", j=8), [pb_k], [xT_ks[tt]])


class PreCast:
    def __init__(self, src, ap, rows, cols):
        self.src, self.ap, self.tk = src, ap, Tk("precast")
        self.sem_tk = Tk("precast_sem")
        rpp = max(128, (8 << 20) // (cols * 4) // 128 * 128)
        self.pieces = [(r0, min(rows, r0 + rpp)) for r0 in range(0, rows, rpp)]

    def emit_piece(self, P, i, pace):
        r0, r1 = self.pieces[i]
        P.dma("pool", self.ap[r0:r1, :], self.src[r0:r1, :], pace, [self.tk], self.sem_tk, nobarrier=True)


class WStream:
    def __init__(self, k, name, kcmax, nbuf=2):
        self.k = k
        self.buf = [k.P.sbuf("%s_w%d" % (name, i), [128, kcmax, 512], BF16) for i in range(nbuf)]
        self.i = 0

    def load(self, W_dram, kc, n0, nsz):
        w, w_k = self.buf[self.i % len(self.buf)]
        self.i += 1
        if isinstance(W_dram, PreCast):
            self.k.P.dma("act", w[:, 0:kc, 0:nsz],
                         W_dram.ap[:, n0:n0 + nsz].rearrange("(c p) n -> p c n", p=128), [W_dram.tk], [w_k], w_k)
        else:
            self.k.P.dma("pool", w[:, 0:kc, 0:nsz],
                         W_dram[:, n0:n0 + nsz].rearrange("(c p) n -> p c n", p=128), [], [w_k], w_k)
        return w, w_k

    def stream(self, jobs):
        nxt = self.load(*jobs[0])
        for i in range(len(jobs)):
            cur = nxt
            if i + 1 < len(jobs):
                nxt = self.load(*jobs[i + 1])
            yield i, cur[0], cur[1]


def linear_tok(k, xT, xT_ks, kc, w, w_k, nsz, tiles, evac, pf_lo=0, pf_hi=3):
    for t in tiles:
        ps, ps_k = k.next_pf(pf_lo, pf_hi)
        for c in range(kc):
            k.mm(ps[:, 0:nsz], xT[:, c, t * 128:(t + 1) * 128], w[:, c, 0:nsz], c == 0, c == kc - 1,
                 [xT_ks[t], w_k], [ps_k], mark=(c == kc - 1))
        evac(t, ps, ps_k)
        k.run_deferred(3)


class Rope:
    def __init__(self, k, name, posT_dram, inv_dram, ntiles, half):
        P, nc = k.P, k.nc
        n = ntiles * half
        self.cos, self.cos_k = P.sbuf(name + "_cos", [128, ntiles, half], F32)
        self.sin, self.sin_k = P.sbuf(name + "_sin", [128, ntiles, half], F32)
        pi_, pi_k = P.sbuf(name + "_pi", [128, ntiles], I32)
        pf, pf_k = P.sbuf(name + "_pf", [128, ntiles], F32)
        inv, inv_k = bcast_load(k, name + "_inv", inv_dram, half)
        ang, ang_k = P.sbuf(name + "_ang", [128, ntiles, half], F32)
        ki, ki_k = P.sbuf(name + "_ki", [128, n], I32)
        kf, kf_k = P.sbuf(name + "_kf", [128, n], F32)
        m, m_k = P.sbuf(name + "_m", [128, n], F32)
        P.dma("sp", pi_[:, :], posT_dram, [], [pi_k], pi_k)
        k.copy("dve", pf[:, :], pi_[:, :], [pi_k], [pf_k])
        for t in range(ntiles):
            k.ts("dve", ang[:, t, :], inv[:, :], pf[:, t:t + 1], None, ALU.mult, None, [inv_k, pf_k], [ang_k])
        angf = ang[:, :, :].rearrange("p t h -> p (t h)")
        for which, (dst, dst_k) in enumerate([(self.sin, self.sin_k), (self.cos, self.cos_k)]):
            if which == 0:
                src, src_k = angf, ang_k
            else:
                k.ts("dve", angf, angf, PI / 2.0, None, ALU.add, None, [ang_k], [ang_k])
                src, src_k = angf, ang_k
            k.ts("dve", kf[:, :], src, 1.0 / TWO_PI, None, ALU.mult, None, [src_k], [kf_k])
            k.copy("dve", ki[:, :], kf[:, :], [kf_k], [ki_k])
            k.copy("dve", kf[:, :], ki[:, :], [ki_k], [kf_k])
            k.stt(m[:, :], kf[:, :], -TWO_PI, src, ALU.mult, ALU.add, [kf_k, src_k], [m_k])
            k.ts("dve", kf[:, :], m[:, :], PI, TWO_PI, ALU.is_gt, ALU.mult, [m_k], [kf_k])
            k.tt("dve", m[:, :], m[:, :], kf[:, :], ALU.subtract, [m_k, kf_k], [m_k])
            k.ts("dve", kf[:, :], m[:, :], -PI, TWO_PI, ALU.is_lt, ALU.mult, [m_k], [kf_k])
            k.tt("dve", m[:, :], m[:, :], kf[:, :], ALU.add, [m_k, kf_k], [m_k])
            k.ts("dve", m[:, :], m[:, :], 3.1415925, -3.1415925, ALU.min, ALU.max, [m_k], [m_k])
            k.act(dst[:, :, :].rearrange("p t h -> p (t h)"), m[:, :], AF.Sin, [m_k], [dst_k])


class QKProc:
    def __init__(self, k, name, ngrp, dn, with_rope=True, nob=6):
        P = k.P
        self.k = k
        self.ngrp, self.dn = ngrp, dn
        w = ngrp * dn
        self.sq = [P.sbuf("%s_sq%d" % (name, i), [128, w], F32) for i in range(2)]
        self.ss = [P.sbuf("%s_ss%d" % (name, i), [128, ngrp], F32) for i in range(2)]
        self.rs = [P.sbuf("%s_rs%d" % (name, i), [128, ngrp], F32) for i in range(2)]
        self.tq = [P.sbuf("%s_tq%d" % (name, i), [128, w], F32) for i in range(2)]
        self.ra = [P.sbuf("%s_ra%d" % (name, i), [128, w], F32) if with_rope else (None, None) for i in range(2)]
        self.nob = nob
        self.ob = [P.sbuf("%s_ob%d" % (name, i), [128, w], BF16) for i in range(nob)]
        self.i = 0

    def run(self, src3, src_k, gain, gain_k, rope, t, do_rope=True, ngrp=None):
        k = self.k
        g, dn = (ngrp or self.ngrp), self.dn
        w = g * dn
        i = self.i % 2
        self.i += 1
        sq, sq_k = self.sq[i]
        ss, ss_k = self.ss[i]
        rs, rs_k = self.rs[i]
        tq, tq_k = self.tq[i]
        ra, ra_k = self.ra[i]
        ob, ob_k = self.ob[(self.i - 1) % self.nob]
        v3 = lambda tl: tl[:, 0:w].rearrange("p (g d) -> p g d", g=g)
        k.act(v3(sq), src3, AF.Square, [src_k], [sq_k])
        k.red(ss[:, 0:g], v3(sq), [sq_k], [ss_k])
        k.rstd(rs[:, 0:g], ss[:, 0:g], 1.0 / dn, rs_k, ss_k)
        k.tt("dve", v3(tq), src3, rs[:, 0:g].unsqueeze(2).to_broadcast([128, g, dn]), ALU.mult,
             [src_k, rs_k], [tq_k])
        gb = gain[:, :].unsqueeze(1).to_broadcast([128, g, dn])
        if not do_rope:
            k.tt("dve", v3(ob), v3(tq), gb, ALU.mult, [tq_k, gain_k], [ob_k])
            return ob, ob_k
        k.tt("dve", v3(tq), v3(tq), gb, ALU.mult, [tq_k, gain_k], [tq_k])
        h = dn // 2
        cosb = rope.cos[:, t, :].unsqueeze(1).to_broadcast([128, g, h])
        sinb = rope.sin[:, t, :].unsqueeze(1).to_broadcast([128, g, h])
        t3, r3, o3 = v3(tq), v3(ra), v3(ob)
        x1, x2 = t3[:, :, 0:h], t3[:, :, h:dn]
        k.tt("dve", r3[:, :, 0:h], x1, cosb, ALU.mult, [tq_k, rope.cos_k], [ra_k])
        k.tt("dve", r3[:, :, h:dn], x2, sinb, ALU.mult, [tq_k, rope.sin_k], [ra_k])
        k.tt("dve", o3[:, :, 0:h], r3[:, :, 0:h], r3[:, :, h:dn], ALU.subtract, [ra_k], [ob_k])
        k.tt("dve", r3[:, :, 0:h], x2, cosb, ALU.mult, [tq_k, rope.cos_k], [ra_k])
        k.tt("dve", r3[:, :, h:dn], x1, sinb, ALU.mult, [tq_k, rope.sin_k], [ra_k])
        k.tt("dve", o3[:, :, h:dn], r3[:, :, 0:h], r3[:, :, h:dn], ALU.add, [ra_k], [ob_k])
        return ob, ob_k


def phaseA_diff(k, io):
    P = k.P
    gain, gain_k = bcast_load(k, "A_ng", io["ng"], D)
    qg, qg_k = bcast_load(k, "A_qg", io["qg"], 128)
    kg, kg_k = bcast_load(k, "A_kg", io["kg"], 128)
    rope = Rope(k, "A_rp", io["posT"], io["inv64"], NT, 64)
    xT, _ = P.sbuf("A_xT", [128, KC, T], BF16)
    xT_ks = [Tk("xT%d" % t) for t in range(NT)]
    nt = NormT(k, "A_n")
    nt.run(io["h"], lambda t: io["h_deps"], gain, gain_k, xT, xT_ks)
    ws = WStream(k, "A", KC)
    qk = QKProc(k, "A_qk", 4, 128)
    stb = [(nt.hb[i][0][:, :].bitcast(BF16), nt.hb[i][1]) for i in range(2)]
    outs = {"QT": Tk("QT"), "KT": Tk("KT"), "V": Tk("V"), "G": Tk("G")}
    si = 0
    for nb, w, w_k in ws.stream([(io["w_in"], KC, nb * 512, 512) for nb in range(16)]):
        kind = nb // 4
        if io.get("early") and nb in io["early"]:
            k.flush()
            io["early"][nb](outs)
        for hf in range(2):
            stf, st_k = stb[si % 2]
            si += 1
            if kind < 2:
                st = stf.rearrange("p (g t) -> p g t", g=4)
                gn, gn_k = (qg, qg_k) if kind == 0 else (kg, kg_k)

                def evac(t, ps, ps_k, st=st, st_k=st_k, gn=gn, gn_k=gn_k, hf=hf):
                    ob, ob_k = qk.run(ps[:, :].rearrange("p (g d) -> p g d", g=4), ps_k, gn, gn_k, rope, t)
                    tl = t - hf * 8
                    transpose_to(k, ob, ob_k, 4, 128, st[:, :, tl * 128:(tl + 1) * 128], st_k)
                linear_tok(k, xT, xT_ks, KC, w, w_k, 512, range(hf * 8, hf * 8 + 8), evac)
                nm = "QT" if kind == 0 else "KT"
                g0 = (nb % 4) * 4

                def out_dma(nm=nm, g0=g0, hf=hf, st=st, st_k=st_k):
                    P.dma("sp", io[nm][g0:g0 + 4, :, hf * 1024:(hf + 1) * 1024].rearrange("g d t -> d g t"), st, [st_k],
                          [outs[nm]], st_k)
                k.defer(out_dma)
            else:
                st = stf.rearrange("p (t n) -> p t n", t=8)
                func = AF.Copy if kind == 2 else AF.Silu

                def evac(t, ps, ps_k, st=st, st_k=st_k, func=func, hf=hf):
                    k.act(st[:, t - hf * 8, :], ps[:, :], func, [ps_k], [st_k])
                linear_tok(k, xT, xT_ks, KC, w, w_k, 512, range(hf * 8, hf * 8 + 8), evac)
                nm = "V" if kind == 2 else "G"
                c0 = (nb % 4) * 512
                k.flush()
                P.dma("sp", io[nm][hf * 1024:(hf + 1) * 1024, c0:c0 + 512].rearrange("(t p) n -> p t n", p=128), st, [st_k],
                      [outs[nm]], st_k)
    k.flush()
    return outs


def phaseB_diff(k, io, lam_init):
    P, nc = k.P, k.nc
    NTS = S // 128
    scale = 128 ** -0.5
    SHIFT = -8.0
    lv = [bcast_load(k, "B_lv%d" % i, io[n], 128) for i, n in enumerate(["lq1", "lk1", "lq2", "lk2"])]
    sg, sg_k = bcast_load(k, "B_sg", io["sg"], 256)
    lt, lt_k = P.sbuf("B_lt", [128, 128], F32)
    l2, l2_k = P.sbuf("B_l2", [128, 2], F32)
    lam, lam_k = P.sbuf("B_lam", [128, 1], F32)
    nlam, nlam_k = P.sbuf("B_nlam", [128, 1], F32)
    for j in range(2):
        k.tt("dve", lt[:, :], lv[2 * j][0][:, :], lv[2 * j + 1][0][:, :], ALU.mult, [lv[2 * j][1], lv[2 * j + 1][1]], [lt_k])
        k.red(l2[:, j:j + 1], lt[:, :], [lt_k], [l2_k])
    k.act(l2[:, :], l2[:, :], AF.Exp, [l2_k], [l2_k])
    k.tt("dve", lam[:, :], l2[:, 0:1], l2[:, 1:2], ALU.subtract, [l2_k], [lam_k])
    k.ts("dve", nlam[:, :], lam[:, :], lam_init, -1.0, ALU.add, ALU.mult, [lam_k], [nlam_k])
    k.ts("dve", sg[:, :], sg[:, :], 1.0 - lam_init, None, ALU.mult, None, [sg_k], [sg_k])
    bias, bias_k = P.sbuf("B_bias", [128, 1], F32)
    k.memset("dve", bias[:, :], SHIFT, [], [bias_k])

    qT = [P.sbuf("B_qT%d" % i, [128, S], BF16) for i in range(2)]
    kT = [P.sbuf("B_kT%d" % i, [128, S], BF16) for i in range(2)]
    va = [P.sbuf("B_va%d" % i, [128, NTS, 257], BF16) for i in range(2)]
    for (v_, v_k) in va:
        k.memset("dve", v_[:, :, 256:257], 1.0, [], [v_k])
    E = [P.sbuf("B_E%d" % i, [128, 512], BF16) for i in range(3)]
    o1 = [P.sbuf("B_o1_%d" % i, [128, 256], F32) for i in range(4)]
    oo = [P.sbuf("B_oo%d" % i, [128, 256], F32) for i in range(2)]
    sq = P.sbuf("B_sq", [128, 256], F32)
    rr = [P.sbuf("B_rr%d" % i, [128, 1], F32) for i in range(2)]
    ss = P.sbuf("B_ss", [128, 1], F32)
    rs = P.sbuf("B_rs", [128, 1], F32)
    ost = [P.sbuf("B_ost%d" % i, [128, 4, 256], BF16) for i in range(2)]
    O_k = Tk("O")
    ei = 0
    for hd in range(4):
        v_, v_k = va[hd % 2]
        P.dma("sp", v_[:, :, 0:256], io["V"][:, hd * 256:(hd + 1) * 256].rearrange("(t p) e -> p t e", p=128),
              io["in_deps"], [v_k], v_k)
        for qb in range(S // 512):
            os_, os_k = ost[qb % 2]
            for c in range(2):
                if qb == 0:
                    q_, q_k = qT[c]
                    k_, k_k = kT[c]
                    P.dma("sp", q_[:, :], io["QT"][hd * 2 + c, :, :], io["in_deps"], [q_k], q_k)
                    P.dma("sp", k_[:, :], io["KT"][hd * 2 + c, :, :], io["in_deps"], [k_k], k_k)
                q_, q_k = qT[c]
                k_, k_k = kT[c]
                acc = [k.pf[j] for j in range(4)]
                nkt = 4 * (qb + 1)
                for kt in range(nkt):
                    j0 = max(0, kt - 4 * qb)
                    c0 = j0 * 128
                    ps, ps_k = k.next_pf(4, 6)
                    k.mm(ps[:, c0:512], k_[:, kt * 128:(kt + 1) * 128], q_[:, qb * 512 + c0:(qb + 1) * 512], True, True,
                         [k_k, q_k], [ps_k], mark=True)
                    e_, e_k = E[ei % 3]
                    ei += 1
                    k.act(e_[:, c0:512], ps[:, c0:512], AF.Exp, [ps_k, bias_k], [e_k], scale=scale, bias=bias[:, 0:1])
                    if kt >= 4 * qb:
                        k.memset("dve", e_[64:128, c0:c0 + 64], 0.0, [], [e_k])
                    for j in range(j0, 4):
                        a_, a_k = acc[j]
                        k.mm(a_[:, 0:257], e_[:, j * 128:(j + 1) * 128], v_[:, kt, 0:257], kt == 0, kt == 4 * qb + j,
                             [e_k, v_k], [a_k], mark=(kt == 4 * qb + j))
                for j in range(4):
                    a_, a_k = acc[j]
                    r_, r_k = rr[j % 2]
                    k.recip(r_[:, :], a_[:, 256:257], [a_k], [r_k])
                    o_, o_k = o1[j]
                    if c == 0:
                        k.ts("dve", o_[:, :], a_[:, 0:256], r_[:, 0:1], None, ALU.mult, None, [a_k, r_k], [o_k])
                    else:
                        k.tt("dve", r_[:, :], r_[:, :], nlam[:, :], ALU.mult, [r_k, nlam_k], [r_k])
                        f_, f_k = oo[j % 2]
                        k.stt(f_[:, :], a_[:, 0:256], r_[:, 0:1], o_[:, :], ALU.mult, ALU.add, [a_k, r_k, o_k], [f_k])
                        k.act(sq[0][:, :], f_[:, :], AF.Square, [f_k], [sq[1]])
                        k.red(ss[0][:, :], sq[0][:, :], [sq[1]], [ss[1]])
                        k.rstd(rs[0][:, :], ss[0][:, :], 1.0 / 256, rs[1], ss[1])
                        k.stt(os_[:, j, :], f_[:, :], rs[0][:, 0:1], sg[:, :], ALU.mult, ALU.mult, [f_k, rs[1], sg_k], [os_k])
            P.dma("sp", io["O"][qb * 512:(qb + 1) * 512, hd * 256:(hd + 1) * 256].rearrange("(j p) e -> p j e", p=128),
                  os_[:, :, :], [os_k], [O_k], os_k)
    return {"O": O_k}


def phaseC(k, io, feat_major_y=False):
    P = k.P
    pg, pg_k = bcast_load(k, "C_pg", io["pg"], D)
    yT, _ = P.sbuf("C_yT", [128, KC, T], BF16)
    yT_ks = [Tk("yT%d" % t) for t in range(NT)]
    ws = WStream(k, "C", KC)
    nt = NormT(k, "C_n")
    if not feat_major_y:
        for t in range(NT):
            hbf = nt.hb[t % 2][0][:, :].bitcast(BF16)
            b_k = nt.hb[t % 2][1]
            o_, g_ = hbf[:, 0:D], hbf[:, D:2 * D]
            P.dma("sp", o_, io["O"][t * 128:(t + 1) * 128, :], io["in_deps"], [b_k], b_k)
            P.dma("sp", g_, io["G"][t * 128:(t + 1) * 128, :], io["in_deps"], [b_k], b_k)
            k.tt("dve", o_, o_, g_, ALU.mult, [b_k], [b_k])
            for half in range(2):
                pb, pb_k = k.next_pb()
                for j in range(8):
                    c = half * 8 + j
                    k.tr(pb[:, j * 128:(j + 1) * 128], o_[:, c * 128:(c + 1) * 128], [b_k], [pb_k], mark=(j == 7))
                k.copy("act" if half == 0 else "dve", yT[:, half * 8:(half + 1) * 8, t * 128:(t + 1) * 128],
                       pb[:, :].rearrange("p (j q) -> p j q", j=8), [pb_k], [yT_ks[t]])
    else:
        for c in range(KC):
            P.dma("sp", yT[:, c, :], io["YT"][c, :, :], io["in_deps"], yT_ks, yT_ks[c])
    slab = [P.sbuf("C_sl%d" % i, [128, 4, 512], F32) for i in range(3)]
    h1_ks = {}

    def slab_load(g, src, deps):
        nb, qt = (g // 4) % 4, g % 4
        sl, sl_k = slab[g % 3]
        P.dma("sp", sl[:, :, :],
              src[qt * 512:qt * 512 + 512, nb * 512:(nb + 1) * 512].rearrange("(t p) n -> p t n", p=128),
              deps(nb, qt), [sl_k], sl_k)

    slab_load(0, io["h"], lambda nb, qt: io["h_deps"])
    for nb, w, w_k in ws.stream([(io["w_out"], KC, nb * 512, 512) for nb in range(4)]):
        for qt in range(4):
            g = nb * 4 + qt
            if g + 1 < 16:
                slab_load(g + 1, io["h"], lambda nb, qt: io["h_deps"])
            sl, sl_k = slab[g % 3]
            r0 = qt * 512

            def evac(t, ps, ps_k, sl=sl, sl_k=sl_k, qt=qt):
                k.tt("dve", sl[:, t - qt * 4, :], sl[:, t - qt * 4, :], ps[:, :], ALU.add, [sl_k, ps_k], [sl_k])
            linear_tok(k, yT, yT_ks, KC, w, w_k, 512, range(qt * 4, qt * 4 + 4), evac)
            hk = Tk("h1_%d_%d" % (nb, qt))
            h1_ks[(nb, qt)] = hk
            P.dma("sp", io["h1"][r0:r0 + 512, nb * 512:(nb + 1) * 512].rearrange("(t p) n -> p t n", p=128),
                  sl[:, :, :], [sl_k], [hk], sl_k)
    nt.run(io["h1"], lambda t: [h1_ks[(nb, t // 4)] for nb in range(4)], pg, pg_k, yT, yT_ks)
    xT, xT_ks = yT, yT_ks
    pT, _ = P.sbuf("C_pT", [128, 2, T], BF16)
    pT_ks = [Tk("pT%d" % t) for t in range(NT)]
    pl = [P.sbuf("C_pl%d" % i, [128, 256], F32) for i in range(2)]
    pc = [P.sbuf("C_pc%d" % i, [128, 256], BF16) for i in range(2)]
    for t in range(NT):
        a_, a_k = pl[t % 2]
        b_, b_k = pc[t % 2]
        P.dma("sp", a_[:, :], io["p"][t * 128:(t + 1) * 128, :], [], [a_k], a_k)
        k.copy("dve", b_[:, :], a_[:, :], [a_k], [b_k])
        pb, pb_k = k.next_pb()
        for j in range(2):
            k.tr(pb[:, j * 128:(j + 1) * 128], b_[:, j * 128:(j + 1) * 128], [b_k], [pb_k], mark=(j == 1))
        k.copy("act", pT[:, :, t * 128:(t + 1) * 128], pb[:, 0:256].rearrange("p (j q) -> p j q", j=2), [pb_k], [pT_ks[t]])
    wp = WStream(k, "Cp", 2)
    gs = [P.sbuf("C_gs%d" % i, [128, 512], F32) for i in range(2)]
    out_k = Tk("hout")
    gi = 0
    slab_load(16, io["h1"], lambda nb, qt: [h1_ks[(nb % 4, qt)]])
    for nb, w, w_k in ws.stream([(io["w_g"], KC, nb * 512, 512) for nb in range(4)]):
        w2, w2_k = wp.load(io["w_p"], 2, nb * 512, 512)
        for qt in range(4):
            g = 16 + nb * 4 + qt
            if g + 1 < 32:
                slab_load(g + 1, io["h1"], lambda nb, qt: [h1_ks[(nb % 4, qt)]])
            sl, sl_k = slab[g % 3]
            r0 = qt * 512
            for t in range(qt * 4, qt * 4 + 4):
                ps, ps_k = k.next_pf(0, 3)
                for c in range(KC):
                    k.mm(ps[:, :], xT[:, c, t * 128:(t + 1) * 128], w[:, c, :], c == 0, c == KC - 1,
                         [xT_ks[t], w_k], [ps_k], mark=(c == KC - 1))
                ps2, ps2_k = k.next_pf(3, 6)
                for c in range(2):
                    k.mm(ps2[:, :], pT[:, c, t * 128:(t + 1) * 128], w2[:, c, :], c == 0, c == 1,
                         [pT_ks[t], w2_k], [ps2_k], mark=(c == 1))
                g_, g_k = gs[gi % 2]
                gi += 1
                k.act(g_[:, :], ps[:, :], AF.Sigmoid, [ps_k], [g_k])
                k.tt("dve", g_[:, :], g_[:, :], ps2[:, :], ALU.mult, [g_k, ps2_k], [g_k])
                k.tt("dve", sl[:, t - qt * 4, :], sl[:, t - qt * 4, :], g_[:, :], ALU.add, [sl_k, g_k], [sl_k])
            P.dma("sp", io["hout"][r0:r0 + 512, nb * 512:(nb + 1) * 512].rearrange("(t p) n -> p t n", p=128),
                  sl[:, :, :], [sl_k], [out_k], sl_k)
    return {"hout": out_k}


def transpose_to(k, ob, ob_k, ngrp, dn, dst3, dst_k):
    def go():
        pb, pb_k = k.next_pb()
        for g in range(ngrp):
            k.tr(pb[0:dn, g * 128:(g + 1) * 128], ob[:, g * dn:(g + 1) * dn], [ob_k], [pb_k], mark=(g == ngrp - 1))
        k.copy("act", dst3, pb[0:dn, 0:ngrp * 128].rearrange("p (g q) -> p g q", g=ngrp), [pb_k], [dst_k])
    k.defer(go)


def phaseA_mla(k, io):
    P = k.P
    gain, gain_k = bcast_load(k, "M_ng", io["ng"], D)
    cqg, cqg_k = bcast_load(k, "M_cqg", io["cqg"], 512)
    ckvg, ckvg_k = bcast_load(k, "M_ckvg", io["ckvg"], 512)
    qng, qng_k = bcast_load(k, "M_qng", io["qng"], 128)
    qrg, qrg_k = bcast_load(k, "M_qrg", io["qrg"], 64)
    kng, kng_k = bcast_load(k, "M_kng", io["kng"], 128)
    krg, krg_k = bcast_load(k, "M_krg", io["krg"], 64)
    rope = Rope(k, "M_rp", io["posT"], io["inv32"], NT, 32)
    xT, _ = P.sbuf("M_xT", [128, KC, T], BF16)
    xT_ks = [Tk("xT%d" % t) for t in range(NT)]
    nt = NormT(k, "M_n")
    nt.run(io["h"], lambda t: io["h_deps"], gain, gain_k, xT, xT_ks)
    ws = WStream(k, "M", KC)
    lat = QKProc(k, "M_lat", 1, 512, with_rope=False, nob=4)
    p128 = QKProc(k, "M_p128", 2, 128, with_rope=False)
    p64 = QKProc(k, "M_p64", 2, 64)
    cqT, _ = P.sbuf("M_cqT", [128, 4, T], BF16)
    ckvT, _ = P.sbuf("M_ckvT", [128, 4, T], BF16)
    cq_ks = [Tk("cq%d" % t) for t in range(NT)]
    ckv_ks = [Tk("ckv%d" % t) for t in range(NT)]
    stb = [(nt.hb[i][0][:, :].bitcast(BF16), nt.hb[i][1]) for i in range(2)]
    stc = [(nt.sq[0][:, :], nt.sq[1])]
    outs = {n: Tk(n) for n in ["QNT", "QPT", "KNT", "KPT", "V", "G"]}
    si = 0
    jobs = [(io["w_in"], KC, 0, 512), (io["w_in"], KC, 512, 512), (io["w_in"], KC, 1024, 64)] + \
           [(io["w_in"], KC, 1088 + i * 512, 512) for i in range(4)]
    for jb, w, w_k in ws.stream(jobs):
        if jb < 2:
            dstT, dks = (cqT, cq_ks) if jb == 0 else (ckvT, ckv_ks)
            gn, gn_k = (cqg, cqg_k) if jb == 0 else (ckvg, ckvg_k)

            def evac(t, ps, ps_k, dstT=dstT, dks=dks, gn=gn, gn_k=gn_k):
                ob, ob_k = lat.run(ps[:, :].rearrange("p (g d) -> p g d", g=1), ps_k, gn, gn_k, None, t, do_rope=False)
                transpose_to(k, ob, ob_k, 4, 128, dstT[:, :, t * 128:(t + 1) * 128], dks[t])
            linear_tok(k, xT, xT_ks, KC, w, w_k, 512, range(NT), evac)
        elif jb == 2:
            kp, kp_k = stc[0]

            def evac(t, ps, ps_k, kp=kp, kp_k=kp_k):
                ob, ob_k = p64.run(ps[:, 0:64].rearrange("p (g d) -> p g d", g=1), ps_k, krg, krg_k, rope, t, ngrp=1)
                transpose_to(k, ob, ob_k, 1, 64, kp[0:64, t * 128:(t + 1) * 128].rearrange("p (g q) -> p g q", g=1), kp_k)
            linear_tok(k, xT, xT_ks, KC, w, w_k, 64, range(NT), evac)
            k.flush()
            P.dma("sp", io["KPT"], kp[0:64, :], [kp_k], [outs["KPT"]], kp_k)
        else:
            for hf in range(2):
                stf, st_k = stb[si % 2]
                si += 1
                st = stf.rearrange("p (t n) -> p t n", t=8)

                def evac(t, ps, ps_k, st=st, st_k=st_k, hf=hf):
                    k.act(st[:, t - hf * 8, :], ps[:, :], AF.Silu, [ps_k], [st_k])
                linear_tok(k, xT, xT_ks, KC, w, w_k, 512, range(hf * 8, hf * 8 + 8), evac)
                k.flush()
                c0 = (jb - 3) * 512
                P.dma("sp", io["G"][hf * 1024:(hf + 1) * 1024, c0:c0 + 512].rearrange("(t p) n -> p t n", p=128), st, [st_k],
                      [outs["G"]], st_k)
    k.flush()
    ws4 = ws
    for jb, w, w_k in ws4.stream([(io["w_uq"], 4, i * 384, 384) for i in range(8)]):
        for hf in range(2):
            stf, st_k = stb[si % 2]
            si += 1
            stn = stf[:, 0:2048].rearrange("p (g t) -> p g t", g=2)
            stp = stf[:, 2048:4096].rearrange("p (g t) -> p g t", g=2)

            def evac(t, ps, ps_k, stn=stn, stp=stp, st_k=st_k, hf=hf):
                v = ps[:, 0:384].rearrange("p (g d) -> p g d", g=2)
                tl = t - hf * 8
                ob, ob_k = p128.run(v[:, :, 0:128], ps_k, qng, qng_k, None, t, do_rope=False)
                transpose_to(k, ob, ob_k, 2, 128, stn[:, :, tl * 128:(tl + 1) * 128], st_k)
                ob, ob_k = p64.run(v[:, :, 128:192], ps_k, qrg, qrg_k, rope, t)
                transpose_to(k, ob, ob_k, 2, 64, stp[0:64, :, tl * 128:(tl + 1) * 128], st_k)
            linear_tok(k, cqT, cq_ks, 4, w, w_k, 384, range(hf * 8, hf * 8 + 8), evac)

            def out_dma(jb=jb, hf=hf, stn=stn, stp=stp, st_k=st_k):
                P.dma("sp", io["QNT"][2 * jb:2 * jb + 2, :, hf * 1024:(hf + 1) * 1024].rearrange("g d t -> d g t"), stn, [st_k],
                      [outs["QNT"]], st_k)
                P.dma("sp", io["QPT"][2 * jb:2 * jb + 2, :, hf * 1024:(hf + 1) * 1024].rearrange("g d t -> d g t"), stp[0:64, :, :], [st_k],
                      [outs["QPT"]], st_k)
            k.defer(out_dma)
    for jb, w, w_k in ws4.stream([(io["w_ukv"], 4, i * 512, 512) for i in range(8)]):
        for hf in range(2):
            stf, st_k = stb[si % 2]
            si += 1
            stn = stf[:, 0:2048].rearrange("p (g t) -> p g t", g=2)
            stv = stf[:, 2048:4096].rearrange("p (t g e) -> p t g e", t=8, g=2)

            def evac(t, ps, ps_k, stn=stn, stv=stv, st_k=st_k, hf=hf):
                v = ps[:, :].rearrange("p (g d) -> p g d", g=2)
                tl = t - hf * 8
                ob, ob_k = p128.run(v[:, :, 0:128], ps_k, kng, kng_k, None, t, do_rope=False)
                transpose_to(k, ob, ob_k, 2, 128, stn[:, :, tl * 128:(tl + 1) * 128], st_k)
                k.act(stv[:, tl, :, :], v[:, :, 128:256], AF.Copy, [ps_k], [st_k])
            linear_tok(k, ckvT, ckv_ks, 4, w, w_k, 512, range(hf * 8, hf * 8 + 8), evac)

            def out_dma(jb=jb, hf=hf, stn=stn, stv=stv, st_k=st_k):
                P.dma("sp", io["KNT"][2 * jb:2 * jb + 2, :, hf * 1024:(hf + 1) * 1024].rearrange("g d t -> d g t"), stn, [st_k],
                      [outs["KNT"]], st_k)
                P.dma("sp", io["V"][hf * 1024:(hf + 1) * 1024, jb * 256:(jb + 1) * 256].rearrange("(t p) (g e) -> p t g e", p=128, g=2),
                      stv, [st_k], [outs["V"]], st_k)
            k.defer(out_dma)
    k.flush()
    return outs


def phaseB_mla(k, io):
    P = k.P
    NTS = S // 128
    scale = 192 ** -0.5
    SHIFT = -8.0
    bias, bias_k = P.sbuf("B_bias", [128, 1], F32)
    k.memset("dve", bias[:, :], SHIFT, [], [bias_k])
    kp, kp_k = P.sbuf("B_kp", [64, S], BF16)
    P.dma("sp", kp[:, :], io["KPT"], io["in_deps"], [kp_k], kp_k)
    qn = [P.sbuf("B_qn%d" % i, [128, S], BF16) for i in range(2)]
    qp = [P.sbuf("B_qp%d" % i, [64, S], BF16) for i in range(2)]
    kn = [P.sbuf("B_kn%d" % i, [128, S], BF16) for i in range(2)]
    va = [P.sbuf("B_va%d" % i, [128, NTS, 129], BF16) for i in range(2)]
    for (v_, v_k) in va:
        k.memset("dve", v_[:, :, 128:129], 1.0, [], [v_k])
    E = [P.sbuf("B_E%d" % i, [128, 512], BF16) for i in range(3)]
    rr = [P.sbuf("B_rr%d" % i, [128, 1], F32) for i in range(2)]
    ost = [P.sbuf("B_ost%d" % i, [128, 4, 128], BF16) for i in range(2)]
    O_k = Tk("O")
    ei = 0
    oi = 0
    for hd in range(8):
        v_, v_k = va[hd % 2]
        q_, q_k = qn[hd % 2]
        qp_, qp_k = qp[hd % 2]
        k_, k_k = kn[hd % 2]
        P.dma("sp", v_[:, :, 0:128], io["V"][:, hd * 128:(hd + 1) * 128].rearrange("(t p) e -> p t e", p=128),
              io["in_deps"], [v_k], v_k)
        P.dma("sp", q_[:, :], io["QNT"][hd, :, :], io["in_deps"], [q_k], q_k)
        P.dma("sp", qp_[:, :], io["QPT"][hd, :, :], io["in_deps"], [qp_k], qp_k)
        P.dma("sp", k_[:, :], io["KNT"][hd, :, :], io["in_deps"], [k_k], k_k)
        for qb in range(S // 512):
            os_, os_k = ost[oi % 2]
            oi += 1
            acc = [k.pf[j] for j in range(4)]
            nkt = 4 * (qb + 1)
            for kt in range(nkt):
                j0 = max(0, kt - 4 * qb)
                c0 = j0 * 128
                ps, ps_k = k.next_pf(4, 6)
                k.mm(ps[:, c0:512], k_[:, kt * 128:(kt + 1) * 128], q_[:, qb * 512 + c0:(qb + 1) * 512], True, False,
                     [k_k, q_k], [ps_k], mark=False)
                k.mm(ps[:, c0:512], kp[:, kt * 128:(kt + 1) * 128], qp_[:, qb * 512 + c0:(qb + 1) * 512], False, True,
                     [kp_k, qp_k], [ps_k], mark=True)
                e_, e_k = E[ei % 3]
                ei += 1
                k.act(e_[:, c0:512], ps[:, c0:512], AF.Exp, [ps_k, bias_k], [e_k], scale=scale, bias=bias[:, 0:1])
                if kt >= 4 * qb:
                    k.memset("dve", e_[64:128, c0:c0 + 64], 0.0, [], [e_k])
                for j in range(j0, 4):
                    a_, a_k = acc[j]
                    k.mm(a_[:, 0:129], e_[:, j * 128:(j + 1) * 128], v_[:, kt, 0:129], kt == 0, kt == 4 * qb + j,
                         [e_k, v_k], [a_k], mark=(kt == 4 * qb + j))
            for j in range(4):
                a_, a_k = acc[j]
                r_, r_k = rr[j % 2]
                k.recip(r_[:, :], a_[:, 128:129], [a_k], [r_k])
                k.ts("dve", os_[:, j, :], a_[:, 0:128], r_[:, 0:1], None, ALU.mult, None, [a_k, r_k], [os_k])
            P.dma("sp", io["O"][qb * 512:(qb + 1) * 512, hd * 128:(hd + 1) * 128].rearrange("(j p) e -> p j e", p=128),
                  os_[:, :, :], [os_k], [O_k], os_k)
    return {"O": O_k}


def phaseA_lru(k, io):
    P = k.P
    gain, gain_k = bcast_load(k, "R_ng", io["ng"], D)
    xT, _ = P.sbuf("R_xT", [128, KC, T], BF16)
    xT_ks = [Tk("xT%d" % t) for t in range(NT)]
    nt = NormT(k, "R_n")
    nt.run(io["h"], lambda t: io["h_deps"], gain, gain_k, xT, xT_ks)
    ws = WStream(k, "R", KC)
    stx = [P.sbuf("R_stx%d" % i, [128, T], F32) for i in range(2)]
    stg = [(nt.hb[i][0][:, :].bitcast(BF16)[:, 0:T], nt.hb[i][1]) for i in range(2)]
    outs = {"XBT": Tk("XBT"), "GT": Tk("GT")}
    si = 0
    for nb, w, w_k in ws.stream([(io["w_in"], KC, nb * 512, 512) for nb in range(8)]):
        for fl in range(4):
            fc = (nb % 4) * 4 + fl
            if nb < 4:
                st, st_k = stx[si % 2]
                st = st[:, :]
            else:
                st, st_k = stg[si % 2]
            si += 1
            for tb in range(4):
                ps, ps_k = k.next_pf(0, 3)
                for c in range(KC):
                    k.mm(ps[:, :], w[:, c, fl * 128:(fl + 1) * 128], xT[:, c, tb * 512:(tb + 1) * 512], c == 0, c == KC - 1,
                         [w_k] + xT_ks[tb * 4:tb * 4 + 4], [ps_k], mark=(c == KC - 1))
                k.act(st[:, tb * 512:(tb + 1) * 512], ps[:, :], AF.Copy if nb < 4 else AF.Silu, [ps_k], [st_k])
            nm = "XBT" if nb < 4 else "GT"
            P.dma("sp", io[nm][fc, :, :], st, [st_k], [outs[nm]], st_k)
    return outs


def phaseB_lru(k, io):
    P, nc = k.P, k.nc
    cw, cw_k = P.sbuf("L_cw", [128, 8, 4], F32)
    P.dma("sp", cw[:, :, :], io["cw"], [], [cw_k], cw_k)
    small = {}
    for n in ["cb", "ba", "bx", "lam"]:
        t_, t_k = P.sbuf("L_" + n, [128, 8], F32)
        P.dma("sp", t_[:, :], io[n], [], [t_k], t_k)
        small[n] = (t_, t_k)
    ce, ce_k = P.sbuf("L_ce", [128, 8], F32)
    k.act(ce[:, :], small["lam"][0][:, :], AF.Exp, [small["lam"][1]], [ce_k], scale=-1.0)
    k.act(ce[:, :], ce[:, :], AF.Ln, [ce_k], [ce_k], bias=1.0)
    k.ts("dve", ce[:, :], ce[:, :], -8.0, None, ALU.mult, None, [ce_k], [ce_k])
    xb = [P.sbuf("L_xb%d" % i, [128, S], F32) for i in range(2)]
    xc = [P.sbuf("L_xc%d" % i, [128, S], F32) for i in range(2)]
    xcb = [P.sbuf("L_xcb%d" % i, [128, S], BF16) for i in range(2)]
    rg = [P.sbuf("L_r%d" % i, [128, S], F32) for i in range(2)]
    ig = [P.sbuf("L_i%d" % i, [128, S], F32) for i in range(2)]
    tmp = P.sbuf("L_tmp", [128, S], F32)
    gt = [P.sbuf("L_g%d" % i, [128, S], BF16) for i in range(2)]
    wa = P.sbuf("L_wa", [128, 2, 256], BF16)
    wx = P.sbuf("L_wx", [128, 2, 256], BF16)
    Y_k = Tk("YT")
    for gb in range(4):
        P.dma("pool", wa[0][:, :, :], io["wa"][gb].rearrange("(c p) n -> p c n", p=128), [], [wa[1]], wa[1])
        P.dma("pool", wx[0][:, :, :], io["wx"][gb].rearrange("(c p) n -> p c n", p=128), [], [wx[1]], wx[1])
        for ic in range(2):
            ch = gb * 2 + ic
            b_, b_k = xb[ic]
            c_, c_k = xc[ic]
            cb_, cb_k = xcb[ic]
            P.dma("sp", b_[:, :], io["XBT"][ch, :, :], io["in_deps"], [b_k], b_k)
            P.dma("sp", gt[ic][0][:, :], io["GT"][ch, :, :], io["in_deps"], [gt[ic][1]], gt[ic][1])
            k.act(c_[:, :], b_[:, :], AF.Identity, [b_k, cw_k, small["cb"][1]], [c_k],
                  scale=cw[:, ch, 3:4], bias=small["cb"][0][:, ch:ch + 1])
            for sh in (1, 2, 3):
                k.stt(c_[:, sh:S], b_[:, 0:S - sh], cw[:, ch, 3 - sh:4 - sh], c_[:, sh:S], ALU.mult, ALU.add,
                      [b_k, cw_k, c_k], [c_k])
            k.copy("act", cb_[:, :], c_[:, :], [c_k], [cb_k])
        for oc in range(2):
            ch = gb * 2 + oc
            for (wt, gate, bname) in ((wa, rg, "ba"), (wx, ig, "bx")):
                g_, g_k = gate[oc]
                for tb in range(S // 512):
                    ps, ps_k = k.next_pf(0, 4)
                    for ic in range(2):
                        k.mm(ps[:, :], wt[0][:, ic, oc * 128:(oc + 1) * 128], xcb[ic][0][:, tb * 512:(tb + 1) * 512], ic == 0, ic == 1,
                             [wt[1], xcb[ic][1]], [ps_k], mark=(ic == 1))
                    k.act(g_[:, tb * 512:(tb + 1) * 512], ps[:, :], AF.Sigmoid, [ps_k, small[bname][1]], [g_k],
                          bias=small[bname][0][:, ch:ch + 1])
            r_, r_k = rg[oc]
            i_, i_k = ig[oc]
            c_, c_k = xc[oc]
            t_, t_k = tmp
            h_, h_k = xb[oc]
            k.act(r_[:, :], r_[:, :], AF.Exp, [r_k, ce_k], [r_k], scale=ce[:, ch:ch + 1])
            k.tt("dve", t_[:, :], r_[:, :], r_[:, :], ALU.mult, [r_k], [t_k])
            k.ts("dve", t_[:, :], t_[:, :], -1.0, 1.0, ALU.mult, ALU.add, [t_k], [t_k])
            k.act(t_[:, :], t_[:, :], AF.Sqrt, [t_k], [t_k])
            k.tt("dve", i_[:, :], i_[:, :], c_[:, :], ALU.mult, [i_k, c_k], [i_k])
            k.tt("dve", i_[:, :], i_[:, :], t_[:, :], ALU.mult, [i_k, t_k], [i_k])
            P.op("dve", lambda: nc.vector.tensor_tensor_scan(out=h_[:, :], data0=r_[:, :], data1=i_[:, :], initial=0.0,
                                                             op0=ALU.mult, op1=ALU.add), [r_k, i_k], [h_k])
            y_, y_k = gt[oc]
            k.tt("dve", y_[:, :], h_[:, :], y_[:, :], ALU.mult, [h_k, y_k], [y_k])
            P.dma("sp", io["YT"][ch, :, :], y_[:, :], [y_k], [Y_k], y_k)
    return {"YT": Y_k}


IDENT = np.eye(128, dtype=np.float32)
INV64 = (10000.0 ** (-np.arange(64, dtype=np.float32) / np.float32(64))).astype(np.float32)
INV32 = (10000.0 ** (-np.arange(32, dtype=np.float32) / np.float32(32))).astype(np.float32)
BF = ml_dtypes.bfloat16


class Launch:
    def __init__(self):
        self.nc = bass.Bass("TRN2", target_bir_lowering=False)
        self.stack = contextlib.ExitStack()
        self.P = Prog(self.nc, self.stack)
        self.ident_d = self.inp("ident", [128, 128], F32)
        self.k = K(self.nc, self.P, self.ident_d)
        self.outs = []

    def inp(self, name, shape, dt):
        return self.nc.dram_tensor(name, list(shape), dt, kind="ExternalInput").ap()

    def out(self, name, shape, dt):
        self.outs.append(name)
        return self.nc.dram_tensor(name, list(shape), dt, kind="ExternalOutput").ap()

    def scratch(self, name, shape, dt):
        return self.nc.dram_tensor(name, list(shape), dt, kind="Internal").ap()

    def run(self, in_maps, final_tks):
        self.P.finish(final_tks)
        self.stack.close()
        for m in in_maps:
            m["ident"] = IDENT
        res = run_bass_kernel_spmd(self.nc, in_maps, core_ids=list(range(NCORES)))
        return res.results


def posT_of(positions, c):
    b, hf = c // 2, c % 2
    return np.ascontiguousarray(positions[b, hf * T:(hf + 1) * T].reshape(NT, 128).T.astype(np.int32))


def launch_A_diff(h_cores, positions, ng, w_in, qg, kg):
    L = Launch()
    io = {"h": L.inp("h", [T, D], F32), "ng": L.inp("ng", [D], F32), "w_in": L.inp("w_in", [D, 8192], F32),
          "qg": L.inp("qg", [128], F32), "kg": L.inp("kg", [128], F32), "posT": L.inp("posT", [128, NT], I32),
          "inv64": L.inp("inv64", [64], F32), "h_deps": [],
          "QT": L.out("QT", [16, 128, T], BF16), "KT": L.out("KT", [16, 128, T], BF16),
          "V": L.out("V", [T, 2048], BF16), "G": L.out("G", [T, 2048], BF16)}
    outs = phaseA_diff(L.k, io)
    in_maps = [{"h": h_cores[c], "ng": ng, "w_in": w_in, "qg": qg, "kg": kg, "posT": posT_of(positions, c),
                "inv64": INV64} for c in range(NCORES)]
    return L.run(in_maps, list(outs.values()))


def launch_B_diff(resA, lq1, lk1, lq2, lk2, sg, lam_init):
    L = Launch()
    io = {"QT": L.inp("QT", [8, 128, S], BF16), "KT": L.inp("KT", [8, 128, S], BF16), "V": L.inp("V", [S, 1024], BF16),
          "lq1": L.inp("lq1", [128], F32), "lk1": L.inp("lk1", [128], F32), "lq2": L.inp("lq2", [128], F32),
          "lk2": L.inp("lk2", [128], F32), "sg": L.inp("sg", [256], F32), "in_deps": [],
          "O": L.out("O", [S, 1024], BF16)}
    outs = phaseB_diff(L.k, io, lam_init)
    in_maps = []
    for c in range(NCORES):
        b, g = c // 2, c % 2
        r0, r1 = resA[2 * b], resA[2 * b + 1]
        in_maps.append({
            "QT": np.concatenate([r0["QT"][g * 8:(g + 1) * 8], r1["QT"][g * 8:(g + 1) * 8]], axis=2),
            "KT": np.concatenate([r0["KT"][g * 8:(g + 1) * 8], r1["KT"][g * 8:(g + 1) * 8]], axis=2),
            "V": np.concatenate([r0["V"][:, g * 1024:(g + 1) * 1024], r1["V"][:, g * 1024:(g + 1) * 1024]], axis=0),
            "lq1": lq1, "lk1": lk1, "lq2": lq2, "lk2": lk2, "sg": sg})
    return L.run(in_maps, list(outs.values()))


def gather_O(resB, width):
    out = []
    for c in range(NCORES):
        b, hf = c // 2, c % 2
        out.append(np.concatenate([resB[2 * b]["O"][hf * T:(hf + 1) * T], resB[2 * b + 1]["O"][hf * T:(hf + 1) * T]], axis=1))
    return out


def launch_C(h_cores, O_cores, G_cores, w_out, pg, w_g, w_p, p_l):
    L = Launch()
    io = {"h": L.inp("h", [T, D], F32), "O": L.inp("O", [T, D], BF16), "G": L.inp("G", [T, D], BF16),
          "w_out": L.inp("w_out", [D, D], F32), "pg": L.inp("pg", [D], F32), "w_g": L.inp("w_g", [D, D], F32),
          "w_p": L.inp("w_p", [256, D], F32), "p": L.inp("p", [T, 256], F32), "in_deps": [], "h_deps": [],
          "h1": L.scratch("h1", [T, D], F32), "hout": L.out("hout", [T, D], F32)}
    outs = phaseC(L.k, io)
    in_maps = []
    for c in range(NCORES):
        b, hf = c // 2, c % 2
        in_maps.append({"h": h_cores[c], "O": O_cores[c], "G": G_cores[c], "w_out": w_out, "pg": pg, "w_g": w_g,
                        "w_p": w_p, "p": np.ascontiguousarray(p_l[b, hf * T:(hf + 1) * T])})
    return L.run(in_maps, list(outs.values()))


def split_cores(x):
    return [np.ascontiguousarray(x[c // 2, (c % 2) * T:(c % 2 + 1) * T]) for c in range(NCORES)]


def layer_diff(h_cores, positions, li, j, I):
    lam_init = 0.8 - 0.6 * math.exp(-0.3 * li)
    rA = launch_A_diff(h_cores, positions, I["norm_gain"][li], I["a_w_in"][j], I["a_q_norm"][j], I["a_k_norm"][j])
    rB = launch_B_diff(rA, I["a_lambda_q1"][j], I["a_lambda_k1"][j], I["a_lambda_q2"][j], I["a_lambda_k2"][j],
                       I["a_sub_norm"][j], lam_init)
    O = gather_O(rB, 1024)
    rC = launch_C(h_cores, O, [r["G"] for r in rA], I["a_w_out"][j], I["ple_norm"][li], I["ple_w_gate"][li],
                  I["ple_w_proj"][li], I["p"][li])
    return [r["hout"] for r in rC], (rA, rB, rC)


def launch_A_mla(h_cores, positions, I, li, j):
    L = Launch()
    io = {"h": L.inp("h", [T, D], F32), "ng": L.inp("ng", [D], F32), "w_in": L.inp("w_in", [D, 3136], F32),
          "cqg": L.inp("cqg", [512], F32), "ckvg": L.inp("ckvg", [512], F32),
          "w_uq": L.inp("w_uq", [512, 3072], F32), "w_ukv": L.inp("w_ukv", [512, 4096], F32),
          "qng": L.inp("qng", [128], F32), "qrg": L.inp("qrg", [64], F32), "kng": L.inp("kng", [128], F32),
          "krg": L.inp("krg", [64], F32), "posT": L.inp("posT", [128, NT], I32), "inv32": L.inp("inv32", [32], F32),
          "h_deps": [],
          "QNT": L.out("QNT", [16, 128, T], BF16), "QPT": L.out("QPT", [16, 64, T], BF16),
          "KNT": L.out("KNT", [16, 128, T], BF16), "KPT": L.out("KPT", [64, T], BF16),
          "V": L.out("V", [T, 2048], BF16), "G": L.out("G", [T, 2048], BF16)}
    outs = phaseA_mla(L.k, io)
    in_maps = [{"h": h_cores[c], "ng": I["norm_gain"][li], "w_in": I["b_w_in"][j], "cqg": I["b_cq_norm"][j],
                "ckvg": I["b_ckv_norm"][j], "w_uq": I["b_w_uq"][j], "w_ukv": I["b_w_ukv"][j],
                "qng": I["b_q_nope_norm"][j], "qrg": I["b_q_rope_norm"][j], "kng": I["b_k_nope_norm"][j],
                "krg": I["b_k_rope_norm"][j], "posT": posT_of(positions, c), "inv32": INV32} for c in range(NCORES)]
    return L.run(in_maps, list(outs.values()))


def launch_B_mla(resA):
    L = Launch()
    io = {"QNT": L.inp("QNT", [8, 128, S], BF16), "QPT": L.inp("QPT", [8, 64, S], BF16),
          "KNT": L.inp("KNT", [8, 128, S], BF16), "KPT": L.inp("KPT", [64, S], BF16),
          "V": L.inp("V", [S, 1024], BF16), "in_deps": [], "O": L.out("O", [S, 1024], BF16)}
    outs = phaseB_mla(L.k, io)
    in_maps = []
    for c in range(NCORES):
        b, g = c // 2, c % 2
        r0, r1 = resA[2 * b], resA[2 * b + 1]
        cat = lambda n, sl, ax: np.concatenate([r0[n][sl], r1[n][sl]], axis=ax)
        in_maps.append({
            "QNT": cat("QNT", slice(g * 8, (g + 1) * 8), 2), "QPT": cat("QPT", slice(g * 8, (g + 1) * 8), 2),
            "KNT": cat("KNT", slice(g * 8, (g + 1) * 8), 2), "KPT": cat("KPT", slice(None), 1),
            "V": np.concatenate([r0["V"][:, g * 1024:(g + 1) * 1024], r1["V"][:, g * 1024:(g + 1) * 1024]], axis=0)})
    return L.run(in_maps, list(outs.values()))


def launch_A_lru(h_cores, I, li, j):
    L = Launch()
    io = {"h": L.inp("h", [T, D], F32), "ng": L.inp("ng", [D], F32), "w_in": L.inp("w_in", [D, 4096], F32), "h_deps": [],
          "XBT": L.out("XBT", [16, 128, T], F32), "GT": L.out("GT", [16, 128, T], BF16)}
    outs = phaseA_lru(L.k, io)
    in_maps = [{"h": h_cores[c], "ng": I["norm_gain"][li], "w_in": I["c_w_in"][j]} for c in range(NCORES)]
    return L.run(in_maps, list(outs.values()))


def launch_B_lru(resA, I, j):
    L = Launch()
    io = {"XBT": L.inp("XBT", [8, 128, S], F32), "GT": L.inp("GT", [8, 128, S], BF16),
          "cw": L.inp("cw", [128, 8, 4], F32), "cb": L.inp("cb", [128, 8], F32), "ba": L.inp("ba", [128, 8], F32),
          "bx": L.inp("bx", [128, 8], F32), "lam": L.inp("lam", [128, 8], F32),
          "wa": L.inp("wa", [4, 256, 256], F32), "wx": L.inp("wx", [4, 256, 256], F32), "in_deps": [],
          "YT": L.out("YT", [8, 128, S], BF16)}
    outs = phaseB_lru(L.k, io)
    in_maps = []
    fm = lambda v, g: np.ascontiguousarray(v[g * 1024:(g + 1) * 1024].reshape(8, 128).T)
    for c in range(NCORES):
        b, g = c // 2, c % 2
        r0, r1 = resA[2 * b], resA[2 * b + 1]
        cwl = I["c_conv_w"][j][:, g * 1024:(g + 1) * 1024]
        in_maps.append({
            "XBT": np.concatenate([r0["XBT"][g * 8:(g + 1) * 8], r1["XBT"][g * 8:(g + 1) * 8]], axis=2),
            "GT": np.concatenate([r0["GT"][g * 8:(g + 1) * 8], r1["GT"][g * 8:(g + 1) * 8]], axis=2),
            "cw": np.ascontiguousarray(cwl.reshape(4, 8, 128).transpose(2, 1, 0)),
            "cb": fm(I["c_conv_b"][j], g), "ba": fm(I["c_b_a"][j], g), "bx": fm(I["c_b_x"][j], g), "lam": fm(I["c_lambda"][j], g),
            "wa": np.ascontiguousarray(I["c_w_a"][j][g * 4:(g + 1) * 4]), "wx": np.ascontiguousarray(I["c_w_x"][j][g * 4:(g + 1) * 4])})
    return L.run(in_maps, list(outs.values()))


def launch_C_lru(h_cores, YT_cores, w_out, pg, w_g, w_p, p_l):
    L = Launch()
    io = {"h": L.inp("h", [T, D], F32), "YT": L.inp("YT", [KC, 128, T], BF16),
          "w_out": L.inp("w_out", [D, D], F32), "pg": L.inp("pg", [D], F32), "w_g": L.inp("w_g", [D, D], F32),
          "w_p": L.inp("w_p", [256, D], F32), "p": L.inp("p", [T, 256], F32), "in_deps": [], "h_deps": [],
          "h1": L.scratch("h1", [T, D], F32), "hout": L.out("hout", [T, D], F32)}
    outs = phaseC(L.k, io, feat_major_y=True)
    in_maps = []
    for c in range(NCORES):
        b, hf = c // 2, c % 2
        in_maps.append({"h": h_cores[c], "YT": YT_cores[c], "w_out": w_out, "pg": pg, "w_g": w_g,
                        "w_p": w_p, "p": np.ascontiguousarray(p_l[b, hf * T:(hf + 1) * T])})
    return L.run(in_maps, list(outs.values()))


def layer_mla(h_cores, positions, li, j, I):
    rA = launch_A_mla(h_cores, positions, I, li, j)
    rB = launch_B_mla(rA)
    O = gather_O(rB, 1024)
    rC = launch_C(h_cores, O, [r["G"] for r in rA], I["b_w_out"][j], I["ple_norm"][li], I["ple_w_gate"][li],
                  I["ple_w_proj"][li], I["p"][li])
    return [r["hout"] for r in rC], (rA, rB, rC)


def layer_lru(h_cores, li, j, I):
    rA = launch_A_lru(h_cores, I, li, j)
    rB = launch_B_lru(rA, I, j)
    YT = []
    for c in range(NCORES):
        b, hf = c // 2, c % 2
        YT.append(np.concatenate([rB[2 * b]["YT"][:, :, hf * T:(hf + 1) * T], rB[2 * b + 1]["YT"][:, :, hf * T:(hf + 1) * T]], axis=0))
    rC = launch_C_lru(h_cores, YT, I["c_w_out"][j], I["ple_norm"][li], I["ple_w_gate"][li], I["ple_w_proj"][li], I["p"][li])
    return [r["hout"] for r in rC], (rA, rB, rC)


NEG = -30000.0
SHIFT = -8.0
PAIRS = [[0, 1], [2, 3], [4, 5], [6, 7]]


def attn_core(k, io, nheads_c, dv, load_head, score_mm, finish_q, mb, mb_k, bias, bias_k, scale):
    P = k.P
    LA = 3
    NE = LA + 2
    E = [P.sbuf("B_E%d" % i, [128, 512], BF16) for i in range(NE)]
    st = {"ei": 0}
    units = [(qb, ktg) for qb in range(T // 512) for ktg in range(8 * qb + 8)]
    acc = [k.pf[j] for j in range(4)]

    def emit_score(u, rd):
        qb, ktg = units[u]
        rank, j = ktg % 2, ktg // 2
        d = ktg - 8 * qb
        j0 = 0 if d < 0 else d // 2
        c0 = j0 * 128
        ps, ps_k = k.next_pf(4, 8)
        score_mm(ps, ps_k, c0, rank, j, qb, rd)
        e_, e_k = E[st["ei"] % NE]
        st["ei"] += 1
        if d < 0:
            k.act(e_[:, c0:512], ps[:, c0:512], AF.Exp, [ps_k, bias_k], [e_k], scale=scale, bias=bias[:, 0:1])
        else:
            if d % 2 == 0:
                b0, b1 = mb[:, 0:1], bias[:, 0:1]
            else:
                b0, b1 = mb[:, 1:2], mb[:, 2:3]
            k.act(e_[:, c0:c0 + 64], ps[:, c0:c0 + 64], AF.Exp, [ps_k, mb_k, bias_k], [e_k], scale=scale, bias=b0)
            k.act(e_[:, c0 + 64:c0 + 128], ps[:, c0 + 64:c0 + 128], AF.Exp, [ps_k, mb_k, bias_k], [e_k], scale=scale, bias=b1)
            if c0 + 128 < 512:
                k.act(e_[:, c0 + 128:512], ps[:, c0 + 128:512], AF.Exp, [ps_k, bias_k], [e_k], scale=scale, bias=bias[:, 0:1])
        return (e_, e_k, j0, rank, j)

    nxt_head = load_head(0)
    for hc in range(nheads_c):
        rd, va, va_k = nxt_head
        pendq = [emit_score(u, rd) for u in range(LA)]
        if hc + 1 < nheads_c:
            nxt_head = load_head(hc + 1)
        if io.get("bg_hook"):
            pace = Tk("pace")
            pace.w = list(acc[3][1].w)
            io["bg_hook"]([pace])
        for u in range(len(units)):
            if u + LA < len(units):
                pendq.append(emit_score(u + LA, rd))
            qb, ktg = units[u]
            e_, e_k, j0, rank, j = pendq.pop(0)
            for jq in range(j0, 4):
                a_, a_k = acc[jq]
                last = 8 * qb + 2 * jq + 1
                k.mm(a_[:, 0:dv + 1], e_[:, jq * 128:(jq + 1) * 128], va[:, rank * 16 + j, 0:dv + 1], ktg == 0, ktg == last,
                     [e_k, va_k], [a_k], mark=(ktg == last))
            if ktg == 8 * qb + 7:
                finish_q(hc, qb, acc)


def phaseB2_diff(k, io, lam_init):
    P = k.P
    scale = 128 ** -0.5
    lv = [bcast_load(k, "B_lv%d" % i, io[n], 128) for i, n in enumerate(["lq1", "lk1", "lq2", "lk2"])]
    sg, sg_k = bcast_load(k, "B_sg", io["sg"], 256)
    lt, lt_k = P.sbuf("B_lt", [128, 128], F32)
    l2, l2_k = P.sbuf("B_l2", [128, 2], F32)
    lam, lam_k = P.sbuf("B_lam", [128, 1], F32)
    nlam, nlam_k = P.sbuf("B_nlam", [128, 1], F32)
    for j in range(2):
        k.tt("dve", lt[:, :], lv[2 * j][0][:, :], lv[2 * j + 1][0][:, :], ALU.mult, [lv[2 * j][1], lv[2 * j + 1][1]], [lt_k])
        k.red(l2[:, j:j + 1], lt[:, :], [lt_k], [l2_k])
    k.act(l2[:, :], l2[:, :], AF.Exp, [l2_k], [l2_k])
    k.tt("dve", lam[:, :], l2[:, 0:1], l2[:, 1:2], ALU.subtract, [l2_k], [lam_k])
    k.ts("dve", nlam[:, :], lam[:, :], lam_init, -1.0, ALU.add, ALU.mult, [lam_k], [nlam_k])
    k.ts("dve", sg[:, :], sg[:, :], 1.0 - lam_init, None, ALU.mult, None, [sg_k], [sg_k])
    bias, bias_k = P.sbuf("B_bias", [128, 1], F32)
    k.memset("dve", bias[:, :], SHIFT, [], [bias_k])
    mb, mb_k = P.sbuf("B_mb", [128, 3], F32)
    P.dma("sp", mb[:, :], io["mb"], [], [mb_k], mb_k)
    qT = [P.sbuf("B_qT%d" % i, [128, T], BF16) for i in range(2)]
    kT = [P.sbuf("B_kT%d" % i, [128, 2, T], BF16) for i in range(2)]
    va = [P.sbuf("B_va%d" % i, [128, 32, 257], BF16) for i in range(2)]
    for (v_, v_k) in va:
        k.memset("dve", v_[:, :, 256:257], 1.0, [], [v_k])
    o1 = [P.sbuf("B_o1_%d" % i, [128, 256], F32) for i in range(4)]
    oo = [P.sbuf("B_oo%d" % i, [128, 256], F32) for i in range(2)]
    sq = P.sbuf("B_sq", [128, 256], F32)
    rr = [P.sbuf("B_rr%d" % i, [128, 1], F32) for i in range(2)]
    ss = P.sbuf("B_ss", [128, 1], F32)
    rs = P.sbuf("B_rs", [128, 1], F32)
    ost = [P.sbuf("B_ost%d" % i, [128, 4, 256], BF16) for i in range(2)]
    O_k = Tk("O")
    state = {"oi": 0}

    def load_head(hc):
        hd, c = hc // 2, hc % 2
        v_, v_k = va[hd % 2]
        if c == 0:
            for rank in range(2):
                for i in range(4):
                    P.dma("sp", v_[:, rank * 16 + 4 * i:rank * 16 + 4 * i + 4, 0:256],
                          io["Vall"](rank, i)[:, hd * 256:(hd + 1) * 256].rearrange("(t p) e -> p t e", p=128),
                          io["in_deps"], [v_k], v_k)
        q_, q_k = qT[hc % 2]
        k_, k_k = kT[hc % 2]
        P.dma("sp", q_[:, :], io["QT"][hc, :, :], io["q_deps"], [q_k], q_k)
        P.dma("sp", k_[:, :, :], io["KTall"](hc).rearrange("r d t -> d r t"), io["in_deps"], [k_k], k_k)
        return (q_, q_k, k_, k_k), v_, v_k

    def score_mm(ps, ps_k, c0, rank, j, qb, rd):
        q_, q_k, k_, k_k = rd
        k.mm(ps[:, c0:512], k_[:, rank, j * 128:(j + 1) * 128], q_[:, qb * 512 + c0:(qb + 1) * 512], True, True,
             [k_k, q_k], [ps_k], mark=True)

    def finish_q(hc, qb, acc):
        hd, c = hc // 2, hc % 2
        for j in range(4):
            a_, a_k = acc[j]
            r_, r_k = rr[j % 2]
            k.recip(r_[:, :], a_[:, 256:257], [a_k], [r_k])
            o_, o_k = o1[j]
            if c == 0:
                k.ts("dve", io["_o1"][qb][0][:, j, :], a_[:, 0:256], r_[:, 0:1], None, ALU.mult, None, [a_k, r_k], [io["_o1"][qb][1]])
            else:
                os_, os_k = ost[state["oi"] % 2]
                k.tt("dve", r_[:, :], r_[:, :], nlam[:, :], ALU.mult, [r_k, nlam_k], [r_k])
                f_, f_k = oo[j % 2]
                k.stt(f_[:, :], a_[:, 0:256], r_[:, 0:1], io["_o1"][qb][0][:, j, :], ALU.mult, ALU.add,
                      [a_k, r_k, io["_o1"][qb][1]], [f_k])
                k.act(sq[0][:, :], f_[:, :], AF.Square, [f_k], [sq[1]])
                k.red(ss[0][:, :], sq[0][:, :], [sq[1]], [ss[1]])
                k.rstd(rs[0][:, :], ss[0][:, :], 1.0 / 256, rs[1], ss[1])
                k.stt(os_[:, j, :], f_[:, :], rs[0][:, 0:1], sg[:, :], ALU.mult, ALU.mult, [f_k, rs[1], sg_k], [os_k])
        if c == 1:
            os_, os_k = ost[state["oi"] % 2]
            state["oi"] += 1
            P.dma("sp", io["O"][qb * 512:(qb + 1) * 512, hd * 256:(hd + 1) * 256].rearrange("(j p) e -> p j e", p=128),
                  os_[:, :, :], [os_k], [O_k], os_k)

    io["_o1"] = [P.sbuf("B_o1q%d" % i, [128, 4, 256], F32) for i in range(4)]
    attn_core(k, io, 16, 256, load_head, score_mm, finish_q, mb, mb_k, bias, bias_k, scale)
    return {"O": O_k}


def phaseB2_mla(k, io):
    P = k.P
    scale = 192 ** -0.5
    bias, bias_k = P.sbuf("B_bias", [128, 1], F32)
    k.memset("dve", bias[:, :], SHIFT, [], [bias_k])
    mb, mb_k = P.sbuf("B_mb", [128, 3], F32)
    P.dma("sp", mb[:, :], io["mb"], [], [mb_k], mb_k)
    kp, kp_k = P.sbuf("B_kp", [64, 2, T], BF16)
    P.dma("sp", kp[:, :, :], io["KPall"].rearrange("r d t -> d r t"), io["in_deps"], [kp_k], kp_k)
    qn = [P.sbuf("B_qn%d" % i, [128, T], BF16) for i in range(2)]
    qp = [P.sbuf("B_qp%d" % i, [64, T], BF16) for i in range(2)]
    kn = [P.sbuf("B_kn%d" % i, [128, 2, T], BF16) for i in range(2)]
    va = [P.sbuf("B_va%d" % i, [128, 32, 129], BF16) for i in range(2)]
    for (v_, v_k) in va:
        k.memset("dve", v_[:, :, 128:129], 1.0, [], [v_k])
    rr = [P.sbuf("B_rr%d" % i, [128, 1], F32) for i in range(2)]
    ost = [P.sbuf("B_ost%d" % i, [128, 4, 128], BF16) for i in range(2)]
    O_k = Tk("O")
    state = {"oi": 0}

    def load_head(hd):
        v_, v_k = va[hd % 2]
        for rank in range(2):
            for i in range(4):
                P.dma("sp", v_[:, rank * 16 + 4 * i:rank * 16 + 4 * i + 4, 0:128],
                      io["Vall"](rank, i)[:, hd * 128:(hd + 1) * 128].rearrange("(t p) e -> p t e", p=128),
                      io["in_deps"], [v_k], v_k)
        q_, q_k = qn[hd % 2]
        qp_, qp_k = qp[hd % 2]
        k_, k_k = kn[hd % 2]
        P.dma("sp", q_[:, :], io["QNT"][hd, :, :], io["q_deps"], [q_k], q_k)
        P.dma("sp", qp_[:, :], io["QPT"][hd, :, :], io["q_deps"], [qp_k], qp_k)
        P.dma("sp", k_[:, :, :], io["KNall"](hd).rearrange("r d t -> d r t"), io["in_deps"], [k_k], k_k)
        return (q_, q_k, qp_, qp_k, k_, k_k), v_, v_k

    def score_mm(ps, ps_k, c0, rank, j, qb, rd):
        q_, q_k, qp_, qp_k, k_, k_k = rd
        k.mm(ps[:, c0:512], k_[:, rank, j * 128:(j + 1) * 128], q_[:, qb * 512 + c0:(qb + 1) * 512], True, False,
             [k_k, q_k], [ps_k], mark=False)
        k.mm(ps[:, c0:512], kp[:, rank, j * 128:(j + 1) * 128], qp_[:, qb * 512 + c0:(qb + 1) * 512], False, True,
             [kp_k, qp_k], [ps_k], mark=True)

    def finish_q(hd, qb, acc):
        os_, os_k = ost[state["oi"] % 2]
        state["oi"] += 1
        for j in range(4):
            a_, a_k = acc[j]
            r_, r_k = rr[j % 2]
            k.recip(r_[:, :], a_[:, 128:129], [a_k], [r_k])
            k.ts("dve", os_[:, j, :], a_[:, 0:128], r_[:, 0:1], None, ALU.mult, None, [a_k, r_k], [os_k])
        P.dma("sp", io["O"][qb * 512:(qb + 1) * 512, hd * 128:(hd + 1) * 128].rearrange("(j p) e -> p j e", p=128),
              os_[:, :, :], [os_k], [O_k], os_k)

    attn_core(k, io, 16, 128, load_head, score_mm, finish_q, mb, mb_k, bias, bias_k, scale)
    return {"O": O_k}


def phaseB2_lru(k, io):
    P, nc = k.P, k.nc
    cw, cw_k = P.sbuf("L_cw", [128, 16, 4], F32)
    P.dma("sp", cw[:, :, :], io["cw"], [], [cw_k], cw_k)
    sel, sel_k = P.sbuf("L_sel", [128, 2], F32)
    P.dma("sp", sel[:, :], io["sel"], [], [sel_k], sel_k)
    small = {}
    for n in ["cb", "ba", "bx", "lam"]:
        t_, t_k = P.sbuf("L_" + n, [128, 16], F32)
        P.dma("sp", t_[:, :], io[n], [], [t_k], t_k)
        small[n] = (t_, t_k)
    ce, ce_k = P.sbuf("L_ce", [128, 16], F32)
    k.act(ce[:, :], small["lam"][0][:, :], AF.Exp, [small["lam"][1]], [ce_k], scale=-1.0)
    k.act(ce[:, :], ce[:, :], AF.Ln, [ce_k], [ce_k], bias=1.0)
    k.ts("dve", ce[:, :], ce[:, :], -8.0, None, ALU.mult, None, [ce_k], [ce_k])
    xb = [P.sbuf("L_xb%d" % i, [128, S], F32) for i in range(2)]
    xc = [P.sbuf("L_xc%d" % i, [128, S], F32) for i in range(2)]
    xcb = [P.sbuf("L_xcb%d" % i, [128, S], BF16) for i in range(2)]
    rg = [P.sbuf("L_r%d" % i, [128, S], F32) for i in range(2)]
    ig = [P.sbuf("L_i%d" % i, [128, S], F32) for i in range(2)]
    tmp = P.sbuf("L_tmp", [128, S], F32)
    gt = [P.sbuf("L_g%d" % i, [128, S], BF16) for i in range(2)]
    wa = P.sbuf("L_wa", [128, 2, 256], BF16)
    wx = P.sbuf("L_wx", [128, 2, 256], BF16)
    Y_k = Tk("YT")
    il = lambda ap, rank: ap.rearrange("p (s r t) -> p s r t", r=2, t=128)[:, :, rank, :]
    for gb in range(8):
        if io.get("bg_hook") and gb > 0:
            pace = Tk("pace")
            pace.w = list(rg[0][1].w)
            io["bg_hook"]([pace])
        for (wt, nm) in ((wa, "wa"), (wx, "wx")):
            if isinstance(io[nm], PreCast):
                P.dma("act", wt[0][:, :, :], io[nm].ap[gb * 256:(gb + 1) * 256, :].rearrange("(c p) n -> p c n", p=128),
                      [io[nm].tk], [wt[1]], wt[1])
            else:
                P.dma("pool", wt[0][:, :, :], io[nm][gb].rearrange("(c p) n -> p c n", p=128), [], [wt[1]], wt[1])
        for ic in range(2):
            ch = gb * 2 + ic
            b_, b_k = xb[ic]
            c_, c_k = xc[ic]
            cb_, cb_k = xcb[ic]
            for rank in range(2):
                P.dma("sp", il(b_[:, :], rank), io["XBall"](rank, ch).rearrange("p (s t) -> p s t", t=128),
                      io["in_deps"], [b_k], b_k)
                P.dma("sp", il(gt[ic][0][:, :], rank), io["GTall"](rank, ch).rearrange("p (s t) -> p s t", t=128),
                      io["in_deps"], [gt[ic][1]], gt[ic][1])
            k.act(c_[:, :], b_[:, :], AF.Identity, [b_k, cw_k, small["cb"][1]], [c_k],
                  scale=cw[:, ch, 3:4], bias=small["cb"][0][:, ch:ch + 1])
            for sh in (1, 2, 3):
                k.stt(c_[:, sh:S], b_[:, 0:S - sh], cw[:, ch, 3 - sh:4 - sh], c_[:, sh:S], ALU.mult, ALU.add,
                      [b_k, cw_k, c_k], [c_k])
            k.copy("act", cb_[:, :], c_[:, :], [c_k], [cb_k])
        for oc in range(2):
            ch = gb * 2 + oc
            for (wt, gate, bname) in ((wa, rg, "ba"), (wx, ig, "bx")):
                g_, g_k = gate[oc]
                for tb in range(S // 512):
                    ps, ps_k = k.next_pf(0, 4)
                    for ic in range(2):
                        k.mm(ps[:, :], wt[0][:, ic, oc * 128:(oc + 1) * 128], xcb[ic][0][:, tb * 512:(tb + 1) * 512], ic == 0, ic == 1,
                             [wt[1], xcb[ic][1]], [ps_k], mark=(ic == 1))
                    k.act(g_[:, tb * 512:(tb + 1) * 512], ps[:, :], AF.Sigmoid, [ps_k, small[bname][1]], [g_k],
                          bias=small[bname][0][:, ch:ch + 1])
            r_, r_k = rg[oc]
            i_, i_k = ig[oc]
            c_, c_k = xc[oc]
            t_, t_k = tmp
            h_, h_k = xb[oc]
            k.act(r_[:, :], r_[:, :], AF.Exp, [r_k, ce_k], [r_k], scale=ce[:, ch:ch + 1])
            k.tt("dve", t_[:, :], r_[:, :], r_[:, :], ALU.mult, [r_k], [t_k])
            k.ts("dve", t_[:, :], t_[:, :], -1.0, 1.0, ALU.mult, ALU.add, [t_k], [t_k])
            k.act(t_[:, :], t_[:, :], AF.Sqrt, [t_k], [t_k])
            k.tt("dve", i_[:, :], i_[:, :], c_[:, :], ALU.mult, [i_k, c_k], [i_k])
            k.tt("dve", i_[:, :], i_[:, :], t_[:, :], ALU.mult, [i_k, t_k], [i_k])
            P.op("dve", lambda: nc.vector.tensor_tensor_scan(out=h_[:, :], data0=r_[:, :], data1=i_[:, :], initial=0.0,
                                                             op0=ALU.mult, op1=ALU.add), [r_k, i_k], [h_k])
            y_, y_k = gt[oc]
            k.tt("dve", t_[:, :], h_[:, :], y_[:, :], ALU.mult, [h_k, y_k], [t_k])
            own = y_[:, 0:T].rearrange("p (s t) -> p s t", t=128)
            k.ts("dve", own, il(t_[:, :], 0), sel[:, 0:1], None, ALU.mult, None, [t_k, sel_k], [y_k])
            k.stt(own, il(t_[:, :], 1), sel[:, 1:2], own, ALU.mult, ALU.add, [t_k, sel_k, y_k], [y_k])
            P.dma("sp", io["YT"][ch, :, :], y_[:, 0:T], [y_k], [Y_k], y_k)
    return {"YT": Y_k}


def tok_split(x_b, r):
    sh = x_b.shape
    return np.ascontiguousarray(x_b.reshape((NT, 2, 128) + sh[1:])[:, r].reshape((T,) + sh[1:]))


def feat_major(v, nch):
    return np.ascontiguousarray(np.asarray(v, np.float32).reshape(nch, 128).T)


LAYER_W = {
    0: [("a_w_in", "w_in"), ("a_q_norm", "qg"), ("a_k_norm", "kg"), ("a_lambda_q1", "lq1"), ("a_lambda_k1", "lk1"),
        ("a_lambda_q2", "lq2"), ("a_lambda_k2", "lk2"), ("a_sub_norm", "sg"), ("a_w_out", "w_out")],
    1: [("b_w_in", "w_in"), ("b_cq_norm", "cqg"), ("b_ckv_norm", "ckvg"), ("b_w_uq", "w_uq"), ("b_w_ukv", "w_ukv"),
        ("b_q_nope_norm", "qng"), ("b_q_rope_norm", "qrg"), ("b_k_nope_norm", "kng"), ("b_k_rope_norm", "krg"),
        ("b_w_out", "w_out")],
    2: [("c_w_in", "w_in"), ("c_w_a", "wa"), ("c_w_x", "wx"), ("c_w_out", "w_out")],
}
COMMON_W = [("norm_gain", "ng"), ("ple_norm", "pg"), ("ple_w_gate", "w_g"), ("ple_w_proj", "w_p")]
W_SHAPES = {"a_w_in": [D, 8192], "a_q_norm": [128], "a_k_norm": [128], "a_lambda_q1": [128], "a_lambda_k1": [128],
            "a_lambda_q2": [128], "a_lambda_k2": [128], "a_sub_norm": [256], "a_w_out": [D, D],
            "b_w_in": [D, 3136], "b_cq_norm": [512], "b_ckv_norm": [512], "b_w_uq": [512, 3072], "b_w_ukv": [512, 4096],
            "b_q_nope_norm": [128], "b_q_rope_norm": [64], "b_k_nope_norm": [128], "b_k_rope_norm": [64], "b_w_out": [D, D],
            "c_w_in": [D, 4096], "c_w_a": [8, 256, 256], "c_w_x": [8, 256, 256], "c_w_out": [D, D],
            "norm_gain": [D], "ple_norm": [D], "ple_w_gate": [D, D], "ple_w_proj": [256, D]}


def build_fused(nlayers=4, stop=None):
    L = Launch()
    P, k = L.P, L.k
    inp, scr = L.inp, L.scratch
    x = inp("x", [T, D], F32)
    p = inp("p", [nlayers, T, 256], F32)
    posT = inp("posT", [128, NT], I32)
    inv64 = inp("inv64", [64], F32)
    inv32 = inp("inv32", [32], F32)
    mb = inp("mb", [128, 3], F32)
    sel = inp("sel", [128, 2], F32)
    out = L.out("hout", [T, D], F32)
    hbuf = [scr("hA", [T, D], F32), scr("hB", [T, D], F32)]
    h1 = scr("h1", [T, D], F32)
    QT = scr("QT", [16, 128, T], BF16)
    QPT = scr("QPT", [16, 64, T], BF16)
    G = scr("G", [T, D], BF16)
    O = scr("O", [T, D], BF16)
    YT = scr("YT", [16, 128, T], BF16)
    h_in = x
    BIGW = ("w_in", "w_out", "w_g", "w_p", "w_uq", "w_ukv", "wa", "wx")
    WALL = {}
    casts = []
    for li in range(nlayers):
        kind = li % 3
        WALL[li] = {}
        for (src, nm) in LAYER_W[kind] + COMMON_W:
            ap = inp("L%d_%s" % (li, nm), W_SHAPES[src], F32)
            if nm in BIGW and not (li == 0 and nm == "w_in"):
                shp = W_SHAPES[src]
                if len(shp) == 3:
                    ap = ap.rearrange("g i j -> (g i) j")
                    shp = [shp[0] * shp[1], shp[2]]
                pc = PreCast(ap, scr("bf_L%d_%s" % (li, nm), shp, BF16), shp[0], shp[1])
                casts.append((li, nm, pc))
                ap = pc
            WALL[li][nm] = ap
    order = {"w_out": 2, "w_g": 2, "w_p": 2, "wa": 1, "wx": 1}
    casts.sort(key=lambda c: (c[0], order.get(c[1], 0)))
    cq = [(pc, i) for (_, _, pc) in casts for i in range(len(pc.pieces))]
    nmix = sum(1 for li in range(nlayers))
    state = {"pos": 0}

    def make_hook(n_hooks, upto):
        todo = max(0, upto - state["pos"])
        per = -(-todo // n_hooks) if n_hooks else 0

        def hook(pace):
            n = min(per, upto - state["pos"])
            for _ in range(max(0, n)):
                pc, i = cq[state["pos"]]
                state["pos"] += 1
                pc.emit_piece(P, i, pace)
        return hook

    def pieces_through(li_max):
        return sum(len(pc.pieces) for (l, _, pc) in casts if l <= li_max)

    for li in range(nlayers):
        kind, j = li % 3, li // 3
        Wl = WALL[li]
        h_out = out if li == nlayers - 1 else hbuf[li % 2]
        tgt = min(1 if li == 0 else nlayers - 1, nlayers - 1)
        upto = pieces_through(tgt)
        bg_hook = make_hook(16 if kind != 2 else 7, upto)
        P.begin_phase()
        k.setup_psum(6, 2)
        if kind == 0:
            send = scr("send%d" % li, [4096, 2048], BF16)
            recv = scr("recv%d" % li, [8192, 2048], BF16)
            r5 = recv.rearrange("(i r x) t -> i r x t", r=2, x=512)
            io = {"h": h_in, "ng": Wl["ng"], "w_in": Wl["w_in"], "qg": Wl["qg"], "kg": Wl["kg"],
                  "posT": posT, "inv64": inv64, "h_deps": [], "QT": QT,
                  "KT": send[0:2048, :].rearrange("(g d) t -> g d t", g=16), "V": send[2048:4096, :], "G": G}
            recv_k = Tk("recv")
            io["early"] = {
                9: lambda o: P.gather_chunks(send[0:2048, :], recv[0:4096, :], 2048, 512, PAIRS, [o["KT"]], [recv_k]),
                13: lambda o: P.gather_chunks(send[2048:4096, :], recv[4096:8192, :], 2048, 512, PAIRS, [o["V"]], [recv_k]),
            }
            outs = phaseA_diff(k, io)

        elif kind == 1:
            send = scr("send%d" % li, [4160, 2048], BF16)
            recv = scr("recv%d" % li, [8320, 2048], BF16)
            r5 = recv[0:8192, :].rearrange("(i r x) t -> i r x t", r=2, x=512)
            io = {"h": h_in, "ng": Wl["ng"], "w_in": Wl["w_in"], "cqg": Wl["cqg"], "ckvg": Wl["ckvg"],
                  "w_uq": Wl["w_uq"], "w_ukv": Wl["w_ukv"], "qng": Wl["qng"], "qrg": Wl["qrg"],
                  "kng": Wl["kng"], "krg": Wl["krg"], "posT": posT, "inv32": inv32, "h_deps": [],
                  "QNT": QT, "QPT": QPT, "KNT": send[0:2048, :].rearrange("(g d) t -> g d t", g=16),
                  "KPT": send[4096:4160, :], "V": send[2048:4096, :], "G": G}
            outs = phaseA_mla(k, io)
            P.gather_chunks(send, recv, 4160, 512, PAIRS, [outs["KNT"], outs["KPT"], outs["V"]], [Tk("recv")])
        else:
            sendx = scr("sendx%d" % li, [2048, T], F32)
            recvx = scr("recvx%d" % li, [4096, T], F32)
            send = scr("send%d" % li, [2048, T], BF16)
            recv = scr("recv%d" % li, [4096, T], BF16)
            io = {"h": h_in, "ng": Wl["ng"], "w_in": Wl["w_in"], "h_deps": [],
                  "XBT": sendx.rearrange("(g d) t -> g d t", g=16), "GT": send.rearrange("(g d) t -> g d t", g=16)}
            outs = phaseA_lru(k, io)
            P.gather_chunks(sendx, recvx, 2048, 256, PAIRS, [outs["XBT"]], [Tk("recvx")])
            P.gather_chunks(send, recv, 2048, 512, PAIRS, [outs["GT"]], [Tk("recv")])
            rx5 = recvx.rearrange("(i r x) t -> i r x t", r=2, x=256)
            r5 = recv.rearrange("(i r x) t -> i r x t", r=2, x=512)
        P.end_phase()
        if stop == "A":
            return L, []
        P.begin_phase()
        k.setup_psum(8, 0) if kind != 2 else k.setup_psum(6, 2)
        if kind == 0:
            io = {"QT": QT, "KTall": (lambda hc, r5=r5: r5[hc // 4, :, (hc % 4) * 128:(hc % 4 + 1) * 128, :]),
                  "Vall": (lambda rank, i, r5=r5: r5[4 + i, rank, :, :]), "mb": mb,
                  "lq1": Wl["lq1"], "lk1": Wl["lk1"], "lq2": Wl["lq2"], "lk2": Wl["lk2"],
                  "sg": Wl["sg"], "in_deps": [], "q_deps": [], "O": O, "bg_hook": bg_hook}
            phaseB2_diff(k, io, 0.8 - 0.6 * math.exp(-0.3 * li))
        elif kind == 1:
            io = {"QNT": QT, "QPT": QPT, "KNall": (lambda hc, r5=r5: r5[hc // 4, :, (hc % 4) * 128:(hc % 4 + 1) * 128, :]),
                  "KPall": recv[8192:8320, :].rearrange("(r d) t -> r d t", r=2),
                  "Vall": (lambda rank, i, r5=r5: r5[4 + i, rank, :, :]), "mb": mb, "in_deps": [], "q_deps": [], "O": O,
                  "bg_hook": bg_hook}
            phaseB2_mla(k, io)
        else:
            cw = inp("cw", [128, 16, 4], F32)
            small = {n: inp(n, [128, 16], F32) for n in ["cb", "ba", "bx", "lam"]}
            io = {"XBall": (lambda rank, ch, rx5=rx5: rx5[ch // 2, rank, (ch % 2) * 128:(ch % 2 + 1) * 128, :]),
                  "GTall": (lambda rank, ch, r5=r5: r5[ch // 4, rank, (ch % 4) * 128:(ch % 4 + 1) * 128, :]),
                  "cw": cw, "cb": small["cb"], "ba": small["ba"], "bx": small["bx"], "lam": small["lam"],
                  "wa": Wl["wa"], "wx": Wl["wx"], "sel": sel, "in_deps": [], "YT": YT, "bg_hook": bg_hook}
            phaseB2_lru(k, io)
        while state["pos"] < upto:
            pc, i = cq[state["pos"]]
            state["pos"] += 1
            pc.emit_piece(P, i, [])
        P.end_phase()
        if stop == "B":
            return L, []
        P.begin_phase()
        k.setup_psum(6, 2)
        io = {"h": h_in, "O": O, "G": G, "YT": YT, "w_out": Wl["w_out"], "pg": Wl["pg"], "w_g": Wl["w_g"],
              "w_p": Wl["w_p"], "p": p[li], "in_deps": [], "h_deps": [], "h1": h1, "hout": h_out}
        outs = phaseC(k, io, feat_major_y=(kind == 2))
        final = list(outs.values())
        P.end_phase()
        h_in = h_out
    return L, final


def fused_in_maps(I, nlayers=4):
    shared = {"inv64": INV64, "inv32": INV32}
    for li in range(nlayers):
        kind, j = li % 3, li // 3
        for (src, nm) in LAYER_W[kind]:
            shared["L%d_%s" % (li, nm)] = np.ascontiguousarray(I[src][j], dtype=np.float32)
        for (src, nm) in COMMON_W:
            shared["L%d_%s" % (li, nm)] = np.ascontiguousarray(I[src][li], dtype=np.float32)
        if kind == 2:
            shared["cw"] = np.ascontiguousarray(np.asarray(I["c_conv_w"][j], np.float32).reshape(4, 16, 128).transpose(2, 1, 0))
            shared["cb"] = feat_major(I["c_conv_b"][j], 16)
            shared["ba"] = feat_major(I["c_b_a"][j], 16)
            shared["bx"] = feat_major(I["c_b_x"][j], 16)
            shared["lam"] = feat_major(I["c_lambda"][j], 16)
    maps = []
    for c in range(NCORES):
        b, r = c // 2, c % 2
        m = dict(shared)
        m["x"] = tok_split(np.asarray(I["x"][b], np.float32), r)
        m["p"] = np.stack([tok_split(np.asarray(I["p"][l, b], np.float32), r) for l in range(nlayers)])
        m["posT"] = np.ascontiguousarray(tok_split(np.asarray(I["positions"][b]), r).reshape(NT, 128).T.astype(np.int32))
        mbv = np.full((128, 3), SHIFT, np.float32)
        if r == 0:
            mbv[64:, 0] = NEG
            mbv[:, 1] = NEG
            mbv[:, 2] = NEG
        else:
            mbv[64:, 1] = NEG
        m["mb"] = mbv
        m["sel"] = np.tile(np.array([[1.0 - r, float(r)]], np.float32), (128, 1))
        maps.append(m)
    return maps


def kernel(**inputs):
    I = {k_: np.asarray(v) for k_, v in inputs.items()}
    L, final = build_fused(4)
    res = L.run(fused_in_maps(I, 4), final)
    out = np.empty((4, S, D), np.float32)
    for c in range(NCORES):
        b, r = c // 2, c % 2
        out[b].reshape(NT, 2, 128, D)[:, r] = res[c]["hout"].reshape(NT, 128, D)
    return out
```

```python
import contextlib
import math
import numpy as np
import ml_dtypes
import concourse.bass as bass
import concourse.mybir as mybir
from concourse.bass_utils import run_bass_kernel_spmd

F32 = mybir.dt.float32
BF16 = mybir.dt.bfloat16
I32 = mybir.dt.int32
AF = mybir.ActivationFunctionType
ALU = mybir.AluOpType
AX = mybir.AxisListType

NCORES = 8
T = 2048
NT = 16
S = 4096
D = 2048
KC = 16
EPS = 1e-6
PI = math.pi
TWO_PI = 2.0 * math.pi


class Tk:
    __slots__ = ("w", "r", "name", "dsem", "dcnt")

    def __init__(self, name=""):
        self.w = []
        self.r = []
        self.name = name
        self.dsem = None
        self.dcnt = 0


class Prog:
    COMPUTE = ("pe", "act", "dve", "pool")

    def __init__(self, nc, stack):
        self.nc = nc
        self.stack = stack
        self.E = {"pe": nc.tensor, "act": nc.scalar, "dve": nc.vector,
                  "pool": nc.gpsimd, "sp": nc.sync}
        self.sems = {}
        self.cnt = {}
        for e in self.COMPUTE:
            self.sems[e] = stack.enter_context(nc.semaphore("c_" + e))
            self.cnt[e] = 0
        self.seen = {e: {} for e in self.E}
        self.ninst = 0
        self.uid = 0
        self.pstack = None
        self.sempool = []
        self.sempool_sw = []
        self.phase_tks = []
        self.phase_sem = stack.enter_context(nc.semaphore("phase"))
        self.phase_no = 0
        self.cc_sem = stack.enter_context(nc.semaphore("cc"))
        self.cc_cnt = 0

    def sbuf(self, name, shape, dtype):
        st = self.pstack if self.pstack is not None else self.stack
        self.uid += 1
        t = st.enter_context(self.nc.sbuf_tensor("%s_%d" % (name, self.uid), shape, dtype))
        return t, Tk(name)

    def begin_phase(self):
        self.pstack = contextlib.ExitStack()
        self.phase_tks = []

    def end_phase(self):
        toks = [(self.sems[e], self.cnt[e], e) for e in self.COMPUTE]
        toks.append((self.cc_sem, self.cc_cnt, "dma"))
        for tk in self.phase_tks:
            toks.append((tk.dsem, tk.dcnt, "dma"))
        self._waits("sp", toks)
        self.phase_no += 1
        self.nc.sync.sem_inc(self.phase_sem, 1)
        for e in self.COMPUTE:
            self.E[e].wait_ge(self.phase_sem, self.phase_no)
        self.ninst += 5
        for tk in self.phase_tks:
            (self.sempool_sw if tk.name == "sw" else self.sempool).append((tk.dsem, tk.dcnt))
            tk.dsem = None
        self.phase_tks = []
        if self.pstack is not None:
            self.pstack.close()
            self.pstack = None

    def gather_chunks(self, send, recv, nrows, rpc, groups, reads, writes):
        i, r0 = 0, 0
        while r0 < nrows:
            n = min(rpc, nrows - r0)
            self.collective("AllGather", send[r0:r0 + n, :], recv[2 * r0:2 * r0 + 2 * n, :], groups, reads, writes)
            r0 += n

    def collective(self, kind, src, dst, groups, reads, writes):
        self._waits("pool", self._deps("pool", reads, writes))
        ins = self.nc.gpsimd.collective_compute(kind, ALU.bypass, replica_groups=groups, ins=[src.opt()], outs=[dst.opt()])
        self.cc_cnt += 1
        ins.then_inc(self.cc_sem)
        self.ninst += 1
        self._record((self.cc_sem, self.cc_cnt, "dma"), reads, writes)

    def psum(self, name, shape, dtype):
        st = self.pstack if self.pstack is not None else self.stack
        self.uid += 1
        t = st.enter_context(self.nc.psum_tensor("%s_%d" % (name, self.uid), shape, dtype))
        return t, Tk(name)

    def newsem(self, name):
        self.nsems = getattr(self, "nsems", 0) + 1
        self.uid += 1
        return self.stack.enter_context(self.nc.semaphore("%s_%d" % (name, self.uid)))

    def _waits(self, e, toks):
        need = {}
        for (s, v, src) in toks:
            k = id(s)
            if k not in need or need[k][1] < v:
                need[k] = (s, v)
        seen = self.seen[e]
        for k, (s, v) in need.items():
            if seen.get(k, 0) >= v:
                continue
            self.E[e].wait_ge(s, v)
            self.ninst += 1
            seen[k] = v

    def _deps(self, e, reads, writes):
        toks = []
        for t in reads:
            toks += t.w
        for t in writes:
            toks += t.w
            for tok in t.r:
                toks.append(tok)
        if e == "pe":
            toks = [t for t in toks if t[2] != "pe"]
        return toks

    def _record(self, tok, reads, writes):
        for t in reads:
            t.r.append(tok)
            if len(t.r) > 48:
                best = {}
                for (s, v, src) in t.r:
                    k = id(s)
                    if k not in best or best[k][1] < v:
                        best[k] = (s, v, src)
                t.r = list(best.values())
        for t in writes:
            t.w = [tok]
            t.r = []

    def op(self, e, fn, reads=(), writes=(), mark=True):
        self._waits(e, self._deps(e, reads, writes))
        ins = fn()
        self.ninst += 1
        if mark:
            self.cnt[e] += 1
            ins.then_inc(self.sems[e], 1)
            tok = (self.sems[e], self.cnt[e], e)
        else:
            tok = (self.sems[e], self.cnt[e] + 1, e)
        self._record(tok, reads, writes)
        return ins

    def dma(self, q, out, in_, reads, writes, semtk, nobarrier=False, **kw):
        if semtk.dsem is None:
            if nobarrier:
                semtk.dsem, semtk.dcnt = self.newsem("bg"), 0
            else:
                pool_ = self.sempool_sw if q == "pool" else self.sempool
                if pool_:
                    semtk.dsem, semtk.dcnt = pool_.pop()
                else:
                    semtk.dsem, semtk.dcnt = self.newsem("d"), 0
                semtk.name = "sw" if q == "pool" else "hw"
                self.phase_tks.append(semtk)
        toks = self._deps(q, reads, writes)
        if semtk.dcnt > 0:
            toks.append((semtk.dsem, semtk.dcnt, "dma"))
        self._waits(q, toks)
        ins = self.E[q].dma_start(out=out, in_=in_, **kw)
        self.ninst += 1
        semtk.dcnt += 16
        ins.then_inc(semtk.dsem, 16)
        tok = (semtk.dsem, semtk.dcnt, "dma")
        self._record(tok, reads, writes)
        return ins

    def finish(self, tks):
        toks = []
        for t in tks:
            toks += t.w
            toks += t.r
        self._waits("sp", toks)


class K:
    def __init__(self, nc, P, ident_dram):
        self.nc = nc
        self.P = P
        self.ident, self.ident_k = P.sbuf("ident_sb", [128, 128], BF16)
        P.dma("pool", self.ident[:, :], ident_dram, [], [self.ident_k], self.ident_k)
        self.pf, self.pb = [], []
        self.pfi = 0
        self.pbi = 0
        self.rr = 0
        self.dq = []

    def setup_psum(self, nf32=6, nbf16=2):
        P = self.P
        self.pf = [P.psum("pf%d" % i, [128, 512], F32) for i in range(nf32)]
        self.pb = [P.psum("pb%d" % i, [128, 1024], BF16) for i in range(nbf16)]

    def defer(self, fn):
        self.dq.append(fn)

    def run_deferred(self, lag):
        while len(self.dq) > lag:
            self.dq.pop(0)()

    def flush(self):
        self.run_deferred(0)

    def next_pf(self, lo=0, hi=6):
        n = hi - lo
        i = lo + (self.pfi % n)
        self.pfi += 1
        return self.pf[i]

    def next_pb(self):
        i = self.pbi % 2
        self.pbi += 1
        return self.pb[i]

    def act(self, out, in_, func, reads, writes, **kw):
        nc = self.nc
        return self.P.op("act", lambda: nc.scalar.activation(out=out, in_=in_, func=func, **kw), reads, writes)

    def tt(self, eng, out, in0, in1, op, reads, writes):
        nc = self.nc
        E = nc.vector if eng == "dve" else nc.gpsimd
        return self.P.op(eng, lambda: E.tensor_tensor(out=out, in0=in0, in1=in1, op=op), reads, writes)

    def ts(self, eng, out, in0, s1, s2, op0, op1, reads, writes):
        nc = self.nc
        E = nc.vector if eng == "dve" else nc.gpsimd
        if op1 is None:
            return self.P.op(eng, lambda: E.tensor_scalar(out=out, in0=in0, scalar1=s1, scalar2=None, op0=op0), reads, writes)
        return self.P.op(eng, lambda: E.tensor_scalar(out=out, in0=in0, scalar1=s1, scalar2=s2, op0=op0, op1=op1), reads, writes)

    def stt(self, out, in0, scalar, in1, op0, op1, reads, writes):
        nc = self.nc
        return self.P.op("dve", lambda: nc.vector.scalar_tensor_tensor(out=out, in0=in0, scalar=scalar, in1=in1, op0=op0, op1=op1), reads, writes)

    def red(self, out, in_, reads, writes):
        nc = self.nc
        return self.P.op("dve", lambda: nc.vector.tensor_reduce(out=out, in_=in_, axis=AX.X, op=ALU.add), reads, writes)

    def recip(self, out, in_, reads, writes):
        nc = self.nc
        return self.P.op("dve", lambda: nc.vector.reciprocal(out=out, in_=in_), reads, writes)

    def copy(self, eng, out, in_, reads, writes):
        nc = self.nc
        if eng == "act":
            return self.P.op("act", lambda: nc.scalar.activation(out=out, in_=in_, func=AF.Copy), reads, writes)
        E = nc.vector if eng == "dve" else nc.gpsimd
        return self.P.op(eng, lambda: E.tensor_copy(out=out, in_=in_), reads, writes)

    def memset(self, eng, ap, val, reads, writes):
        nc = self.nc
        E = nc.vector if eng == "dve" else nc.gpsimd
        return self.P.op(eng, lambda: E.memset(ap, val), reads, writes)

    def mm(self, out, lhsT, rhs, start, stop, reads, writes, mark):
        nc = self.nc
        return self.P.op("pe", lambda: nc.tensor.matmul(out, lhsT=lhsT, rhs=rhs, start=start, stop=stop), reads, writes, mark=mark)

    def tr(self, out, in_, reads, writes, mark):
        nc = self.nc
        ident = self.ident
        return self.P.op("pe", lambda: nc.tensor.transpose(out=out, in_=in_, identity=ident[:, :]),
                         list(reads) + [self.ident_k], writes, mark=mark)

    def rstd(self, out, ss, scale, out_k, ss_k):
        self.ts("dve", ss, ss, scale, EPS, ALU.mult, ALU.add, [ss_k], [ss_k])
        self.act(ss, ss, AF.Sqrt, [ss_k], [ss_k])
        self.recip(out, ss, [ss_k], [out_k])


def bcast_load(k, name, vec_dram, n, q="sp"):
    t, tk = k.P.sbuf(name, [128, n], F32)
    k.P.dma(q, t[:, :], vec_dram.partition_broadcast(128), [], [tk], tk)
    return t, tk


class NormT:
    def __init__(self, k, name):
        P = k.P
        self.k = k
        self.hb = [P.sbuf("%s_hb%d" % (name, i), [128, D], F32) for i in range(2)]
        self.sq = P.sbuf(name + "_sq", [128, D], BF16)
        self.ss = P.sbuf(name + "_ss", [128, 1], F32)
        self.rs = P.sbuf(name + "_rs", [128, 1], F32)
        self.xn = [P.sbuf("%s_xn%d" % (name, i), [128, D], BF16) for i in range(2)]

    def run(self, h_dram, h_deps, gain_bc, gain_k, xT, xT_ks, ntiles=NT, tile0=0):
        k, P = self.k, self.k.P
        for t in range(ntiles):
            hb, hb_k = self.hb[t % 2]
            xn, xn_k = self.xn[t % 2]
            sq, sq_k = self.sq
            ss, ss_k = self.ss
            rs, rs_k = self.rs
            P.dma("sp", hb[:, :], h_dram[t * 128:(t + 1) * 128, :], h_deps(t), [hb_k], hb_k)
            k.act(sq[:, :], hb[:, :], AF.Square, [hb_k], [sq_k])
            k.red(ss[:, 0:1], sq[:, :], [sq_k], [ss_k])
            k.rstd(rs[:, 0:1], ss[:, 0:1], 1.0 / D, rs_k, ss_k)
            k.stt(xn[:, :], hb[:, :], rs[:, 0:1], gain_bc[:, :], ALU.mult, ALU.mult, [hb_k, rs_k, gain_k], [xn_k])
            tt = tile0 + t
            for half in range(2):
                pb, pb_k = k.next_pb()
                for j in range(8):
                    c = half * 8 + j
                    k.tr(pb[:, j * 128:(j + 1) * 128], xn[:, c * 128:(c + 1) * 128], [xn_k], [pb_k], mark=(j == 7))
                k.copy("act" if half == 0 else "dve",
                       xT[:, half * 8:(half + 1) * 8, tt * 128:(tt + 1) * 128],
                       pb[:, :].rearrange("p (j q) -> p j q", j=8), [pb_k], [xT_ks[tt]])


class PreCast:
    def __init__(self, src, ap, rows, cols):
        self.src, self.ap, self.tk = src, ap, Tk("precast")
        self.sem_tk = Tk("precast_sem")
        rpp = max(128, (8 << 20) // (cols * 4) // 128 * 128)
        self.pieces = [(r0, min(rows, r0 + rpp)) for r0 in range(0, rows, rpp)]

    def emit_piece(self, P, i, pace):
        r0, r1 = self.pieces[i]
        P.dma("pool", self.ap[r0:r1, :], self.src[r0:r1, :], pace, [self.tk], self.sem_tk, nobarrier=True)


class WStream:
    def __init__(self, k, name, kcmax, nbuf=2):
        self.k = k
        self.buf = [k.P.sbuf("%s_w%d" % (name, i), [128, kcmax, 512], BF16) for i in range(nbuf)]
        self.i = 0

    def load(self, W_dram, kc, n0, nsz):
        w, w_k = self.buf[self.i % len(self.buf)]
        self.i += 1
        if isinstance(W_dram, PreCast):
            self.k.P.dma("act", w[:, 0:kc, 0:nsz],
                         W_dram.ap[:, n0:n0 + nsz].rearrange("(c p) n -> p c n", p=128), [W_dram.tk], [w_k], w_k)
        else:
            self.k.P.dma("pool", w[:, 0:kc, 0:nsz],
                         W_dram[:, n0:n0 + nsz].rearrange("(c p) n -> p c n", p=128), [], [w_k], w_k)
        return w, w_k

    def stream(self, jobs):
        nxt = self.load(*jobs[0])
        for i in range(len(jobs)):
            cur = nxt
            if i + 1 < len(jobs):
                nxt = self.load(*jobs[i + 1])
            yield i, cur[0], cur[1]


def linear_tok(k, xT, xT_ks, kc, w, w_k, nsz, tiles, evac, pf_lo=0, pf_hi=3):
    for t in tiles:
        ps, ps_k = k.next_pf(pf_lo, pf_hi)
        for c in range(kc):
            k.mm(ps[:, 0:nsz], xT[:, c, t * 128:(t + 1) * 128], w[:, c, 0:nsz], c == 0, c == kc - 1,
                 [xT_ks[t], w_k], [ps_k], mark=(c == kc - 1))
        evac(t, ps, ps_k)
        k.run_deferred(3)


class Rope:
    def __init__(self, k, name, posT_dram, inv_dram, ntiles, half):
        P, nc = k.P, k.nc
        n = ntiles * half
        self.cos, self.cos_k = P.sbuf(name + "_cos", [128, ntiles, half], F32)
        self.sin, self.sin_k = P.sbuf(name + "_sin", [128, ntiles, half], F32)
        pi_, pi_k = P.sbuf(name + "_pi", [128, ntiles], I32)
        pf, pf_k = P.sbuf(name + "_pf", [128, ntiles], F32)
        inv, inv_k = bcast_load(k, name + "_inv", inv_dram, half)
        ang, ang_k = P.sbuf(name + "_ang", [128, ntiles, half], F32)
        ki, ki_k = P.sbuf(name + "_ki", [128, n], I32)
        kf, kf_k = P.sbuf(name + "_kf", [128, n], F32)
        m, m_k = P.sbuf(name + "_m", [128, n], F32)
        P.dma("sp", pi_[:, :], posT_dram, [], [pi_k], pi_k)
        k.copy("dve", pf[:, :], pi_[:, :], [pi_k], [pf_k])
        for t in range(ntiles):
            k.ts("dve", ang[:, t, :], inv[:, :], pf[:, t:t + 1], None, ALU.mult, None, [inv_k, pf_k], [ang_k])
        angf = ang[:, :, :].rearrange("p t h -> p (t h)")
        for which, (dst, dst_k) in enumerate([(self.sin, self.sin_k), (self.cos, self.cos_k)]):
            if which == 0:
                src, src_k = angf, ang_k
            else:
                k.ts("dve", angf, angf, PI / 2.0, None, ALU.add, None, [ang_k], [ang_k])
                src, src_k = angf, ang_k
            k.ts("dve", kf[:, :], src, 1.0 / TWO_PI, None, ALU.mult, None, [src_k], [kf_k])
            k.copy("dve", ki[:, :], kf[:, :], [kf_k], [ki_k])
            k.copy("dve", kf[:, :], ki[:, :], [ki_k], [kf_k])
            k.stt(m[:, :], kf[:, :], -TWO_PI, src, ALU.mult, ALU.add, [kf_k, src_k], [m_k])
            k.ts("dve", kf[:, :], m[:, :], PI, TWO_PI, ALU.is_gt, ALU.mult, [m_k], [kf_k])
            k.tt("dve", m[:, :], m[:, :], kf[:, :], ALU.subtract, [m_k, kf_k], [m_k])
            k.ts("dve", kf[:, :], m[:, :], -PI, TWO_PI, ALU.is_lt, ALU.mult, [m_k], [kf_k])
            k.tt("dve", m[:, :], m[:, :], kf[:, :], ALU.add, [m_k, kf_k], [m_k])
            k.ts("dve", m[:, :], m[:, :], 3.1415925, -3.1415925, ALU.min, ALU.max, [m_k], [m_k])
            k.act(dst[:, :, :].rearrange("p t h -> p (t h)"), m[:, :], AF.Sin, [m_k], [dst_k])


class QKProc:
    def __init__(self, k, name, ngrp, dn, with_rope=True, nob=6):
        P = k.P
        self.k = k
        self.ngrp, self.dn = ngrp, dn
        w = ngrp * dn
        self.sq = [P.sbuf("%s_sq%d" % (name, i), [128, w], F32) for i in range(2)]
        self.ss = [P.sbuf("%s_ss%d" % (name, i), [128, ngrp], F32) for i in range(2)]
        self.rs = [P.sbuf("%s_rs%d" % (name, i), [128, ngrp], F32) for i in range(2)]
        self.tq = [P.sbuf("%s_tq%d" % (name, i), [128, w], F32) for i in range(2)]
        self.ra = [P.sbuf("%s_ra%d" % (name, i), [128, w], F32) if with_rope else (None, None) for i in range(2)]
        self.nob = nob
        self.ob = [P.sbuf("%s_ob%d" % (name, i), [128, w], BF16) for i in range(nob)]
        self.i = 0

    def run(self, src3, src_k, gain, gain_k, rope, t, do_rope=True, ngrp=None):
        k = self.k
        g, dn = (ngrp or self.ngrp), self.dn
        w = g * dn
        i = self.i % 2
        self.i += 1
        sq, sq_k = self.sq[i]
        ss, ss_k = self.ss[i]
        rs, rs_k = self.rs[i]
        tq, tq_k = self.tq[i]
        ra, ra_k = self.ra[i]
        ob, ob_k = self.ob[(self.i - 1) % self.nob]
        v3 = lambda tl: tl[:, 0:w].rearrange("p (g d) -> p g d", g=g)
        k.act(v3(sq), src3, AF.Square, [src_k], [sq_k])
        k.red(ss[:, 0:g], v3(sq), [sq_k], [ss_k])
        k.rstd(rs[:, 0:g], ss[:, 0:g], 1.0 / dn, rs_k, ss_k)
        k.tt("dve", v3(tq), src3, rs[:, 0:g].unsqueeze(2).to_broadcast([128, g, dn]), ALU.mult,
             [src_k, rs_k], [tq_k])
        gb = gain[:, :].unsqueeze(1).to_broadcast([128, g, dn])
        if not do_rope:
            k.tt("dve", v3(ob), v3(tq), gb, ALU.mult, [tq_k, gain_k], [ob_k])
            return ob, ob_k
        k.tt("dve", v3(tq), v3(tq), gb, ALU.mult, [tq_k, gain_k], [tq_k])
        h = dn // 2
        cosb = rope.cos[:, t, :].unsqueeze(1).to_broadcast([128, g, h])
        sinb = rope.sin[:, t, :].unsqueeze(1).to_broadcast([128, g, h])
        t3, r3, o3 = v3(tq), v3(ra), v3(ob)
        x1, x2 = t3[:, :, 0:h], t3[:, :, h:dn]
        k.tt("dve", r3[:, :, 0:h], x1, cosb, ALU.mult, [tq_k, rope.cos_k], [ra_k])
        k.tt("dve", r3[:, :, h:dn], x2, sinb, ALU.mult, [tq_k, rope.sin_k], [ra_k])
        k.tt("dve", o3[:, :, 0:h], r3[:, :, 0:h], r3[:, :, h:dn], ALU.subtract, [ra_k], [ob_k])
        k.tt("dve", r3[:, :, 0:h], x2, cosb, ALU.mult, [tq_k, rope.cos_k], [ra_k])
        k.tt("dve", r3[:, :, h:dn], x1, sinb, ALU.mult, [tq_k, rope.sin_k], [ra_k])
        k.tt("dve", o3[:, :, h:dn], r3[:, :, 0:h], r3[:, :, h:dn], ALU.add, [ra_k], [ob_k])
        return ob, ob_k


def phaseA_diff(k, io):
    P = k.P
    gain, gain_k = bcast_load(k, "A_ng", io["ng"], D)
    qg, qg_k = bcast_load(k, "A_qg", io["qg"], 128)
    kg, kg_k = bcast_load(k, "A_kg", io["kg"], 128)
    rope = Rope(k, "A_rp", io["posT"], io["inv64"], NT, 64)
    xT, _ = P.sbuf("A_xT", [128, KC, T], BF16)
    xT_ks = [Tk("xT%d" % t) for t in range(NT)]
    nt = NormT(k, "A_n")
    nt.run(io["h"], lambda t: io["h_deps"], gain, gain_k, xT, xT_ks)
    ws = WStream(k, "A", KC)
    qk = QKProc(k, "A_qk", 4, 128)
    stb = [(nt.hb[i][0][:, :].bitcast(BF16), nt.hb[i][1]) for i in range(2)]
    outs = {"QT": Tk("QT"), "KT": Tk("KT"), "V": Tk("V"), "G": Tk("G")}
    si = 0
    for nb, w, w_k in ws.stream([(io["w_in"], KC, nb * 512, 512) for nb in range(16)]):
        kind = nb // 4
        for hf in range(2):
            stf, st_k = stb[si % 2]
            si += 1
            if kind < 2:
                st = stf.rearrange("p (g t) -> p g t", g=4)
                gn, gn_k = (qg, qg_k) if kind == 0 else (kg, kg_k)

                def evac(t, ps, ps_k, st=st, st_k=st_k, gn=gn, gn_k=gn_k, hf=hf):
                    ob, ob_k = qk.run(ps[:, :].rearrange("p (g d) -> p g d", g=4), ps_k, gn, gn_k, rope, t)
                    tl = t - hf * 8
                    transpose_to(k, ob, ob_k, 4, 128, st[:, :, tl * 128:(tl + 1) * 128], st_k)
                linear_tok(k, xT, xT_ks, KC, w, w_k, 512, range(hf * 8, hf * 8 + 8), evac)
                nm = "QT" if kind == 0 else "KT"
                g0 = (nb % 4) * 4

                def out_dma(nm=nm, g0=g0, hf=hf, st=st, st_k=st_k):
                    P.dma("sp", io[nm][g0:g0 + 4, :, hf * 1024:(hf + 1) * 1024].rearrange("g d t -> d g t"), st, [st_k],
                          [outs[nm]], st_k)
                k.defer(out_dma)
            else:
                st = stf.rearrange("p (t n) -> p t n", t=8)
                func = AF.Copy if kind == 2 else AF.Silu

                def evac(t, ps, ps_k, st=st, st_k=st_k, func=func, hf=hf):
                    k.act(st[:, t - hf * 8, :], ps[:, :], func, [ps_k], [st_k])
                linear_tok(k, xT, xT_ks, KC, w, w_k, 512, range(hf * 8, hf * 8 + 8), evac)
                nm = "V" if kind == 2 else "G"
                c0 = (nb % 4) * 512
                k.flush()
                P.dma("sp", io[nm][hf * 1024:(hf + 1) * 1024, c0:c0 + 512].rearrange("(t p) n -> p t n", p=128), st, [st_k],
                      [outs[nm]], st_k)
    k.flush()
    return outs


def phaseB_diff(k, io, lam_init):
    P, nc = k.P, k.nc
    NTS = S // 128
    scale = 128 ** -0.5
    SHIFT = -8.0
    lv = [bcast_load(k, "B_lv%d" % i, io[n], 128) for i, n in enumerate(["lq1", "lk1", "lq2", "lk2"])]
    sg, sg_k = bcast_load(k, "B_sg", io["sg"], 256)
    lt, lt_k = P.sbuf("B_lt", [128, 128], F32)
    l2, l2_k = P.sbuf("B_l2", [128, 2], F32)
    lam, lam_k = P.sbuf("B_lam", [128, 1], F32)
    nlam, nlam_k = P.sbuf("B_nlam", [128, 1], F32)
    for j in range(2):
        k.tt("dve", lt[:, :], lv[2 * j][0][:, :], lv[2 * j + 1][0][:, :], ALU.mult, [lv[2 * j][1], lv[2 * j + 1][1]], [lt_k])
        k.red(l2[:, j:j + 1], lt[:, :], [lt_k], [l2_k])
    k.act(l2[:, :], l2[:, :], AF.Exp, [l2_k], [l2_k])
    k.tt("dve", lam[:, :], l2[:, 0:1], l2[:, 1:2], ALU.subtract, [l2_k], [lam_k])
    k.ts("dve", nlam[:, :], lam[:, :], lam_init, -1.0, ALU.add, ALU.mult, [lam_k], [nlam_k])
    k.ts("dve", sg[:, :], sg[:, :], 1.0 - lam_init, None, ALU.mult, None, [sg_k], [sg_k])
    bias, bias_k = P.sbuf("B_bias", [128, 1], F32)
    k.memset("dve", bias[:, :], SHIFT, [], [bias_k])

    qT = [P.sbuf("B_qT%d" % i, [128, S], BF16) for i in range(2)]
    kT = [P.sbuf("B_kT%d" % i, [128, S], BF16) for i in range(2)]
    va = [P.sbuf("B_va%d" % i, [128, NTS, 257], BF16) for i in range(2)]
    for (v_, v_k) in va:
        k.memset("dve", v_[:, :, 256:257], 1.0, [], [v_k])
    E = [P.sbuf("B_E%d" % i, [128, 512], BF16) for i in range(3)]
    o1 = [P.sbuf("B_o1_%d" % i, [128, 256], F32) for i in range(4)]
    oo = [P.sbuf("B_oo%d" % i, [128, 256], F32) for i in range(2)]
    sq = P.sbuf("B_sq", [128, 256], F32)
    rr = [P.sbuf("B_rr%d" % i, [128, 1], F32) for i in range(2)]
    ss = P.sbuf("B_ss", [128, 1], F32)
    rs = P.sbuf("B_rs", [128, 1], F32)
    ost = [P.sbuf("B_ost%d" % i, [128, 4, 256], BF16) for i in range(2)]
    O_k = Tk("O")
    ei = 0
    for hd in range(4):
        v_, v_k = va[hd % 2]
        P.dma("sp", v_[:, :, 0:256], io["V"][:, hd * 256:(hd + 1) * 256].rearrange("(t p) e -> p t e", p=128),
              io["in_deps"], [v_k], v_k)
        for qb in range(S // 512):
            os_, os_k = ost[qb % 2]
            for c in range(2):
                if qb == 0:
                    q_, q_k = qT[c]
                    k_, k_k = kT[c]
                    P.dma("sp", q_[:, :], io["QT"][hd * 2 + c, :, :], io["in_deps"], [q_k], q_k)
                    P.dma("sp", k_[:, :], io["KT"][hd * 2 + c, :, :], io["in_deps"], [k_k], k_k)
                q_, q_k = qT[c]
                k_, k_k = kT[c]
                acc = [k.pf[j] for j in range(4)]
                nkt = 4 * (qb + 1)
                for kt in range(nkt):
                    j0 = max(0, kt - 4 * qb)
                    c0 = j0 * 128
                    ps, ps_k = k.next_pf(4, 6)
                    k.mm(ps[:, c0:512], k_[:, kt * 128:(kt + 1) * 128], q_[:, qb * 512 + c0:(qb + 1) * 512], True, True,
                         [k_k, q_k], [ps_k], mark=True)
                    e_, e_k = E[ei % 3]
                    ei += 1
                    k.act(e_[:, c0:512], ps[:, c0:512], AF.Exp, [ps_k, bias_k], [e_k], scale=scale, bias=bias[:, 0:1])
                    if kt >= 4 * qb:
                        k.memset("dve", e_[64:128, c0:c0 + 64], 0.0, [], [e_k])
                    for j in range(j0, 4):
                        a_, a_k = acc[j]
                        k.mm(a_[:, 0:257], e_[:, j * 128:(j + 1) * 128], v_[:, kt, 0:257], kt == 0, kt == 4 * qb + j,
                             [e_k, v_k], [a_k], mark=(kt == 4 * qb + j))
                for j in range(4):
                    a_, a_k = acc[j]
                    r_, r_k = rr[j % 2]
                    k.recip(r_[:, :], a_[:, 256:257], [a_k], [r_k])
                    o_, o_k = o1[j]
                    if c == 0:
                        k.ts("dve", o_[:, :], a_[:, 0:256], r_[:, 0:1], None, ALU.mult, None, [a_k, r_k], [o_k])
                    else:
                        k.tt("dve", r_[:, :], r_[:, :], nlam[:, :], ALU.mult, [r_k, nlam_k], [r_k])
                        f_, f_k = oo[j % 2]
                        k.stt(f_[:, :], a_[:, 0:256], r_[:, 0:1], o_[:, :], ALU.mult, ALU.add, [a_k, r_k, o_k], [f_k])
                        k.act(sq[0][:, :], f_[:, :], AF.Square, [f_k], [sq[1]])
                        k.red(ss[0][:, :], sq[0][:, :], [sq[1]], [ss[1]])
                        k.rstd(rs[0][:, :], ss[0][:, :], 1.0 / 256, rs[1], ss[1])
                        k.stt(os_[:, j, :], f_[:, :], rs[0][:, 0:1], sg[:, :], ALU.mult, ALU.mult, [f_k, rs[1], sg_k], [os_k])
            P.dma("sp", io["O"][qb * 512:(qb + 1) * 512, hd * 256:(hd + 1) * 256].rearrange("(j p) e -> p j e", p=128),
                  os_[:, :, :], [os_k], [O_k], os_k)
    return {"O": O_k}


def phaseC(k, io, feat_major_y=False):
    P = k.P
    pg, pg_k = bcast_load(k, "C_pg", io["pg"], D)
    yT, _ = P.sbuf("C_yT", [128, KC, T], BF16)
    yT_ks = [Tk("yT%d" % t) for t in range(NT)]
    ws = WStream(k, "C", KC)
    nt = NormT(k, "C_n")
    if not feat_major_y:
        for t in range(NT):
            hbf = nt.hb[t % 2][0][:, :].bitcast(BF16)
            b_k = nt.hb[t % 2][1]
            o_, g_ = hbf[:, 0:D], hbf[:, D:2 * D]
            P.dma("sp", o_, io["O"][t * 128:(t + 1) * 128, :], io["in_deps"], [b_k], b_k)
            P.dma("sp", g_, io["G"][t * 128:(t + 1) * 128, :], io["in_deps"], [b_k], b_k)
            k.tt("dve", o_, o_, g_, ALU.mult, [b_k], [b_k])
            for half in range(2):
                pb, pb_k = k.next_pb()
                for j in range(8):
                    c = half * 8 + j
                    k.tr(pb[:, j * 128:(j + 1) * 128], o_[:, c * 128:(c + 1) * 128], [b_k], [pb_k], mark=(j == 7))
                k.copy("act" if half == 0 else "dve", yT[:, half * 8:(half + 1) * 8, t * 128:(t + 1) * 128],
                       pb[:, :].rearrange("p (j q) -> p j q", j=8), [pb_k], [yT_ks[t]])
    else:
        for c in range(KC):
            P.dma("sp", yT[:, c, :], io["YT"][c, :, :], io["in_deps"], yT_ks, yT_ks[c])
    slab = [P.sbuf("C_sl%d" % i, [128, 4, 512], F32) for i in range(3)]
    h1_ks = {}

    def slab_load(g, src, deps):
        nb, qt = (g // 4) % 4, g % 4
        sl, sl_k = slab[g % 3]
        P.dma("sp", sl[:, :, :],
              src[qt * 512:qt * 512 + 512, nb * 512:(nb + 1) * 512].rearrange("(t p) n -> p t n", p=128),
              deps(nb, qt), [sl_k], sl_k)

    slab_load(0, io["h"], lambda nb, qt: io["h_deps"])
    for nb, w, w_k in ws.stream([(io["w_out"], KC, nb * 512, 512) for nb in range(4)]):
        for qt in range(4):
            g = nb * 4 + qt
            if g + 1 < 16:
                slab_load(g + 1, io["h"], lambda nb, qt: io["h_deps"])
            sl, sl_k = slab[g % 3]
            r0 = qt * 512

            def evac(t, ps, ps_k, sl=sl, sl_k=sl_k, qt=qt):
                k.tt("dve", sl[:, t - qt * 4, :], sl[:, t - qt * 4, :], ps[:, :], ALU.add, [sl_k, ps_k], [sl_k])
            linear_tok(k, yT, yT_ks, KC, w, w_k, 512, range(qt * 4, qt * 4 + 4), evac)
            hk = Tk("h1_%d_%d" % (nb, qt))
            h1_ks[(nb, qt)] = hk
            P.dma("sp", io["h1"][r0:r0 + 512, nb * 512:(nb + 1) * 512].rearrange("(t p) n -> p t n", p=128),
                  sl[:, :, :], [sl_k], [hk], sl_k)
    nt.run(io["h1"], lambda t: [h1_ks[(nb, t // 4)] for nb in range(4)], pg, pg_k, yT, yT_ks)
    xT, xT_ks = yT, yT_ks
    pT, _ = P.sbuf("C_pT", [128, 2, T], BF16)
    pT_ks = [Tk("pT%d" % t) for t in range(NT)]
    pl = [P.sbuf("C_pl%d" % i, [128, 256], F32) for i in range(2)]
    pc = [P.sbuf("C_pc%d" % i, [128, 256], BF16) for i in range(2)]
    for t in range(NT):
        a_, a_k = pl[t % 2]
        b_, b_k = pc[t % 2]
        P.dma("sp", a_[:, :], io["p"][t * 128:(t + 1) * 128, :], [], [a_k], a_k)
        k.copy("dve", b_[:, :], a_[:, :], [a_k], [b_k])
        pb, pb_k = k.next_pb()
        for j in range(2):
            k.tr(pb[:, j * 128:(j + 1) * 128], b_[:, j * 128:(j + 1) * 128], [b_k], [pb_k], mark=(j == 1))
        k.copy("act", pT[:, :, t * 128:(t + 1) * 128], pb[:, 0:256].rearrange("p (j q) -> p j q", j=2), [pb_k], [pT_ks[t]])
    wp = WStream(k, "Cp", 2)
    gs = [P.sbuf("C_gs%d" % i, [128, 512], F32) for i in range(2)]
    out_k = Tk("hout")
    gi = 0
    slab_load(16, io["h1"], lambda nb, qt: [h1_ks[(nb % 4, qt)]])
    for nb, w, w_k in ws.stream([(io["w_g"], KC, nb * 512, 512) for nb in range(4)]):
        w2, w2_k = wp.load(io["w_p"], 2, nb * 512, 512)
        for qt in range(4):
            g = 16 + nb * 4 + qt
            if g + 1 < 32:
                slab_load(g + 1, io["h1"], lambda nb, qt: [h1_ks[(nb % 4, qt)]])
            sl, sl_k = slab[g % 3]
            r0 = qt * 512
            for t in range(qt * 4, qt * 4 + 4):
                ps, ps_k = k.next_pf(0, 3)
                for c in range(KC):
                    k.mm(ps[:, :], xT[:, c, t * 128:(t + 1) * 128], w[:, c, :], c == 0, c == KC - 1,
                         [xT_ks[t], w_k], [ps_k], mark=(c == KC - 1))
                ps2, ps2_k = k.next_pf(3, 6)
                for c in range(2):
                    k.mm(ps2[:, :], pT[:, c, t * 128:(t + 1) * 128], w2[:, c, :], c == 0, c == 1,
                         [pT_ks[t], w2_k], [ps2_k], mark=(c == 1))
                g_, g_k = gs[gi % 2]
                gi += 1
                k.act(g_[:, :], ps[:, :], AF.Sigmoid, [ps_k], [g_k])
                k.tt("dve", g_[:, :], g_[:, :], ps2[:, :], ALU.mult, [g_k, ps2_k], [g_k])
                k.tt("dve", sl[:, t - qt * 4, :], sl[:, t - qt * 4, :], g_[:, :], ALU.add, [sl_k, g_k], [sl_k])
            P.dma("sp", io["hout"][r0:r0 + 512, nb * 512:(nb + 1) * 512].rearrange("(t p) n -> p t n", p=128),
                  sl[:, :, :], [sl_k], [out_k], sl_k)
    return {"hout": out_k}


def transpose_to(k, ob, ob_k, ngrp, dn, dst3, dst_k):
    def go():
        pb, pb_k = k.next_pb()
        for g in range(ngrp):
            k.tr(pb[0:dn, g * 128:(g + 1) * 128], ob[:, g * dn:(g + 1) * dn], [ob_k], [pb_k], mark=(g == ngrp - 1))
        k.copy("act", dst3, pb[0:dn, 0:ngrp * 128].rearrange("p (g q) -> p g q", g=ngrp), [pb_k], [dst_k])
    k.defer(go)


def phaseA_mla(k, io):
    P = k.P
    gain, gain_k = bcast_load(k, "M_ng", io["ng"], D)
    cqg, cqg_k = bcast_load(k, "M_cqg", io["cqg"], 512)
    ckvg, ckvg_k = bcast_load(k, "M_ckvg", io["ckvg"], 512)
    qng, qng_k = bcast_load(k, "M_qng", io["qng"], 128)
    qrg, qrg_k = bcast_load(k, "M_qrg", io["qrg"], 64)
    kng, kng_k = bcast_load(k, "M_kng", io["kng"], 128)
    krg, krg_k = bcast_load(k, "M_krg", io["krg"], 64)
    rope = Rope(k, "M_rp", io["posT"], io["inv32"], NT, 32)
    xT, _ = P.sbuf("M_xT", [128, KC, T], BF16)
    xT_ks = [Tk("xT%d" % t) for t in range(NT)]
    nt = NormT(k, "M_n")
    nt.run(io["h"], lambda t: io["h_deps"], gain, gain_k, xT, xT_ks)
    ws = WStream(k, "M", KC)
    lat = QKProc(k, "M_lat", 1, 512, with_rope=False, nob=4)
    p128 = QKProc(k, "M_p128", 2, 128, with_rope=False)
    p64 = QKProc(k, "M_p64", 2, 64)
    cqT, _ = P.sbuf("M_cqT", [128, 4, T], BF16)
    ckvT, _ = P.sbuf("M_ckvT", [128, 4, T], BF16)
    cq_ks = [Tk("cq%d" % t) for t in range(NT)]
    ckv_ks = [Tk("ckv%d" % t) for t in range(NT)]
    stb = [(nt.hb[i][0][:, :].bitcast(BF16), nt.hb[i][1]) for i in range(2)]
    stc = [(nt.sq[0][:, :], nt.sq[1])]
    outs = {n: Tk(n) for n in ["QNT", "QPT", "KNT", "KPT", "V", "G"]}
    si = 0
    jobs = [(io["w_in"], KC, 0, 512), (io["w_in"], KC, 512, 512), (io["w_in"], KC, 1024, 64)] + \
           [(io["w_in"], KC, 1088 + i * 512, 512) for i in range(4)]
    for jb, w, w_k in ws.stream(jobs):
        if jb < 2:
            dstT, dks = (cqT, cq_ks) if jb == 0 else (ckvT, ckv_ks)
            gn, gn_k = (cqg, cqg_k) if jb == 0 else (ckvg, ckvg_k)

            def evac(t, ps, ps_k, dstT=dstT, dks=dks, gn=gn, gn_k=gn_k):
                ob, ob_k = lat.run(ps[:, :].rearrange("p (g d) -> p g d", g=1), ps_k, gn, gn_k, None, t, do_rope=False)
                transpose_to(k, ob, ob_k, 4, 128, dstT[:, :, t * 128:(t + 1) * 128], dks[t])
            linear_tok(k, xT, xT_ks, KC, w, w_k, 512, range(NT), evac)
        elif jb == 2:
            kp, kp_k = stc[0]

            def evac(t, ps, ps_k, kp=kp, kp_k=kp_k):
                ob, ob_k = p64.run(ps[:, 0:64].rearrange("p (g d) -> p g d", g=1), ps_k, krg, krg_k, rope, t, ngrp=1)
                transpose_to(k, ob, ob_k, 1, 64, kp[0:64, t * 128:(t + 1) * 128].rearrange("p (g q) -> p g q", g=1), kp_k)
            linear_tok(k, xT, xT_ks, KC, w, w_k, 64, range(NT), evac)
            k.flush()
            P.dma("sp", io["KPT"], kp[0:64, :], [kp_k], [outs["KPT"]], kp_k)
        else:
            for hf in range(2):
                stf, st_k = stb[si % 2]
                si += 1
                st = stf.rearrange("p (t n) -> p t n", t=8)

                def evac(t, ps, ps_k, st=st, st_k=st_k, hf=hf):
                    k.act(st[:, t - hf * 8, :], ps[:, :], AF.Silu, [ps_k], [st_k])
                linear_tok(k, xT, xT_ks, KC, w, w_k, 512, range(hf * 8, hf * 8 + 8), evac)
                k.flush()
                c0 = (jb - 3) * 512
                P.dma("sp", io["G"][hf * 1024:(hf + 1) * 1024, c0:c0 + 512].rearrange("(t p) n -> p t n", p=128), st, [st_k],
                      [outs["G"]], st_k)
    k.flush()
    ws4 = ws
    for jb, w, w_k in ws4.stream([(io["w_uq"], 4, i * 384, 384) for i in range(8)]):
        for hf in range(2):
            stf, st_k = stb[si % 2]
            si += 1
            stn = stf[:, 0:2048].rearrange("p (g t) -> p g t", g=2)
            stp = stf[:, 2048:4096].rearrange("p (g t) -> p g t", g=2)

            def evac(t, ps, ps_k, stn=stn, stp=stp, st_k=st_k, hf=hf):
                v = ps[:, 0:384].rearrange("p (g d) -> p g d", g=2)
                tl = t - hf * 8
                ob, ob_k = p128.run(v[:, :, 0:128], ps_k, qng, qng_k, None, t, do_rope=False)
                transpose_to(k, ob, ob_k, 2, 128, stn[:, :, tl * 128:(tl + 1) * 128], st_k)
                ob, ob_k = p64.run(v[:, :, 128:192], ps_k, qrg, qrg_k, rope, t)
                transpose_to(k, ob, ob_k, 2, 64, stp[0:64, :, tl * 128:(tl + 1) * 128], st_k)
            linear_tok(k, cqT, cq_ks, 4, w, w_k, 384, range(hf * 8, hf * 8 + 8), evac)

            def out_dma(jb=jb, hf=hf, stn=stn, stp=stp, st_k=st_k):
                P.dma("sp", io["QNT"][2 * jb:2 * jb + 2, :, hf * 1024:(hf + 1) * 1024].rearrange("g d t -> d g t"), stn, [st_k],
                      [outs["QNT"]], st_k)
                P.dma("sp", io["QPT"][2 * jb:2 * jb + 2, :, hf * 1024:(hf + 1) * 1024].rearrange("g d t -> d g t"), stp[0:64, :, :], [st_k],
                      [outs["QPT"]], st_k)
            k.defer(out_dma)
    for jb, w, w_k in ws4.stream([(io["w_ukv"], 4, i * 512, 512) for i in range(8)]):
        for hf in range(2):
            stf, st_k = stb[si % 2]
            si += 1
            stn = stf[:, 0:2048].rearrange("p (g t) -> p g t", g=2)
            stv = stf[:, 2048:4096].rearrange("p (t g e) -> p t g e", t=8, g=2)

            def evac(t, ps, ps_k, stn=stn, stv=stv, st_k=st_k, hf=hf):
                v = ps[:, :].rearrange("p (g d) -> p g d", g=2)
                tl = t - hf * 8
                ob, ob_k = p128.run(v[:, :, 0:128], ps_k, kng, kng_k, None, t, do_rope=False)
                transpose_to(k, ob, ob_k, 2, 128, stn[:, :, tl * 128:(tl + 1) * 128], st_k)
                k.act(stv[:, tl, :, :], v[:, :, 128:256], AF.Copy, [ps_k], [st_k])
            linear_tok(k, ckvT, ckv_ks, 4, w, w_k, 512, range(hf * 8, hf * 8 + 8), evac)

            def out_dma(jb=jb, hf=hf, stn=stn, stv=stv, st_k=st_k):
                P.dma("sp", io["KNT"][2 * jb:2 * jb + 2, :, hf * 1024:(hf + 1) * 1024].rearrange("g d t -> d g t"), stn, [st_k],
                      [outs["KNT"]], st_k)
                P.dma("sp", io["V"][hf * 1024:(hf + 1) * 1024, jb * 256:(jb + 1) * 256].rearrange("(t p) (g e) -> p t g e", p=128, g=2),
                      stv, [st_k], [outs["V"]], st_k)
            k.defer(out_dma)
    k.flush()
    return outs


def phaseB_mla(k, io):
    P = k.P
    NTS = S // 128
    scale = 192 ** -0.5
    SHIFT = -8.0
    bias, bias_k = P.sbuf("B_bias", [128, 1], F32)
    k.memset("dve", bias[:, :], SHIFT, [], [bias_k])
    kp, kp_k = P.sbuf("B_kp", [64, S], BF16)
    P.dma("sp", kp[:, :], io["KPT"], io["in_deps"], [kp_k], kp_k)
    qn = [P.sbuf("B_qn%d" % i, [128, S], BF16) for i in range(2)]
    qp = [P.sbuf("B_qp%d" % i, [64, S], BF16) for i in range(2)]
    kn = [P.sbuf("B_kn%d" % i, [128, S], BF16) for i in range(2)]
    va = [P.sbuf("B_va%d" % i, [128, NTS, 129], BF16) for i in range(2)]
    for (v_, v_k) in va:
        k.memset("dve", v_[:, :, 128:129], 1.0, [], [v_k])
    E = [P.sbuf("B_E%d" % i, [128, 512], BF16) for i in range(3)]
    rr = [P.sbuf("B_rr%d" % i, [128, 1], F32) for i in range(2)]
    ost = [P.sbuf("B_ost%d" % i, [128, 4, 128], BF16) for i in range(2)]
    O_k = Tk("O")
    ei = 0
    oi = 0
    for hd in range(8):
        v_, v_k = va[hd % 2]
        q_, q_k = qn[hd % 2]
        qp_, qp_k = qp[hd % 2]
        k_, k_k = kn[hd % 2]
        P.dma("sp", v_[:, :, 0:128], io["V"][:, hd * 128:(hd + 1) * 128].rearrange("(t p) e -> p t e", p=128),
              io["in_deps"], [v_k], v_k)
        P.dma("sp", q_[:, :], io["QNT"][hd, :, :], io["in_deps"], [q_k], q_k)
        P.dma("sp", qp_[:, :], io["QPT"][hd, :, :], io["in_deps"], [qp_k], qp_k)
        P.dma("sp", k_[:, :], io["KNT"][hd, :, :], io["in_deps"], [k_k], k_k)
        for qb in range(S // 512):
            os_, os_k = ost[oi % 2]
            oi += 1
            acc = [k.pf[j] for j in range(4)]
            nkt = 4 * (qb + 1)
            for kt in range(nkt):
                j0 = max(0, kt - 4 * qb)
                c0 = j0 * 128
                ps, ps_k = k.next_pf(4, 6)
                k.mm(ps[:, c0:512], k_[:, kt * 128:(kt + 1) * 128], q_[:, qb * 512 + c0:(qb + 1) * 512], True, False,
                     [k_k, q_k], [ps_k], mark=False)
                k.mm(ps[:, c0:512], kp[:, kt * 128:(kt + 1) * 128], qp_[:, qb * 512 + c0:(qb + 1) * 512], False, True,
                     [kp_k, qp_k], [ps_k], mark=True)
                e_, e_k = E[ei % 3]
                ei += 1
                k.act(e_[:, c0:512], ps[:, c0:512], AF.Exp, [ps_k, bias_k], [e_k], scale=scale, bias=bias[:, 0:1])
                if kt >= 4 * qb:
                    k.memset("dve", e_[64:128, c0:c0 + 64], 0.0, [], [e_k])
                for j in range(j0, 4):
                    a_, a_k = acc[j]
                    k.mm(a_[:, 0:129], e_[:, j * 128:(j + 1) * 128], v_[:, kt, 0:129], kt == 0, kt == 4 * qb + j,
                         [e_k, v_k], [a_k], mark=(kt == 4 * qb + j))
            for j in range(4):
                a_, a_k = acc[j]
                r_, r_k = rr[j % 2]
                k.recip(r_[:, :], a_[:, 128:129], [a_k], [r_k])
                k.ts("dve", os_[:, j, :], a_[:, 0:128], r_[:, 0:1], None, ALU.mult, None, [a_k, r_k], [os_k])
            P.dma("sp", io["O"][qb * 512:(qb + 1) * 512, hd * 128:(hd + 1) * 128].rearrange("(j p) e -> p j e", p=128),
                  os_[:, :, :], [os_k], [O_k], os_k)
    return {"O": O_k}


def phaseA_lru(k, io):
    P = k.P
    gain, gain_k = bcast_load(k, "R_ng", io["ng"], D)
    xT, _ = P.sbuf("R_xT", [128, KC, T], BF16)
    xT_ks = [Tk("xT%d" % t) for t in range(NT)]
    nt = NormT(k, "R_n")
    nt.run(io["h"], lambda t: io["h_deps"], gain, gain_k, xT, xT_ks)
    ws = WStream(k, "R", KC)
    stx = [P.sbuf("R_stx%d" % i, [128, T], F32) for i in range(2)]
    stg = [(nt.hb[i][0][:, :].bitcast(BF16)[:, 0:T], nt.hb[i][1]) for i in range(2)]
    outs = {"XBT": Tk("XBT"), "GT": Tk("GT")}
    si = 0
    for nb, w, w_k in ws.stream([(io["w_in"], KC, nb * 512, 512) for nb in range(8)]):
        for fl in range(4):
            fc = (nb % 4) * 4 + fl
            if nb < 4:
                st, st_k = stx[si % 2]
                st = st[:, :]
            else:
                st, st_k = stg[si % 2]
            si += 1
            for tb in range(4):
                ps, ps_k = k.next_pf(0, 3)
                for c in range(KC):
                    k.mm(ps[:, :], w[:, c, fl * 128:(fl + 1) * 128], xT[:, c, tb * 512:(tb + 1) * 512], c == 0, c == KC - 1,
                         [w_k] + xT_ks[tb * 4:tb * 4 + 4], [ps_k], mark=(c == KC - 1))
                k.act(st[:, tb * 512:(tb + 1) * 512], ps[:, :], AF.Copy if nb < 4 else AF.Silu, [ps_k], [st_k])
            nm = "XBT" if nb < 4 else "GT"
            P.dma("sp", io[nm][fc, :, :], st, [st_k], [outs[nm]], st_k)
    return outs


def phaseB_lru(k, io):
    P, nc = k.P, k.nc
    cw, cw_k = P.sbuf("L_cw", [128, 8, 4], F32)
    P.dma("sp", cw[:, :, :], io["cw"], [], [cw_k], cw_k)
    small = {}
    for n in ["cb", "ba", "bx", "lam"]:
        t_, t_k = P.sbuf("L_" + n, [128, 8], F32)
        P.dma("sp", t_[:, :], io[n], [], [t_k], t_k)
        small[n] = (t_, t_k)
    ce, ce_k = P.sbuf("L_ce", [128, 8], F32)
    k.act(ce[:, :], small["lam"][0][:, :], AF.Exp, [small["lam"][1]], [ce_k], scale=-1.0)
    k.act(ce[:, :], ce[:, :], AF.Ln, [ce_k], [ce_k], bias=1.0)
    k.ts("dve", ce[:, :], ce[:, :], -8.0, None, ALU.mult, None, [ce_k], [ce_k])
    xb = [P.sbuf("L_xb%d" % i, [128, S], F32) for i in range(2)]
    xc = [P.sbuf("L_xc%d" % i, [128, S], F32) for i in range(2)]
    xcb = [P.sbuf("L_xcb%d" % i, [128, S], BF16) for i in range(2)]
    rg = [P.sbuf("L_r%d" % i, [128, S], F32) for i in range(2)]
    ig = [P.sbuf("L_i%d" % i, [128, S], F32) for i in range(2)]
    tmp = P.sbuf("L_tmp", [128, S], F32)
    gt = [P.sbuf("L_g%d" % i, [128, S], BF16) for i in range(2)]
    wa = P.sbuf("L_wa", [128, 2, 256], BF16)
    wx = P.sbuf("L_wx", [128, 2, 256], BF16)
    Y_k = Tk("YT")
    for gb in range(4):
        P.dma("pool", wa[0][:, :, :], io["wa"][gb].rearrange("(c p) n -> p c n", p=128), [], [wa[1]], wa[1])
        P.dma("pool", wx[0][:, :, :], io["wx"][gb].rearrange("(c p) n -> p c n", p=128), [], [wx[1]], wx[1])
        for ic in range(2):
            ch = gb * 2 + ic
            b_, b_k = xb[ic]
            c_, c_k = xc[ic]
            cb_, cb_k = xcb[ic]
            P.dma("sp", b_[:, :], io["XBT"][ch, :, :], io["in_deps"], [b_k], b_k)
            P.dma("sp", gt[ic][0][:, :], io["GT"][ch, :, :], io["in_deps"], [gt[ic][1]], gt[ic][1])
            k.act(c_[:, :], b_[:, :], AF.Identity, [b_k, cw_k, small["cb"][1]], [c_k],
                  scale=cw[:, ch, 3:4], bias=small["cb"][0][:, ch:ch + 1])
            for sh in (1, 2, 3):
                k.stt(c_[:, sh:S], b_[:, 0:S - sh], cw[:, ch, 3 - sh:4 - sh], c_[:, sh:S], ALU.mult, ALU.add,
                      [b_k, cw_k, c_k], [c_k])
            k.copy("act", cb_[:, :], c_[:, :], [c_k], [cb_k])
        for oc in range(2):
            ch = gb * 2 + oc
            for (wt, gate, bname) in ((wa, rg, "ba"), (wx, ig, "bx")):
                g_, g_k = gate[oc]
                for tb in range(S // 512):
                    ps, ps_k = k.next_pf(0, 4)
                    for ic in range(2):
                        k.mm(ps[:, :], wt[0][:, ic, oc * 128:(oc + 1) * 128], xcb[ic][0][:, tb * 512:(tb + 1) * 512], ic == 0, ic == 1,
                             [wt[1], xcb[ic][1]], [ps_k], mark=(ic == 1))
                    k.act(g_[:, tb * 512:(tb + 1) * 512], ps[:, :], AF.Sigmoid, [ps_k, small[bname][1]], [g_k],
                          bias=small[bname][0][:, ch:ch + 1])
            r_, r_k = rg[oc]
            i_, i_k = ig[oc]
            c_, c_k = xc[oc]
            t_, t_k = tmp
            h_, h_k = xb[oc]
            k.act(r_[:, :], r_[:, :], AF.Exp, [r_k, ce_k], [r_k], scale=ce[:, ch:ch + 1])
            k.tt("dve", t_[:, :], r_[:, :], r_[:, :], ALU.mult, [r_k], [t_k])
            k.ts("dve", t_[:, :], t_[:, :], -1.0, 1.0, ALU.mult, ALU.add, [t_k], [t_k])
            k.act(t_[:, :], t_[:, :], AF.Sqrt, [t_k], [t_k])
            k.tt("dve", i_[:, :], i_[:, :], c_[:, :], ALU.mult, [i_k, c_k], [i_k])
            k.tt("dve", i_[:, :], i_[:, :], t_[:, :], ALU.mult, [i_k, t_k], [i_k])
            P.op("dve", lambda: nc.vector.tensor_tensor_scan(out=h_[:, :], data0=r_[:, :], data1=i_[:, :], initial=0.0,
                                                             op0=ALU.mult, op1=ALU.add), [r_k, i_k], [h_k])
            y_, y_k = gt[oc]
            k.tt("dve", y_[:, :], h_[:, :], y_[:, :], ALU.mult, [h_k, y_k], [y_k])
            P.dma("sp", io["YT"][ch, :, :], y_[:, :], [y_k], [Y_k], y_k)
    return {"YT": Y_k}


IDENT = np.eye(128, dtype=np.float32)
INV64 = (10000.0 ** (-np.arange(64, dtype=np.float32) / np.float32(64))).astype(np.float32)
INV32 = (10000.0 ** (-np.arange(32, dtype=np.float32) / np.float32(32))).astype(np.float32)
BF = ml_dtypes.bfloat16


class Launch:
    def __init__(self):
        self.nc = bass.Bass("TRN2", target_bir_lowering=False)
        self.stack = contextlib.ExitStack()
        self.P = Prog(self.nc, self.stack)
        self.ident_d = self.inp("ident", [128, 128], F32)
        self.k = K(self.nc, self.P, self.ident_d)
        self.outs = []

    def inp(self, name, shape, dt):
        return self.nc.dram_tensor(name, list(shape), dt, kind="ExternalInput").ap()

    def out(self, name, shape, dt):
        self.outs.append(name)
        return self.nc.dram_tensor(name, list(shape), dt, kind="ExternalOutput").ap()

    def scratch(self, name, shape, dt):
        return self.nc.dram_tensor(name, list(shape), dt, kind="Internal").ap()

    def run(self, in_maps, final_tks):
        self.P.finish(final_tks)
        self.stack.close()
        for m in in_maps:
            m["ident"] = IDENT
        res = run_bass_kernel_spmd(self.nc, in_maps, core_ids=list(range(NCORES)))
        return res.results


def posT_of(positions, c):
    b, hf = c // 2, c % 2
    return np.ascontiguousarray(positions[b, hf * T:(hf + 1) * T].reshape(NT, 128).T.astype(np.int32))


def launch_A_diff(h_cores, positions, ng, w_in, qg, kg):
    L = Launch()
    io = {"h": L.inp("h", [T, D], F32), "ng": L.inp("ng", [D], F32), "w_in": L.inp("w_in", [D, 8192], F32),
          "qg": L.inp("qg", [128], F32), "kg": L.inp("kg", [128], F32), "posT": L.inp("posT", [128, NT], I32),
          "inv64": L.inp("inv64", [64], F32), "h_deps": [],
          "QT": L.out("QT", [16, 128, T], BF16), "KT": L.out("KT", [16, 128, T], BF16),
          "V": L.out("V", [T, 2048], BF16), "G": L.out("G", [T, 2048], BF16)}
    outs = phaseA_diff(L.k, io)
    in_maps = [{"h": h_cores[c], "ng": ng, "w_in": w_in, "qg": qg, "kg": kg, "posT": posT_of(positions, c),
                "inv64": INV64} for c in range(NCORES)]
    return L.run(in_maps, list(outs.values()))


def launch_B_diff(resA, lq1, lk1, lq2, lk2, sg, lam_init):
    L = Launch()
    io = {"QT": L.inp("QT", [8, 128, S], BF16), "KT": L.inp("KT", [8, 128, S], BF16), "V": L.inp("V", [S, 1024], BF16),
          "lq1": L.inp("lq1", [128], F32), "lk1": L.inp("lk1", [128], F32), "lq2": L.inp("lq2", [128], F32),
          "lk2": L.inp("lk2", [128], F32), "sg": L.inp("sg", [256], F32), "in_deps": [],
          "O": L.out("O", [S, 1024], BF16)}
    outs = phaseB_diff(L.k, io, lam_init)
    in_maps = []
    for c in range(NCORES):
        b, g = c // 2, c % 2
        r0, r1 = resA[2 * b], resA[2 * b + 1]
        in_maps.append({
            "QT": np.concatenate([r0["QT"][g * 8:(g + 1) * 8], r1["QT"][g * 8:(g + 1) * 8]], axis=2),
            "KT": np.concatenate([r0["KT"][g * 8:(g + 1) * 8], r1["KT"][g * 8:(g + 1) * 8]], axis=2),
            "V": np.concatenate([r0["V"][:, g * 1024:(g + 1) * 1024], r1["V"][:, g * 1024:(g + 1) * 1024]], axis=0),
            "lq1": lq1, "lk1": lk1, "lq2": lq2, "lk2": lk2, "sg": sg})
    return L.run(in_maps, list(outs.values()))


def gather_O(resB, width):
    out = []
    for c in range(NCORES):
        b, hf = c // 2, c % 2
        out.append(np.concatenate([resB[2 * b]["O"][hf * T:(hf + 1) * T], resB[2 * b + 1]["O"][hf * T:(hf + 1) * T]], axis=1))
    return out


def launch_C(h_cores, O_cores, G_cores, w_out, pg, w_g, w_p, p_l):
    L = Launch()
    io = {"h": L.inp("h", [T, D], F32), "O": L.inp("O", [T, D], BF16), "G": L.inp("G", [T, D], BF16),
          "w_out": L.inp("w_out", [D, D], F32), "pg": L.inp("pg", [D], F32), "w_g": L.inp("w_g", [D, D], F32),
          "w_p": L.inp("w_p", [256, D], F32), "p": L.inp("p", [T, 256], F32), "in_deps": [], "h_deps": [],
          "h1": L.scratch("h1", [T, D], F32), "hout": L.out("hout", [T, D], F32)}
    outs = phaseC(L.k, io)
    in_maps = []
    for c in range(NCORES):
        b, hf = c // 2, c % 2
        in_maps.append({"h": h_cores[c], "O": O_cores[c], "G": G_cores[c], "w_out": w_out, "pg": pg, "w_g": w_g,
                        "w_p": w_p, "p": np.ascontiguousarray(p_l[b, hf * T:(hf + 1) * T])})
    return L.run(in_maps, list(outs.values()))


def split_cores(x):
    return [np.ascontiguousarray(x[c // 2, (c % 2) * T:(c % 2 + 1) * T]) for c in range(NCORES)]


def layer_diff(h_cores, positions, li, j, I):
    lam_init = 0.8 - 0.6 * math.exp(-0.3 * li)
    rA = launch_A_diff(h_cores, positions, I["norm_gain"][li], I["a_w_in"][j], I["a_q_norm"][j], I["a_k_norm"][j])
    rB = launch_B_diff(rA, I["a_lambda_q1"][j], I["a_lambda_k1"][j], I["a_lambda_q2"][j], I["a_lambda_k2"][j],
                       I["a_sub_norm"][j], lam_init)
    O = gather_O(rB, 1024)
    rC = launch_C(h_cores, O, [r["G"] for r in rA], I["a_w_out"][j], I["ple_norm"][li], I["ple_w_gate"][li],
                  I["ple_w_proj"][li], I["p"][li])
    return [r["hout"] for r in rC], (rA, rB, rC)


def launch_A_mla(h_cores, positions, I, li, j):
    L = Launch()
    io = {"h": L.inp("h", [T, D], F32), "ng": L.inp("ng", [D], F32), "w_in": L.inp("w_in", [D, 3136], F32),
          "cqg": L.inp("cqg", [512], F32), "ckvg": L.inp("ckvg", [512], F32),
          "w_uq": L.inp("w_uq", [512, 3072], F32), "w_ukv": L.inp("w_ukv", [512, 4096], F32),
          "qng": L.inp("qng", [128], F32), "qrg": L.inp("qrg", [64], F32), "kng": L.inp("kng", [128], F32),
          "krg": L.inp("krg", [64], F32), "posT": L.inp("posT", [128, NT], I32), "inv32": L.inp("inv32", [32], F32),
          "h_deps": [],
          "QNT": L.out("QNT", [16, 128, T], BF16), "QPT": L.out("QPT", [16, 64, T], BF16),
          "KNT": L.out("KNT", [16, 128, T], BF16), "KPT": L.out("KPT", [64, T], BF16),
          "V": L.out("V", [T, 2048], BF16), "G": L.out("G", [T, 2048], BF16)}
    outs = phaseA_mla(L.k, io)
    in_maps = [{"h": h_cores[c], "ng": I["norm_gain"][li], "w_in": I["b_w_in"][j], "cqg": I["b_cq_norm"][j],
                "ckvg": I["b_ckv_norm"][j], "w_uq": I["b_w_uq"][j], "w_ukv": I["b_w_ukv"][j],
                "qng": I["b_q_nope_norm"][j], "qrg": I["b_q_rope_norm"][j], "kng": I["b_k_nope_norm"][j],
                "krg": I["b_k_rope_norm"][j], "posT": posT_of(positions, c), "inv32": INV32} for c in range(NCORES)]
    return L.run(in_maps, list(outs.values()))


def launch_B_mla(resA):
    L = Launch()
    io = {"QNT": L.inp("QNT", [8, 128, S], BF16), "QPT": L.inp("QPT", [8, 64, S], BF16),
          "KNT": L.inp("KNT", [8, 128, S], BF16), "KPT": L.inp("KPT", [64, S], BF16),
          "V": L.inp("V", [S, 1024], BF16), "in_deps": [], "O": L.out("O", [S, 1024], BF16)}
    outs = phaseB_mla(L.k, io)
    in_maps = []
    for c in range(NCORES):
        b, g = c // 2, c % 2
        r0, r1 = resA[2 * b], resA[2 * b + 1]
        cat = lambda n, sl, ax: np.concatenate([r0[n][sl], r1[n][sl]], axis=ax)
        in_maps.append({
            "QNT": cat("QNT", slice(g * 8, (g + 1) * 8), 2), "QPT": cat("QPT", slice(g * 8, (g + 1) * 8), 2),
            "KNT": cat("KNT", slice(g * 8, (g + 1) * 8), 2), "KPT": cat("KPT", slice(None), 1),
            "V": np.concatenate([r0["V"][:, g * 1024:(g + 1) * 1024], r1["V"][:, g * 1024:(g + 1) * 1024]], axis=0)})
    return L.run(in_maps, list(outs.values()))


def launch_A_lru(h_cores, I, li, j):
    L = Launch()
    io = {"h": L.inp("h", [T, D], F32), "ng": L.inp("ng", [D], F32), "w_in": L.inp("w_in", [D, 4096], F32), "h_deps": [],
          "XBT": L.out("XBT", [16, 128, T], F32), "GT": L.out("GT", [16, 128, T], BF16)}
    outs = phaseA_lru(L.k, io)
    in_maps = [{"h": h_cores[c], "ng": I["norm_gain"][li], "w_in": I["c_w_in"][j]} for c in range(NCORES)]
    return L.run(in_maps, list(outs.values()))


def launch_B_lru(resA, I, j):
    L = Launch()
    io = {"XBT": L.inp("XBT", [8, 128, S], F32), "GT": L.inp("GT", [8, 128, S], BF16),
          "cw": L.inp("cw", [128, 8, 4], F32), "cb": L.inp("cb", [128, 8], F32), "ba": L.inp("ba", [128, 8], F32),
          "bx": L.inp("bx", [128, 8], F32), "lam": L.inp("lam", [128, 8], F32),
          "wa": L.inp("wa", [4, 256, 256], F32), "wx": L.inp("wx", [4, 256, 256], F32), "in_deps": [],
          "YT": L.out("YT", [8, 128, S], BF16)}
    outs = phaseB_lru(L.k, io)
    in_maps = []
    fm = lambda v, g: np.ascontiguousarray(v[g * 1024:(g + 1) * 1024].reshape(8, 128).T)
    for c in range(NCORES):
        b, g = c // 2, c % 2
        r0, r1 = resA[2 * b], resA[2 * b + 1]
        cwl = I["c_conv_w"][j][:, g * 1024:(g + 1) * 1024]
        in_maps.append({
            "XBT": np.concatenate([r0["XBT"][g * 8:(g + 1) * 8], r1["XBT"][g * 8:(g + 1) * 8]], axis=2),
            "GT": np.concatenate([r0["GT"][g * 8:(g + 1) * 8], r1["GT"][g * 8:(g + 1) * 8]], axis=2),
            "cw": np.ascontiguousarray(cwl.reshape(4, 8, 128).transpose(2, 1, 0)),
            "cb": fm(I["c_conv_b"][j], g), "ba": fm(I["c_b_a"][j], g), "bx": fm(I["c_b_x"][j], g), "lam": fm(I["c_lambda"][j], g),
            "wa": np.ascontiguousarray(I["c_w_a"][j][g * 4:(g + 1) * 4]), "wx": np.ascontiguousarray(I["c_w_x"][j][g * 4:(g + 1) * 4])})
    return L.run(in_maps, list(outs.values()))


def launch_C_lru(h_cores, YT_cores, w_out, pg, w_g, w_p, p_l):
    L = Launch()
    io = {"h": L.inp("h", [T, D], F32), "YT": L.inp("YT", [KC, 128, T], BF16),
          "w_out": L.inp("w_out", [D, D], F32), "pg": L.inp("pg", [D], F32), "w_g": L.inp("w_g", [D, D], F32),
          "w_p": L.inp("w_p", [256, D], F32), "p": L.inp("p", [T, 256], F32), "in_deps": [], "h_deps": [],
          "h1": L.scratch("h1", [T, D], F32), "hout": L.out("hout", [T, D], F32)}
    outs = phaseC(L.k, io, feat_major_y=True)
    in_maps = []
    for c in range(NCORES):
        b, hf = c // 2, c % 2
        in_maps.append({"h": h_cores[c], "YT": YT_cores[c], "w_out": w_out, "pg": pg, "w_g": w_g,
                        "w_p": w_p, "p": np.ascontiguousarray(p_l[b, hf * T:(hf + 1) * T])})
    return L.run(in_maps, list(outs.values()))


def layer_mla(h_cores, positions, li, j, I):
    rA = launch_A_mla(h_cores, positions, I, li, j)
    rB = launch_B_mla(rA)
    O = gather_O(rB, 1024)
    rC = launch_C(h_cores, O, [r["G"] for r in rA], I["b_w_out"][j], I["ple_norm"][li], I["ple_w_gate"][li],
                  I["ple_w_proj"][li], I["p"][li])
    return [r["hout"] for r in rC], (rA, rB, rC)


def layer_lru(h_cores, li, j, I):
    rA = launch_A_lru(h_cores, I, li, j)
    rB = launch_B_lru(rA, I, j)
    YT = []
    for c in range(NCORES):
        b, hf = c // 2, c % 2
        YT.append(np.concatenate([rB[2 * b]["YT"][:, :, hf * T:(hf + 1) * T], rB[2 * b + 1]["YT"][:, :, hf * T:(hf + 1) * T]], axis=0))
    rC = launch_C_lru(h_cores, YT, I["c_w_out"][j], I["ple_norm"][li], I["ple_w_gate"][li], I["ple_w_proj"][li], I["p"][li])
    return [r["hout"] for r in rC], (rA, rB, rC)


NEG = -30000.0
SHIFT = -8.0
PAIRS = [[0, 1], [2, 3], [4, 5], [6, 7]]


def attn_core(k, io, nheads_c, dv, load_head, score_mm, finish_q, mb, mb_k, bias, bias_k, scale):
    P = k.P
    LA = 3
    NE = LA + 2
    E = [P.sbuf("B_E%d" % i, [128, 512], BF16) for i in range(NE)]
    st = {"ei": 0}
    units = [(qb, ktg) for qb in range(T // 512) for ktg in range(8 * qb + 8)]
    acc = [k.pf[j] for j in range(4)]

    def emit_score(u, rd):
        qb, ktg = units[u]
        rank, j = ktg % 2, ktg // 2
        d = ktg - 8 * qb
        j0 = 0 if d < 0 else d // 2
        c0 = j0 * 128
        ps, ps_k = k.next_pf(4, 8)
        score_mm(ps, ps_k, c0, rank, j, qb, rd)
        e_, e_k = E[st["ei"] % NE]
        st["ei"] += 1
        if d < 0:
            k.act(e_[:, c0:512], ps[:, c0:512], AF.Exp, [ps_k, bias_k], [e_k], scale=scale, bias=bias[:, 0:1])
        else:
            if d % 2 == 0:
                b0, b1 = mb[:, 0:1], bias[:, 0:1]
            else:
                b0, b1 = mb[:, 1:2], mb[:, 2:3]
            k.act(e_[:, c0:c0 + 64], ps[:, c0:c0 + 64], AF.Exp, [ps_k, mb_k, bias_k], [e_k], scale=scale, bias=b0)
            k.act(e_[:, c0 + 64:c0 + 128], ps[:, c0 + 64:c0 + 128], AF.Exp, [ps_k, mb_k, bias_k], [e_k], scale=scale, bias=b1)
            if c0 + 128 < 512:
                k.act(e_[:, c0 + 128:512], ps[:, c0 + 128:512], AF.Exp, [ps_k, bias_k], [e_k], scale=scale, bias=bias[:, 0:1])
        return (e_, e_k, j0, rank, j)

    nxt_head = load_head(0)
    for hc in range(nheads_c):
        rd, va, va_k = nxt_head
        pendq = [emit_score(u, rd) for u in range(LA)]
        if hc + 1 < nheads_c:
            nxt_head = load_head(hc + 1)
        if io.get("bg_hook"):
            pace = Tk("pace")
            pace.w = list(acc[3][1].w)
            io["bg_hook"]([pace])
        for u in range(len(units)):
            if u + LA < len(units):
                pendq.append(emit_score(u + LA, rd))
            qb, ktg = units[u]
            e_, e_k, j0, rank, j = pendq.pop(0)
            for jq in range(j0, 4):
                a_, a_k = acc[jq]
                last = 8 * qb + 2 * jq + 1
                k.mm(a_[:, 0:dv + 1], e_[:, jq * 128:(jq + 1) * 128], va[:, rank * 16 + j, 0:dv + 1], ktg == 0, ktg == last,
                     [e_k, va_k], [a_k], mark=(ktg == last))
            if ktg == 8 * qb + 7:
                finish_q(hc, qb, acc)


def phaseB2_diff(k, io, lam_init):
    P = k.P
    scale = 128 ** -0.5
    lv = [bcast_load(k, "B_lv%d" % i, io[n], 128) for i, n in enumerate(["lq1", "lk1", "lq2", "lk2"])]
    sg, sg_k = bcast_load(k, "B_sg", io["sg"], 256)
    lt, lt_k = P.sbuf("B_lt", [128, 128], F32)
    l2, l2_k = P.sbuf("B_l2", [128, 2], F32)
    lam, lam_k = P.sbuf("B_lam", [128, 1], F32)
    nlam, nlam_k = P.sbuf("B_nlam", [128, 1], F32)
    for j in range(2):
        k.tt("dve", lt[:, :], lv[2 * j][0][:, :], lv[2 * j + 1][0][:, :], ALU.mult, [lv[2 * j][1], lv[2 * j + 1][1]], [lt_k])
        k.red(l2[:, j:j + 1], lt[:, :], [lt_k], [l2_k])
    k.act(l2[:, :], l2[:, :], AF.Exp, [l2_k], [l2_k])
    k.tt("dve", lam[:, :], l2[:, 0:1], l2[:, 1:2], ALU.subtract, [l2_k], [lam_k])
    k.ts("dve", nlam[:, :], lam[:, :], lam_init, -1.0, ALU.add, ALU.mult, [lam_k], [nlam_k])
    k.ts("dve", sg[:, :], sg[:, :], 1.0 - lam_init, None, ALU.mult, None, [sg_k], [sg_k])
    bias, bias_k = P.sbuf("B_bias", [128, 1], F32)
    k.memset("dve", bias[:, :], SHIFT, [], [bias_k])
    mb, mb_k = P.sbuf("B_mb", [128, 3], F32)
    P.dma("sp", mb[:, :], io["mb"], [], [mb_k], mb_k)
    qT = [P.sbuf("B_qT%d" % i, [128, T], BF16) for i in range(2)]
    kT = [P.sbuf("B_kT%d" % i, [128, 2, T], BF16) for i in range(2)]
    va = [P.sbuf("B_va%d" % i, [128, 32, 257], BF16) for i in range(2)]
    for (v_, v_k) in va:
        k.memset("dve", v_[:, :, 256:257], 1.0, [], [v_k])
    o1 = [P.sbuf("B_o1_%d" % i, [128, 256], F32) for i in range(4)]
    oo = [P.sbuf("B_oo%d" % i, [128, 256], F32) for i in range(2)]
    sq = P.sbuf("B_sq", [128, 256], F32)
    rr = [P.sbuf("B_rr%d" % i, [128, 1], F32) for i in range(2)]
    ss = P.sbuf("B_ss", [128, 1], F32)
    rs = P.sbuf("B_rs", [128, 1], F32)
    ost = [P.sbuf("B_ost%d" % i, [128, 4, 256], BF16) for i in range(2)]
    O_k = Tk("O")
    state = {"oi": 0}

    def load_head(hc):
        hd, c = hc // 2, hc % 2
        v_, v_k = va[hd % 2]
        if c == 0:
            for rank in range(2):
                for i in range(4):
                    P.dma("sp", v_[:, rank * 16 + 4 * i:rank * 16 + 4 * i + 4, 0:256],
                          io["Vall"](rank, i)[:, hd * 256:(hd + 1) * 256].rearrange("(t p) e -> p t e", p=128),
                          io["in_deps"], [v_k], v_k)
        q_, q_k = qT[hc % 2]
        k_, k_k = kT[hc % 2]
        P.dma("sp", q_[:, :], io["QT"][hc, :, :], io["q_deps"], [q_k], q_k)
        P.dma("sp", k_[:, :, :], io["KTall"](hc).rearrange("r d t -> d r t"), io["in_deps"], [k_k], k_k)
        return (q_, q_k, k_, k_k), v_, v_k

    def score_mm(ps, ps_k, c0, rank, j, qb, rd):
        q_, q_k, k_, k_k = rd
        k.mm(ps[:, c0:512], k_[:, rank, j * 128:(j + 1) * 128], q_[:, qb * 512 + c0:(qb + 1) * 512], True, True,
             [k_k, q_k], [ps_k], mark=True)

    def finish_q(hc, qb, acc):
        hd, c = hc // 2, hc % 2
        for j in range(4):
            a_, a_k = acc[j]
            r_, r_k = rr[j % 2]
            k.recip(r_[:, :], a_[:, 256:257], [a_k], [r_k])
            o_, o_k = o1[j]
            if c == 0:
                k.ts("dve", io["_o1"][qb][0][:, j, :], a_[:, 0:256], r_[:, 0:1], None, ALU.mult, None, [a_k, r_k], [io["_o1"][qb][1]])
            else:
                os_, os_k = ost[state["oi"] % 2]
                k.tt("dve", r_[:, :], r_[:, :], nlam[:, :], ALU.mult, [r_k, nlam_k], [r_k])
                f_, f_k = oo[j % 2]
                k.stt(f_[:, :], a_[:, 0:256], r_[:, 0:1], io["_o1"][qb][0][:, j, :], ALU.mult, ALU.add,
                      [a_k, r_k, io["_o1"][qb][1]], [f_k])
                k.act(sq[0][:, :], f_[:, :], AF.Square, [f_k], [sq[1]])
                k.red(ss[0][:, :], sq[0][:, :], [sq[1]], [ss[1]])
                k.rstd(rs[0][:, :], ss[0][:, :], 1.0 / 256, rs[1], ss[1])
                k.stt(os_[:, j, :], f_[:, :], rs[0][:, 0:1], sg[:, :], ALU.mult, ALU.mult, [f_k, rs[1], sg_k], [os_k])
        if c == 1:
            os_, os_k = ost[state["oi"] % 2]
            state["oi"] += 1
            P.dma("sp", io["O"][qb * 512:(qb + 1) * 512, hd * 256:(hd + 1) * 256].rearrange("(j p) e -> p j e", p=128),
                  os_[:, :, :], [os_k], [O_k], os_k)

    io["_o1"] = [P.sbuf("B_o1q%d" % i, [128, 4, 256], F32) for i in range(4)]
    attn_core(k, io, 16, 256, load_head, score_mm, finish_q, mb, mb_k, bias, bias_k, scale)
    return {"O": O_k}


def phaseB2_mla(k, io):
    P = k.P
    scale = 192 ** -0.5
    bias, bias_k = P.sbuf("B_bias", [128, 1], F32)
    k.memset("dve", bias[:, :], SHIFT, [], [bias_k])
    mb, mb_k = P.sbuf("B_mb", [128, 3], F32)
    P.dma("sp", mb[:, :], io["mb"], [], [mb_k], mb_k)
    kp, kp_k = P.sbuf("B_kp", [64, 2, T], BF16)
    P.dma("sp", kp[:, :, :], io["KPall"].rearrange("r d t -> d r t"), io["in_deps"], [kp_k], kp_k)
    qn = [P.sbuf("B_qn%d" % i, [128, T], BF16) for i in range(2)]
    qp = [P.sbuf("B_qp%d" % i, [64, T], BF16) for i in range(2)]
    kn = [P.sbuf("B_kn%d" % i, [128, 2, T], BF16) for i in range(2)]
    va = [P.sbuf("B_va%d" % i, [128, 32, 129], BF16) for i in range(2)]
    for (v_, v_k) in va:
        k.memset("dve", v_[:, :, 128:129], 1.0, [], [v_k])
    rr = [P.sbuf("B_rr%d" % i, [128, 1], F32) for i in range(2)]
    ost = [P.sbuf("B_ost%d" % i, [128, 4, 128], BF16) for i in range(2)]
    O_k = Tk("O")
    state = {"oi": 0}

    def load_head(hd):
        v_, v_k = va[hd % 2]
        for rank in range(2):
            for i in range(4):
                P.dma("sp", v_[:, rank * 16 + 4 * i:rank * 16 + 4 * i + 4, 0:128],
                      io["Vall"](rank, i)[:, hd * 128:(hd + 1) * 128].rearrange("(t p) e -> p t e", p=128),
                      io["in_deps"], [v_k], v_k)
        q_, q_k = qn[hd % 2]
        qp_, qp_k = qp[hd % 2]
        k_, k_k = kn[hd % 2]
        P.dma("sp", q_[:, :], io["QNT"][hd, :, :], io["q_deps"], [q_k], q_k)
        P.dma("sp", qp_[:, :], io["QPT"][hd, :, :], io["q_deps"], [qp_k], qp_k)
        P.dma("sp", k_[:, :, :], io["KNall"](hd).rearrange("r d t -> d r t"), io["in_deps"], [k_k], k_k)
        return (q_, q_k, qp_, qp_k, k_, k_k), v_, v_k

    def score_mm(ps, ps_k, c0, rank, j, qb, rd):
        q_, q_k, qp_, qp_k, k_, k_k = rd
        k.mm(ps[:, c0:512], k_[:, rank, j * 128:(j + 1) * 128], q_[:, qb * 512 + c0:(qb + 1) * 512], True, False,
             [k_k, q_k], [ps_k], mark=False)
        k.mm(ps[:, c0:512], kp[:, rank, j * 128:(j + 1) * 128], qp_[:, qb * 512 + c0:(qb + 1) * 512], False, True,
             [kp_k, qp_k], [ps_k], mark=True)

    def finish_q(hd, qb, acc):
        os_, os_k = ost[state["oi"] % 2]
        state["oi"] += 1
        for j in range(4):
            a_, a_k = acc[j]
            r_, r_k = rr[j % 2]
            k.recip(r_[:, :], a_[:, 128:129], [a_k], [r_k])
            k.ts("dve", os_[:, j, :], a_[:, 0:128], r_[:, 0:1], None, ALU.mult, None, [a_k, r_k], [os_k])
        P.dma("sp", io["O"][qb * 512:(qb + 1) * 512, hd * 128:(hd + 1) * 128].rearrange("(j p) e -> p j e", p=128),
              os_[:, :, :], [os_k], [O_k], os_k)

    attn_core(k, io, 16, 128, load_head, score_mm, finish_q, mb, mb_k, bias, bias_k, scale)
    return {"O": O_k}


def phaseB2_lru(k, io):
    P, nc = k.P, k.nc
    cw, cw_k = P.sbuf("L_cw", [128, 16, 4], F32)
    P.dma("sp", cw[:, :, :], io["cw"], [], [cw_k], cw_k)
    sel, sel_k = P.sbuf("L_sel", [128, 2], F32)
    P.dma("sp", sel[:, :], io["sel"], [], [sel_k], sel_k)
    small = {}
    for n in ["cb", "ba", "bx", "lam"]:
        t_, t_k = P.sbuf("L_" + n, [128, 16], F32)
        P.dma("sp", t_[:, :], io[n], [], [t_k], t_k)
        small[n] = (t_, t_k)
    ce, ce_k = P.sbuf("L_ce", [128, 16], F32)
    k.act(ce[:, :], small["lam"][0][:, :], AF.Exp, [small["lam"][1]], [ce_k], scale=-1.0)
    k.act(ce[:, :], ce[:, :], AF.Ln, [ce_k], [ce_k], bias=1.0)
    k.ts("dve", ce[:, :], ce[:, :], -8.0, None, ALU.mult, None, [ce_k], [ce_k])
    xb = [P.sbuf("L_xb%d" % i, [128, S], F32) for i in range(2)]
    xc = [P.sbuf("L_xc%d" % i, [128, S], F32) for i in range(2)]
    xcb = [P.sbuf("L_xcb%d" % i, [128, S], BF16) for i in range(2)]
    rg = [P.sbuf("L_r%d" % i, [128, S], F32) for i in range(2)]
    ig = [P.sbuf("L_i%d" % i, [128, S], F32) for i in range(2)]
    tmp = P.sbuf("L_tmp", [128, S], F32)
    gt = [P.sbuf("L_g%d" % i, [128, S], BF16) for i in range(2)]
    wa = P.sbuf("L_wa", [128, 2, 256], BF16)
    wx = P.sbuf("L_wx", [128, 2, 256], BF16)
    Y_k = Tk("YT")
    il = lambda ap, rank: ap.rearrange("p (s r t) -> p s r t", r=2, t=128)[:, :, rank, :]
    for gb in range(8):
        if io.get("bg_hook") and gb > 0:
            pace = Tk("pace")
            pace.w = list(rg[0][1].w)
            io["bg_hook"]([pace])
        for (wt, nm) in ((wa, "wa"), (wx, "wx")):
            if isinstance(io[nm], PreCast):
                P.dma("act", wt[0][:, :, :], io[nm].ap[gb * 256:(gb + 1) * 256, :].rearrange("(c p) n -> p c n", p=128),
                      [io[nm].tk], [wt[1]], wt[1])
            else:
                P.dma("pool", wt[0][:, :, :], io[nm][gb].rearrange("(c p) n -> p c n", p=128), [], [wt[1]], wt[1])
        for ic in range(2):
            ch = gb * 2 + ic
            b_, b_k = xb[ic]
            c_, c_k = xc[ic]
            cb_, cb_k = xcb[ic]
            for rank in range(2):
                P.dma("sp", il(b_[:, :], rank), io["XBall"](rank, ch).rearrange("p (s t) -> p s t", t=128),
                      io["in_deps"], [b_k], b_k)
                P.dma("sp", il(gt[ic][0][:, :], rank), io["GTall"](rank, ch).rearrange("p (s t) -> p s t", t=128),
                      io["in_deps"], [gt[ic][1]], gt[ic][1])
            k.act(c_[:, :], b_[:, :], AF.Identity, [b_k, cw_k, small["cb"][1]], [c_k],
                  scale=cw[:, ch, 3:4], bias=small["cb"][0][:, ch:ch + 1])
            for sh in (1, 2, 3):
                k.stt(c_[:, sh:S], b_[:, 0:S - sh], cw[:, ch, 3 - sh:4 - sh], c_[:, sh:S], ALU.mult, ALU.add,
                      [b_k, cw_k, c_k], [c_k])
            k.copy("act", cb_[:, :], c_[:, :], [c_k], [cb_k])
        for oc in range(2):
            ch = gb * 2 + oc
            for (wt, gate, bname) in ((wa, rg, "ba"), (wx, ig, "bx")):
                g_, g_k = gate[oc]
                for tb in range(S // 512):
                    ps, ps_k = k.next_pf(0, 4)
                    for ic in range(2):
                        k.mm(ps[:, :], wt[0][:, ic, oc * 128:(oc + 1) * 128], xcb[ic][0][:, tb * 512:(tb + 1) * 512], ic == 0, ic == 1,
                             [wt[1], xcb[ic][1]], [ps_k], mark=(ic == 1))
                    k.act(g_[:, tb * 512:(tb + 1) * 512], ps[:, :], AF.Sigmoid, [ps_k, small[bname][1]], [g_k],
                          bias=small[bname][0][:, ch:ch + 1])
            r_, r_k = rg[oc]
            i_, i_k = ig[oc]
            c_, c_k = xc[oc]
            t_, t_k = tmp
            h_, h_k = xb[oc]
            k.act(r_[:, :], r_[:, :], AF.Exp, [r_k, ce_k], [r_k], scale=ce[:, ch:ch + 1])
            k.tt("dve", t_[:, :], r_[:, :], r_[:, :], ALU.mult, [r_k], [t_k])
            k.ts("dve", t_[:, :], t_[:, :], -1.0, 1.0, ALU.mult, ALU.add, [t_k], [t_k])
            k.act(t_[:, :], t_[:, :], AF.Sqrt, [t_k], [t_k])
            k.tt("dve", i_[:, :], i_[:, :], c_[:, :], ALU.mult, [i_k, c_k], [i_k])
            k.tt("dve", i_[:, :], i_[:, :], t_[:, :], ALU.mult, [i_k, t_k], [i_k])
            P.op("dve", lambda: nc.vector.tensor_tensor_scan(out=h_[:, :], data0=r_[:, :], data1=i_[:, :], initial=0.0,
                                                             op0=ALU.mult, op1=ALU.add), [r_k, i_k], [h_k])
            y_, y_k = gt[oc]
            k.tt("dve", t_[:, :], h_[:, :], y_[:, :], ALU.mult, [h_k, y_k], [t_k])
            own = y_[:, 0:T].rearrange("p (s t) -> p s t", t=128)
            k.ts("dve", own, il(t_[:, :], 0), sel[:, 0:1], None, ALU.mult, None, [t_k, sel_k], [y_k])
            k.stt(own, il(t_[:, :], 1), sel[:, 1:2], own, ALU.mult, ALU.add, [t_k, sel_k, y_k], [y_k])
            P.dma("sp", io["YT"][ch, :, :], y_[:, 0:T], [y_k], [Y_k], y_k)
    return {"YT": Y_k}


def tok_split(x_b, r):
    sh = x_b.shape
    return np.ascontiguousarray(x_b.reshape((NT, 2, 128) + sh[1:])[:, r].reshape((T,) + sh[1:]))


def feat_major(v, nch):
    return np.ascontiguousarray(np.asarray(v, np.float32).reshape(nch, 128).T)


LAYER_W = {
    0: [("a_w_in", "w_in"), ("a_q_norm", "qg"), ("a_k_norm", "kg"), ("a_lambda_q1", "lq1"), ("a_lambda_k1", "lk1"),
        ("a_lambda_q2", "lq2"), ("a_lambda_k2", "lk2"), ("a_sub_norm", "sg"), ("a_w_out", "w_out")],
    1: [("b_w_in", "w_in"), ("b_cq_norm", "cqg"), ("b_ckv_norm", "ckvg"), ("b_w_uq", "w_uq"), ("b_w_ukv", "w_ukv"),
        ("b_q_nope_norm", "qng"), ("b_q_rope_norm", "qrg"), ("b_k_nope_norm", "kng"), ("b_k_rope_norm", "krg"),
        ("b_w_out", "w_out")],
    2: [("c_w_in", "w_in"), ("c_w_a", "wa"), ("c_w_x", "wx"), ("c_w_out", "w_out")],
}
COMMON_W = [("norm_gain", "ng"), ("ple_norm", "pg"), ("ple_w_gate", "w_g"), ("ple_w_proj", "w_p")]
W_SHAPES = {"a_w_in": [D, 8192], "a_q_norm": [128], "a_k_norm": [128], "a_lambda_q1": [128], "a_lambda_k1": [128],
            "a_lambda_q2": [128], "a_lambda_k2": [128], "a_sub_norm": [256], "a_w_out": [D, D],
            "b_w_in": [D, 3136], "b_cq_norm": [512], "b_ckv_norm": [512], "b_w_uq": [512, 3072], "b_w_ukv": [512, 4096],
            "b_q_nope_norm": [128], "b_q_rope_norm": [64], "b_k_nope_norm": [128], "b_k_rope_norm": [64], "b_w_out": [D, D],
            "c_w_in": [D, 4096], "c_w_a": [8, 256, 256], "c_w_x": [8, 256, 256], "c_w_out": [D, D],
            "norm_gain": [D], "ple_norm": [D], "ple_w_gate": [D, D], "ple_w_proj": [256, D]}


def build_fused(nlayers=4, stop=None):
    L = Launch()
    P, k = L.P, L.k
    inp, scr = L.inp, L.scratch
    x = inp("x", [T, D], F32)
    p = inp("p", [nlayers, T, 256], F32)
    posT = inp("posT", [128, NT], I32)
    inv64 = inp("inv64", [64], F32)
    inv32 = inp("inv32", [32], F32)
    mb = inp("mb", [128, 3], F32)
    sel = inp("sel", [128, 2], F32)
    out = L.out("hout", [T, D], F32)
    hbuf = [scr("hA", [T, D], F32), scr("hB", [T, D], F32)]
    h1 = scr("h1", [T, D], F32)
    QT = scr("QT", [16, 128, T], BF16)
    QPT = scr("QPT", [16, 64, T], BF16)
    G = scr("G", [T, D], BF16)
    O = scr("O", [T, D], BF16)
    YT = scr("YT", [16, 128, T], BF16)
    h_in = x
    BIGW = ("w_in", "w_out", "w_g", "w_p", "w_uq", "w_ukv", "wa", "wx")
    WALL = {}
    casts = []
    for li in range(nlayers):
        kind = li % 3
        WALL[li] = {}
        for (src, nm) in LAYER_W[kind] + COMMON_W:
            ap = inp("L%d_%s" % (li, nm), W_SHAPES[src], F32)
            if nm in BIGW and not (li == 0 and nm == "w_in"):
                shp = W_SHAPES[src]
                if len(shp) == 3:
                    ap = ap.rearrange("g i j -> (g i) j")
                    shp = [shp[0] * shp[1], shp[2]]
                pc = PreCast(ap, scr("bf_L%d_%s" % (li, nm), shp, BF16), shp[0], shp[1])
                casts.append((li, nm, pc))
                ap = pc
            WALL[li][nm] = ap
    order = {"w_out": 2, "w_g": 2, "w_p": 2, "wa": 1, "wx": 1}
    casts.sort(key=lambda c: (c[0], order.get(c[1], 0)))
    cq = [(pc, i) for (_, _, pc) in casts for i in range(len(pc.pieces))]
    nmix = sum(1 for li in range(nlayers))
    state = {"pos": 0}

    def make_hook(n_hooks, upto):
        todo = max(0, upto - state["pos"])
        per = -(-todo // n_hooks) if n_hooks else 0

        def hook(pace):
            n = min(per, upto - state["pos"])
            for _ in range(max(0, n)):
                pc, i = cq[state["pos"]]
                state["pos"] += 1
                pc.emit_piece(P, i, pace)
        return hook

    def pieces_through(li_max):
        return sum(len(pc.pieces) for (l, _, pc) in casts if l <= li_max)

    for li in range(nlayers):
        kind, j = li % 3, li // 3
        Wl = WALL[li]
        h_out = out if li == nlayers - 1 else hbuf[li % 2]
        tgt = min(2 if li == 0 else nlayers - 1, nlayers - 1)
        upto = pieces_through(tgt)
        bg_hook = make_hook(16 if kind != 2 else 7, upto)
        P.begin_phase()
        k.setup_psum(6, 2)
        if kind == 0:
            send = scr("send%d" % li, [4096, 2048], BF16)
            recv = scr("recv%d" % li, [8192, 2048], BF16)
            r5 = recv.rearrange("(i r x) t -> i r x t", r=2, x=512)
            io = {"h": h_in, "ng": Wl["ng"], "w_in": Wl["w_in"], "qg": Wl["qg"], "kg": Wl["kg"],
                  "posT": posT, "inv64": inv64, "h_deps": [], "QT": QT,
                  "KT": send[0:2048, :].rearrange("(g d) t -> g d t", g=16), "V": send[2048:4096, :], "G": G}
            outs = phaseA_diff(k, io)
            P.gather_chunks(send, recv, 4096, 512, PAIRS, [outs["KT"], outs["V"]], [Tk("recv")])

        elif kind == 1:
            send = scr("send%d" % li, [4160, 2048], BF16)
            recv = scr("recv%d" % li, [8320, 2048], BF16)
            r5 = recv[0:8192, :].rearrange("(i r x) t -> i r x t", r=2, x=512)
            io = {"h": h_in, "ng": Wl["ng"], "w_in": Wl["w_in"], "cqg": Wl["cqg"], "ckvg": Wl["ckvg"],
                  "w_uq": Wl["w_uq"], "w_ukv": Wl["w_ukv"], "qng": Wl["qng"], "qrg": Wl["qrg"],
                  "kng": Wl["kng"], "krg": Wl["krg"], "posT": posT, "inv32": inv32, "h_deps": [],
                  "QNT": QT, "QPT": QPT, "KNT": send[0:2048, :].rearrange("(g d) t -> g d t", g=16),
                  "KPT": send[4096:4160, :], "V": send[2048:4096, :], "G": G}
            outs = phaseA_mla(k, io)
            P.gather_chunks(send, recv, 4160, 512, PAIRS, [outs["KNT"], outs["KPT"], outs["V"]], [Tk("recv")])
        else:
            sendx = scr("sendx%d" % li, [2048, T], F32)
            recvx = scr("recvx%d" % li, [4096, T], F32)
            send = scr("send%d" % li, [2048, T], BF16)
            recv = scr("recv%d" % li, [4096, T], BF16)
            io = {"h": h_in, "ng": Wl["ng"], "w_in": Wl["w_in"], "h_deps": [],
                  "XBT": sendx.rearrange("(g d) t -> g d t", g=16), "GT": send.rearrange("(g d) t -> g d t", g=16)}
            outs = phaseA_lru(k, io)
            P.gather_chunks(sendx, recvx, 2048, 256, PAIRS, [outs["XBT"]], [Tk("recvx")])
            P.gather_chunks(send, recv, 2048, 512, PAIRS, [outs["GT"]], [Tk("recv")])
            rx5 = recvx.rearrange("(i r x) t -> i r x t", r=2, x=256)
            r5 = recv.rearrange("(i r x) t -> i r x t", r=2, x=512)
        P.end_phase()
        if stop == "A":
            return L, []
        P.begin_phase()
        k.setup_psum(8, 0) if kind != 2 else k.setup_psum(6, 2)
        if kind == 0:
            io = {"QT": QT, "KTall": (lambda hc, r5=r5: r5[hc // 4, :, (hc % 4) * 128:(hc % 4 + 1) * 128, :]),
                  "Vall": (lambda rank, i, r5=r5: r5[4 + i, rank, :, :]), "mb": mb,
                  "lq1": Wl["lq1"], "lk1": Wl["lk1"], "lq2": Wl["lq2"], "lk2": Wl["lk2"],
                  "sg": Wl["sg"], "in_deps": [], "q_deps": [], "O": O, "bg_hook": bg_hook}
            phaseB2_diff(k, io, 0.8 - 0.6 * math.exp(-0.3 * li))
        elif kind == 1:
            io = {"QNT": QT, "QPT": QPT, "KNall": (lambda hc, r5=r5: r5[hc // 4, :, (hc % 4) * 128:(hc % 4 + 1) * 128, :]),
                  "KPall": recv[8192:8320, :].rearrange("(r d) t -> r d t", r=2),
                  "Vall": (lambda rank, i, r5=r5: r5[4 + i, rank, :, :]), "mb": mb, "in_deps": [], "q_deps": [], "O": O,
                  "bg_hook": bg_hook}
            phaseB2_mla(k, io)
        else:
            cw = inp("cw", [128, 16, 4], F32)
            small = {n: inp(n, [128, 16], F32) for n in ["cb", "ba", "bx", "lam"]}
            io = {"XBall": (lambda rank, ch, rx5=rx5: rx5[ch // 2, rank, (ch % 2) * 128:(ch % 2 + 1) * 128, :]),
                  "GTall": (lambda rank, ch, r5=r5: r5[ch // 4, rank, (ch % 4) * 128:(ch % 4 + 1) * 128, :]),
                  "cw": cw, "cb": small["cb"], "ba": small["ba"], "bx": small["bx"], "lam": small["lam"],
                  "wa": Wl["wa"], "wx": Wl["wx"], "sel": sel, "in_deps": [], "YT": YT, "bg_hook": bg_hook}
            phaseB2_lru(k, io)
        while state["pos"] < upto:
            pc, i = cq[state["pos"]]
            state["pos"] += 1
            pc.emit_piece(P, i, [])
        P.end_phase()
        if stop == "B":
            return L, []
        P.begin_phase()
        k.setup_psum(6, 2)
        io = {"h": h_in, "O": O, "G": G, "YT": YT, "w_out": Wl["w_out"], "pg": Wl["pg"], "w_g": Wl["w_g"],
              "w_p": Wl["w_p"], "p": p[li], "in_deps": [], "h_deps": [], "h1": h1, "hout": h_out}
        outs = phaseC(k, io, feat_major_y=(kind == 2))
        final = list(outs.values())
        P.end_phase()
        h_in = h_out
    return L, final


def fused_in_maps(I, nlayers=4):
    shared = {"inv64": INV64, "inv32": INV32}
    for li in range(nlayers):
        kind, j = li % 3, li // 3
        for (src, nm) in LAYER_W[kind]:
            shared["L%d_%s" % (li, nm)] = np.ascontiguousarray(I[src][j], dtype=np.float32)
        for (src, nm) in COMMON_W:
            shared["L%d_%s" % (li, nm)] = np.ascontiguousarray(I[src][li], dtype=np.float32)
        if kind == 2:
            shared["cw"] = np.ascontiguousarray(np.asarray(I["c_conv_w"][j], np.float32).reshape(4, 16, 128).transpose(2, 1, 0))
            shared["cb"] = feat_major(I["c_conv_b"][j], 16)
            shared["ba"] = feat_major(I["c_b_a"][j], 16)
            shared["bx"] = feat_major(I["c_b_x"][j], 16)
            shared["lam"] = feat_major(I["c_lambda"][j], 16)
    maps = []
    for c in range(NCORES):
        b, r = c // 2, c % 2
        m = dict(shared)
        m["x"] = tok_split(np.asarray(I["x"][b], np.float32), r)
        m["p"] = np.stack([tok_split(np.asarray(I["p"][l, b], np.float32), r) for l in range(nlayers)])
        m["posT"] = np.ascontiguousarray(tok_split(np.asarray(I["positions"][b]), r).reshape(NT, 128).T.astype(np.int32))
        mbv = np.full((128, 3), SHIFT, np.float32)
        if r == 0:
            mbv[64:, 0] = NEG
            mbv[:, 1] = NEG
            mbv[:, 2] = NEG
        else:
            mbv[64:, 1] = NEG
        m["mb"] = mbv
        m["sel"] = np.tile(np.array([[1.0 - r, float(r)]], np.float32), (128, 1))
        maps.append(m)
    return maps


def kernel(**inputs):
    I = {k_: np.asarray(v) for k_, v in inputs.items()}
    L, final = build_fused(4)
    res = L.run(fused_in_maps(I, 4), final)
    out = np.empty((4, S, D), np.float32)
    for c in range(NCORES):
        b, r = c // 2, c % 2
        out[b].reshape(NT, 2, 128, D)[:, r] = res[c]["hout"].reshape(NT, 128, D)
    return out
```
